# Optimizing a Trainium2 kernel written in Bass

```python
import jax, jax.numpy as jnp
from jax import lax
import numpy as np

D_MODEL = 1024
BATCH = 4
SEQ = 4096
DEPTH = 1

CHUNK = 64
D_MIX = D_MODEL
D_CONV = D_MIX // 2
CONV_HEADS = 8
CONV_WIDTH = 3
D_POOL = D_MIX - D_CONV
POOL_WINDOWS = (2, 4, 8, 16)
POOL_GROUPS = len(POOL_WINDOWS)
POOL_GC = D_POOL // POOL_GROUPS
D_FF = 2816
EPS = 1e-6

kernel_name = "hybrid_conv_pool_macaron_block"


def rms_norm(x, g):
    xf = x.astype(jnp.float32)
    y = xf * lax.rsqrt(jnp.mean(xf * xf, axis=-1, keepdims=True) + EPS)
    return (y * g.astype(jnp.float32)).astype(x.dtype)


def swiglu(h, w_gate, w_up, w_down):
    return (jax.nn.silu(h @ w_gate) * (h @ w_up)) @ w_down


def causal_depthwise_conv(z, w):
    s = z.shape[1]
    zp = jnp.pad(z, ((0, 0), (CONV_WIDTH - 1, 0), (0, 0)))
    return sum(w[k] * zp[:, k:k + s] for k in range(CONV_WIDTH))


def multiscale_pool_minus_self(u):
    s = u.shape[1]
    uf = u.astype(jnp.float32)
    c = jnp.cumsum(uf, axis=1)
    t1 = jnp.arange(1, s + 1, dtype=jnp.float32)
    outs = []
    for g, w in enumerate(POOL_WINDOWS):
        cg = c[:, :, g]
        shifted = jnp.pad(cg, ((0, 0), (w, 0), (0, 0)))[:, :s]
        count = jnp.minimum(t1, float(w))[None, :, None]
        outs.append((cg - shifted) / count - uf[:, :, g])
    return jnp.stack(outs, axis=2).astype(u.dtype)


def setup_inputs(seed: int = 0) -> dict:
    key = jax.random.key(seed)
    ks = jax.random.split(key, 20)
    f32 = jnp.float32

    def nrm(k, shape, fan_in):
        return jax.random.normal(k, shape, f32) * (fan_in ** -0.5)

    def gain(k, shape):
        return 1.0 + 0.02 * jax.random.normal(k, shape, f32)

    L = DEPTH
    return {
        "x": jax.random.normal(ks[0], (BATCH, SEQ, D_MODEL), f32),
        "norm_ffn1": gain(ks[1], (L, D_MODEL)),
        "ffn1_w_gate": nrm(ks[2], (L, D_MODEL, D_FF), D_MODEL),
        "ffn1_w_up": nrm(ks[3], (L, D_MODEL, D_FF), D_MODEL),
        "ffn1_w_down": nrm(ks[4], (L, D_FF, D_MODEL), D_FF),
        "norm_mix": gain(ks[5], (L, D_MODEL)),
        "w_in": nrm(ks[6], (L, D_MODEL, 3 * D_CONV + D_POOL), D_MODEL),
        "conv_w": nrm(ks[7], (L, CONV_WIDTH, D_CONV), CONV_WIDTH),
        "pool_w": nrm(ks[8], (L, POOL_GROUPS, POOL_GC, POOL_GC), POOL_GC),
        "pool_scale": gain(ks[9], (L, D_POOL)),
        "w_out": nrm(ks[10], (L, D_MIX, D_MODEL), D_MIX),
        "norm_ffn2": gain(ks[11], (L, D_MODEL)),
        "ffn2_w_gate": nrm(ks[12], (L, D_MODEL, D_FF), D_MODEL),
        "ffn2_w_up": nrm(ks[13], (L, D_MODEL, D_FF), D_MODEL),
        "ffn2_w_down": nrm(ks[14], (L, D_FF, D_MODEL), D_FF),
        "norm_final": gain(ks[15], (D_MODEL,)),
    }


def reference(x, norm_ffn1, ffn1_w_gate, ffn1_w_up, ffn1_w_down, norm_mix, w_in, conv_w,
              pool_w, pool_scale, w_out, norm_ffn2, ffn2_w_gate, ffn2_w_up, ffn2_w_down,
              norm_final):
    b, s, _ = x.shape
    for l in range(DEPTH):
        x = x + 0.5 * swiglu(rms_norm(x, norm_ffn1[l]), ffn1_w_gate[l], ffn1_w_up[l], ffn1_w_down[l])

        h = rms_norm(x, norm_mix[l])
        proj = h @ w_in[l]
        v = proj[..., :D_CONV]
        gate_b = proj[..., D_CONV:2 * D_CONV]
        gate_c = proj[..., 2 * D_CONV:3 * D_CONV]
        u = proj[..., 3 * D_CONV:]

        y_a = gate_b * causal_depthwise_conv(gate_c * v, conv_w[l])

        ug = u.reshape(b, s, POOL_GROUPS, POOL_GC)
        pooled = multiscale_pool_minus_self(ug)
        y_b = jnp.einsum("bsgc,gcd->bsgd", pooled, pool_w[l]).reshape(b, s, D_POOL) * pool_scale[l]

        x = x + jnp.concatenate([y_a, y_b], axis=-1) @ w_out[l]

        x = x + 0.5 * swiglu(rms_norm(x, norm_ffn2[l]), ffn2_w_gate[l], ffn2_w_up[l], ffn2_w_down[l])
    return rms_norm(x, norm_final)
```

```python
import numpy as np
import concourse.bass as bass
import concourse.mybir as mybir
from concourse.bass_utils import run_bass_kernel_spmd

F32 = mybir.dt.float32
BF16 = mybir.dt.bfloat16
AF = mybir.ActivationFunctionType
ALU = mybir.AluOpType

D = 1024
DC = 8
DFF = 2816
FC = 22
T = 512
HALO = 16
TOK = 2048
NCORE = 8
EPS = 1e-6
WINDOWS = (2, 4, 8, 16)

NSLOT = 6
SLOT = 4096
NROT = 6
NSQ = 8
NSILU = 3

G_FFN1, G_MIX, G_FFN2, G_FINAL = 0, 1, 2, 3


def _slab_table():
    tab = {}
    off = 0

    def put(name, n):
        nonlocal off
        tab[name] = (off, n)
        off += n

    for k in (1, 2):
        for s in range(11):
            put(("gu", k, s), 8 * 2 * 256)
        for d in range(8):
            put(("dn", k, d), FC * 128)
    put(("win", "u"), 8 * 512)
    for j in range(4):
        put(("win", j), 8 * 384)
    for s in range(2):
        put(("wo", s), 8 * 512)
    return tab, off


SLABS, WCOLS = _slab_table()


def _build_wstream(inp):
    w = np.empty((128, WCOLS), np.float32)

    def kview(m):
        K, N = m.shape
        return m.reshape(K // 128, 128, N).transpose(1, 0, 2)

    for k in (1, 2):
        wg = kview(np.asarray(inp[f"ffn{k}_w_gate"][0]))
        wu = kview(np.asarray(inp[f"ffn{k}_w_up"][0]))
        wd = kview(np.asarray(inp[f"ffn{k}_w_down"][0]))
        for s in range(11):
            off, n = SLABS[("gu", k, s)]
            blk = np.stack([wg[:, :, 256 * s:256 * (s + 1)], wu[:, :, 256 * s:256 * (s + 1)]], axis=2)
            w[:, off:off + n] = blk.reshape(128, n)
        for d in range(8):
            off, n = SLABS[("dn", k, d)]
            w[:, off:off + n] = wd[:, :, 128 * d:128 * (d + 1)].reshape(128, n)
    win = kview(np.asarray(inp["w_in"][0]))
    off, n = SLABS[("win", "u")]
    w[:, off:off + n] = win[:, :, 1536:2048].reshape(128, n)
    for j in range(4):
        off, n = SLABS[("win", j)]
        blk = np.stack([win[:, :, 0 + 128 * j:128 * (j + 1)],
                        win[:, :, 1024 + 128 * j:1024 + 128 * (j + 1)],
                        win[:, :, 512 + 128 * j:512 + 128 * (j + 1)]],
                       axis=2)
        w[:, off:off + n] = blk.reshape(128, n)
    wo = kview(np.asarray(inp["w_out"][0]))
    for s in range(2):
        off, n = SLABS[("wo", s)]
        w[:, off:off + n] = wo[:, :, 512 * s:512 * (s + 1)].reshape(128, n)
    return w


ENGS = ("pe", "act", "dve", "pool", "sp")


class Op:
    __slots__ = ("eng", "emit", "deps", "pos", "signal", "sigidx", "dma_key", "dma_cnt", "name", "waits")

    def __init__(self, eng, emit, name):
        self.eng = eng
        self.emit = emit
        self.deps = {}
        self.signal = False
        self.sigidx = 0
        self.dma_key = None
        self.dma_cnt = 0
        self.name = name
        self.waits = []


class Sched:
    def __init__(self):
        self.ops = {e: [] for e in ENGS}
        self.last_writer = {}
        self.readers = {}
        self.dma_counts = {}

    def add(self, eng, emit, reads=(), writes=(), dma_key=None, name="", extra_deps=()):
        op = Op(eng, emit, name)
        for dep in extra_deps:
            op.deps[dep] = "raw"
        for r in reads:
            w = self.last_writer.get(r)
            if w is not None:
                op.deps[w] = "raw"
        for r in writes:
            w = self.last_writer.get(r)
            if w is not None and w not in op.deps:
                op.deps[w] = "waw"
            for rd in self.readers.get(r, ()):
                if rd not in op.deps:
                    op.deps[rd] = "war"
        op.deps.pop(op, None)
        for r in reads:
            self.readers.setdefault(r, []).append(op)
        for r in writes:
            self.last_writer[r] = op
            self.readers[r] = []
        if dma_key is not None:
            op.dma_key = dma_key
            self.dma_counts[dma_key] = self.dma_counts.get(dma_key, 0) + 1
            op.dma_cnt = self.dma_counts[dma_key]
        op.pos = len(self.ops[eng])
        self.ops[eng].append(op)
        return op

    def plan(self):
        for eng in ENGS:
            maxpos = {}
            maxdma = {}
            for op in self.ops[eng]:
                for dep, kind in sorted(op.deps.items(), key=lambda kv: -kv[0].pos):
                    if dep.dma_key is not None:
                        if maxdma.get(dep.dma_key, 0) >= dep.dma_cnt:
                            continue
                        maxdma[dep.dma_key] = dep.dma_cnt
                        op.waits.append(dep)
                        continue
                    if dep.eng == eng:
                        if eng == "pe" or kind != "raw":
                            continue
                    if maxpos.get(dep.eng, -1) >= dep.pos:
                        continue
                    maxpos[dep.eng] = dep.pos
                    dep.signal = True
                    op.waits.append(dep)
        for eng in ENGS:
            n = 0
            for op in self.ops[eng]:
                if op.signal:
                    n += 1
                    op.sigidx = n


class TileD:
    def __init__(self, name, W, xslot, par, tok0, is_H=False, is_first=False):
        self.name, self.W, self.xslot, self.par, self.tok0 = name, W, xslot, par, tok0
        self.is_H, self.is_first = is_H, is_first


def build_program():
    nc = bass.Bass("TRN2", target_bir_lowering=False)
    xT = nc.dram_tensor("xT", [128, DC, HALO + TOK], F32, kind="ExternalInput").ap()
    wst = nc.dram_tensor("wst", [128, WCOLS], F32, kind="ExternalInput").ap()
    gains_d = nc.dram_tensor("gains", [128, 4, DC], F32, kind="ExternalInput").ap()
    convw_d = nc.dram_tensor("convw", [128, 4, 3], F32, kind="ExternalInput").ap()
    pscale_d = nc.dram_tensor("pscale", [128, 4], F32, kind="ExternalInput").ap()
    invc_d = nc.dram_tensor("invc", [128, 4, HALO], F32, kind="ExternalInput").ap()
    poolw_d = nc.dram_tensor("poolw", [128, 4, 128], F32, kind="ExternalInput").ap()
    outT = nc.dram_tensor("outT", [128, DC, TOK], F32, kind="ExternalOutput").ap()

    S = Sched()
    from contextlib import ExitStack
    with ExitStack() as es:
        def sb(name, shape, dt):
            return es.enter_context(nc.sbuf_tensor(name, shape, dt))

        xs = sb("xs", [128, 3, DC, T], F32)
        xH = sb("xH", [128, DC, HALO], F32)
        hb = sb("hb", [128, 2, DC, T], BF16)
        hH = sb("hH", [128, DC, HALO], BF16)
        hid = sb("hid", [128, 2, FC, T], BF16)
        hidH = sb("hidH", [128, FC, HALO], BF16)
        ring = sb("ring", [128, NSLOT, SLOT], BF16)
        sq = sb("sq", [128, NSQ, T], BF16)
        sqH = sb("sqH", [128, DC, HALO], BF16)
        rstd = sb("rstd", [128, 2, T], F32)
        rstdH = sb("rstdH", [128, HALO], F32)
        silu = sb("silu", [128, NSILU, T], F32)
        gains = sb("gains_sb", [128, 4, DC], F32)
        convw = sb("convw_sb", [128, 4, 3], F32)
        pscale = sb("pscale_sb", [128, 4], F32)
        invc = sb("invc_sb", [128, 4, HALO], F32)
        poolw = sb("poolw_sb", [128, 4, 128], BF16)
        ones = sb("ones_sb", [128, 128], BF16)
        epsb = sb("eps_sb", [128, 1], F32)
        uext = sb("uext", [128, 2, HALO + T], F32)
        zext = sb("zext", [128, 2, HALO + T], F32)
        vsb = sb("vsb", [128, 2, T], F32)
        t1b = sb("t1b", [128, 2, T], F32)
        lvl = sb("lvl", [128, 2, HALO + T], F32)
        pooled = sb("pooled", [128, 2, 4, T], BF16)
        uhalo = sb("uhalo", [128, 4, HALO], F32)
        zhalo = sb("zhalo", [128, 4, HALO], F32)
        t16 = sb("t16", [128, HALO], F32)
        banks = [es.enter_context(nc.psum_tensor(f"ps{i}", [128, T], F32)) for i in range(8)]

        sem = {e: es.enter_context(nc.semaphore(f"sem_{e}")) for e in ("pe", "act", "dve")}
        dma_sems = {}

        def dsem(key):
            if key not in dma_sems:
                dma_sems[key] = es.enter_context(nc.semaphore("dma_" + "_".join(str(k) for k in key)))
            return dma_sems[key]

        ctr = {"bank": 0, "slot": 0, "sq": 0, "silu": 0, "uext": 0, "zext": 0, "vsb": 0, "t1": 0}

        def nxt(kind, n):
            v = ctr[kind]
            ctr[kind] = (v + 1) % n
            return v

        def next_bank():
            return nxt("bank", NROT)

        def x_ap(t, c):
            return xH[:, c, :] if t.is_H else xs[:, t.xslot, c, :]

        def x_reg(t, c):
            return ("xH", c) if t.is_H else ("x", t.xslot, c)

        def h_ap(t, c):
            return hH[:, c, :] if t.is_H else hb[:, t.par, c, :]

        def h_reg(t, c):
            return ("hH", c) if t.is_H else ("h", t.par, c)

        def hid_ap(t, f):
            return hidH[:, f, :] if t.is_H else hid[:, t.par, f, :]

        def hid_reg(t, f):
            return ("hidH", f) if t.is_H else ("hid", t.par, f)

        def rstd_ap(t):
            return rstdH[:, :] if t.is_H else rstd[:, t.par, :]

        def rstd_reg(t):
            return ("rstdH",) if t.is_H else ("rstd", t.par)

        slab_slot = {}
        nload = [0]
        xload_ops = {}

        def load_slab(name):
            off, n = SLABS[name]
            slot = nxt("slot", NSLOT)
            assert slot not in slab_slot.values(), (name, slot, slab_slot)
            slab_slot[name] = slot
            nload[0] += 1
            S.add("pool",
                  lambda e, slot=slot, off=off, n=n: e.dma_start(out=ring[:, slot, 0:n], in_=wst[:, off:off + n]),
                  writes=[("ring", slot)], dma_key=("ring", slot), name=f"ld{name}",
                  extra_deps=([xload_ops["A0"]] if nload[0] <= 1 else [xload_ops["B0"]] if nload[0] <= NSLOT else []))
            return slot

        pending = []
        sqstate = {0: [], 1: []}

        def defer(fn, thr, writes=(), tag=None):
            pending.append([thr, fn, set(writes), tag])

        def flush(force=False, passed=0, reads=()):
            rs = set(reads)
            for it in pending:
                it[0] -= passed
            if force or any(it[2] & rs for it in pending):
                items = list(pending)
                del pending[:]
                for it in items:
                    it[1]()
                return
            ready = [it for it in pending if it[0] <= 0]
            if ready:
                pending[:] = [it for it in pending if it[0] > 0]
                for it in ready:
                    it[1]()

        def expedite(base=8, step=2):
            for idx, it in enumerate(pending):
                it[0] = min(it[0], base + step * idx)

        def mm_group(bank, W, pairs, reads, name="", fine_reads=None):
            flush(reads=reads)
            n = len(pairs)
            if fine_reads is None:
                def emit(pe, bank=bank, W=W, pairs=pairs):
                    ins = None
                    for i, (l, r) in enumerate(pairs):
                        ins = pe.matmul(banks[bank][:, 0:W], lhsT=l, rhs=r, start=(i == 0), stop=(i == n - 1))
                    return ins
                S.add("pe", emit, reads=reads, writes=[("ps", bank)], name=name)
            else:
                for i, (l, r) in enumerate(pairs):
                    S.add("pe", lambda pe, i=i, l=l, r=r, bank=bank, W=W: pe.matmul(
                        banks[bank][:, 0:W], lhsT=l, rhs=r, start=(i == 0), stop=(i == n - 1)),
                        reads=fine_reads[i], writes=[("ps", bank)], name=name + "1")
            flush(passed=len(pairs))

        def norm_accum(t, c, thr=10):
            if t.is_H:
                S.add("act", lambda e, c=c: e.activation(out=sqH[:, c, :], in_=xH[:, c, :], func=AF.Square),
                      reads=[x_reg(t, c)], writes=[("sqH", c)], name="sqH")
                return
            i = nxt("sq", NSQ)
            if any(it[3] == ("sq", i) for it in pending):
                flush(force=True)
            S.add("act", lambda e, i=i, t=t, c=c: e.activation(out=sq[:, i, :], in_=x_ap(t, c), func=AF.Square),
                  reads=[x_reg(t, c)], writes=[("sq", i)], name="sq")
            st = sqstate[t.par]
            st.append(i)

            def sq_add(a, b):
                S.add("dve", lambda e, a=a, b=b: e.tensor_tensor(out=sq[:, a, :], in0=sq[:, a, :], in1=sq[:, b, :], op=ALU.add),
                      reads=[("sq", a), ("sq", b)], writes=[("sq", a)], name="sqadd")
            if c % 4 >= 1:
                sq_add(st[0], st[-1])
            if c % 4 == 3:
                bank = 6 + t.par
                i0_ = st[0]
                defer(lambda bank=bank, i=i0_, c=c: S.add(
                    "pe", lambda pe: pe.matmul(banks[bank][:, :], lhsT=ones[:, :], rhs=sq[:, i, :],
                                               start=(c == 3), stop=(c == DC - 1)),
                    reads=[("sq", i), ("ones",)], writes=[("ps", bank)], name="ss"), thr, tag=("sq", i0_))
                del st[:]

        def norm_rstd(t):
            W = t.W
            if t.is_H:
                bank = next_bank()
                mm_group(bank, W, [(ones[:, :], sqH[:, c, :]) for c in range(DC)],
                         reads=[("sqH", c) for c in range(DC)] + [("ones",)], name="ssH")
            else:
                bank = 6 + t.par
            S.add("act", lambda e, t=t, bank=bank, W=W: e.activation(out=rstd_ap(t), in_=banks[bank][:, 0:W], func=AF.Sqrt,
                                                                     bias=epsb[:, 0:1], scale=1.0),
                  reads=[("ps", bank), ("eps",)], writes=[rstd_reg(t)], name="sqrt")
            S.add("dve", lambda e, t=t: e.reciprocal(out=rstd_ap(t), in_=rstd_ap(t)),
                  reads=[rstd_reg(t)], writes=[rstd_reg(t)], name="recip")

        def norm_apply(t, gi, final, chunks):
            for c in chunks:
                if final:
                    S.add("dve", lambda e, t=t, c=c, gi=gi: e.scalar_tensor_tensor(
                        out=x_ap(t, c), in0=x_ap(t, c), scalar=gains[:, gi, c:c + 1], in1=rstd_ap(t),
                        op0=ALU.mult, op1=ALU.mult),
                        reads=[x_reg(t, c), rstd_reg(t), ("c_gains",)], writes=[x_reg(t, c)], name="fin")
                else:
                    S.add("dve", lambda e, t=t, c=c, gi=gi: e.scalar_tensor_tensor(
                        out=h_ap(t, c), in0=x_ap(t, c), scalar=gains[:, gi, c:c + 1], in1=rstd_ap(t),
                        op0=ALU.mult, op1=ALU.mult),
                        reads=[x_reg(t, c), rstd_reg(t), ("c_gains",)], writes=[h_reg(t, c)], name="napply")

        def store_chunks(t, q):
            S.add("sp", lambda e, t=t, q=q: e.dma_start(out=outT[:, 2 * q:2 * q + 2, t.tok0:t.tok0 + T],
                                                       in_=xs[:, t.xslot, 2 * q:2 * q + 2, :]),
                  reads=[("x", t.xslot, 2 * q), ("x", t.xslot, 2 * q + 1)], writes=[("out", t.name, q)],
                  dma_key=("st", t.xslot, q), name="store")

        def norm_finish(t, gi, final=False, thr=10, step=2):
            if t.is_H:
                norm_rstd(t)
                norm_apply(t, gi, final, range(DC))
                return
            defer(lambda: norm_rstd(t), thr)
            for q in range(4):
                def fn(q=q):
                    norm_apply(t, gi, final, (2 * q, 2 * q + 1))
                    if final:
                        store_chunks(t, q)
                wr = [] if final else [h_reg(t, 2 * q), h_reg(t, 2 * q + 1)]
                defer(fn, thr + step * (q + 1), writes=wr, tag=("final", t.xslot) if final else None)

        def gateup(k, tiles, skew=2, mid=None, mid_at=2):
            def work(s, slot, t):
                W = t.W
                for fi in range(2):
                    f = 2 * s + fi
                    hreads = [h_reg(t, kc) for kc in range(DC)] + [("ring", slot)]
                    fr = [[h_reg(t, kc), ("ring", slot)] for kc in range(DC)] if (s == 0 and fi == 0) else None
                    bg = next_bank()
                    mm_group(bg, W, [(ring[:, slot, kc * 512 + fi * 128: kc * 512 + fi * 128 + 128], h_ap(t, kc))
                                     for kc in range(DC)], hreads, name="gate", fine_reads=fr)
                    bu = next_bank()
                    mm_group(bu, W, [(ring[:, slot, kc * 512 + 256 + fi * 128: kc * 512 + 256 + fi * 128 + 128], h_ap(t, kc))
                                     for kc in range(DC)], hreads, name="up")
                    si = nxt("silu", NSILU)
                    S.add("act", lambda e, si=si, bg=bg, W=W: e.activation(out=silu[:, si, 0:W], in_=banks[bg][:, 0:W],
                                                                         func=AF.Silu),
                          reads=[("ps", bg)], writes=[("silu", si)], name="silu")
                    S.add("dve", lambda e, si=si, bu=bu, W=W, t=t, f=f: e.tensor_tensor(
                        out=hid_ap(t, f), in0=silu[:, si, 0:W], in1=banks[bu][:, 0:W], op=ALU.mult),
                        reads=[("silu", si), ("ps", bu)], writes=[hid_reg(t, f)], name="hmul")

            slots = {}
            for s in range(skew):
                nm = ("gu", k, s)
                slots[s] = slab_slot[nm] if nm in slab_slot else load_slab(nm)
            for ti, t in enumerate(tiles):
                for s in range(skew):
                    if mid is not None and ti == len(tiles) - 2 and s == mid_at:
                        mid()
                    work(s, slots[s], t)
            for s in range(skew):
                del slab_slot[("gu", k, s)]
            for s in range(skew, 11):
                slot = load_slab(("gu", k, s))
                for t in tiles:
                    work(s, slot, t)
                del slab_slot[("gu", k, s)]

        def down(k, tiles, after, exp_step=2):
            for half in range(2):
                for t in tiles:
                    W = t.W
                    for d in range(4 * half, 4 * half + 4):
                        name = ("dn", k, d)
                        if name not in slab_slot:
                            load_slab(name)
                        slot = slab_slot[name]
                        bank = next_bank()
                        mm_group(bank, W, [(ring[:, slot, fc * 128: fc * 128 + 128], hid_ap(t, fc)) for fc in range(FC)],
                                 [hid_reg(t, fc) for fc in range(FC)] + [("ring", slot)], name="down")
                        S.add("dve", lambda e, t=t, d=d, bank=bank, W=W: e.scalar_tensor_tensor(
                            out=x_ap(t, d), in0=banks[bank][:, 0:W], scalar=0.5, in1=x_ap(t, d),
                            op0=ALU.mult, op1=ALU.add),
                            reads=[("ps", bank), x_reg(t, d)], writes=[x_reg(t, d)], name="xupd")
                        norm_accum(t, d, thr=(10 if d == DC - 1 else 30))
                    if half == 1:
                        after(t)
                for d in range(4 * half, 4 * half + 4):
                    del slab_slot[("dn", k, d)]
            expedite(step=exp_step)

        def pooling(t, g, bank):
            W = t.W
            if t.is_H:
                S.add("act", lambda e, g=g, bank=bank: e.copy(out=uhalo[:, g, :], in_=banks[bank][:, 0:HALO]),
                      reads=[("ps", bank)], writes=[("uhalo", g)], name="uH")
                return
            ub = nxt("uext", 2)
            S.add("act", lambda e, ub=ub, bank=bank: e.copy(out=uext[:, ub, HALO:HALO + T], in_=banks[bank][:, :]),
                  reads=[("ps", bank)], writes=[("uext", ub)], name="ucopy")
            S.add("act", lambda e, ub=ub, g=g: e.copy(out=uext[:, ub, 0:HALO], in_=uhalo[:, g, :]),
                  reads=[("uhalo", g)], writes=[("uext", ub)], name="uhalo_in")
            m = g + 1
            start = {m: HALO}
            for i in range(m, 1, -1):
                start[i - 1] = start[i] - (1 << (i - 1))
            prev_ap, prev_reg = (lambda lo, hi, ub=ub: uext[:, ub, lo:hi]), ("uext", ub)
            for i in range(1, m + 1):
                li = i % 2
                sh = 1 << (i - 1)
                lo = start[i]
                S.add("dve", lambda e, prev_ap=prev_ap, li=li, lo=lo, sh=sh: e.tensor_tensor(
                    out=lvl[:, li, lo:HALO + T], in0=prev_ap(lo, HALO + T), in1=prev_ap(lo - sh, HALO + T - sh), op=ALU.add),
                    reads=[prev_reg], writes=[("lvl", li)], name="padd")
                prev_ap, prev_reg = (lambda lo, hi, li=li: lvl[:, li, lo:hi]), ("lvl", li)
            w = float(1 << m)
            S.add("dve", lambda e, prev_ap=prev_ap, ub=ub, g=g, w=w, t=t: e.scalar_tensor_tensor(
                out=pooled[:, t.par, g, :], in0=prev_ap(HALO, HALO + T), scalar=1.0 / w, in1=uext[:, ub, HALO:HALO + T],
                op0=ALU.mult, op1=ALU.subtract),
                reads=[prev_reg, ("uext", ub)], writes=[("pooled", t.par, g)], name="pooled")
            if t.is_first:
                S.add("dve", lambda e, prev_ap=prev_ap, g=g: e.tensor_tensor(
                    out=t16[:, :], in0=prev_ap(HALO, 2 * HALO), in1=invc[:, g, :], op=ALU.mult),
                    reads=[prev_reg, ("c_invc",)], writes=[("t16",)], name="fix1")
                S.add("dve", lambda e, ub=ub, g=g, t=t: e.tensor_tensor(
                    out=pooled[:, t.par, g, 0:HALO], in0=t16[:, :], in1=uext[:, ub, HALO:2 * HALO], op=ALU.subtract),
                    reads=[("t16",), ("uext", ub)], writes=[("pooled", t.par, g)], name="fix2")
            S.add("act", lambda e, ub=ub, g=g: e.copy(out=uhalo[:, g, :], in_=uext[:, ub, T:T + HALO]),
                  reads=[("uext", ub)], writes=[("uhalo", g)], name="uhalo_out")

        def conv(t, j, bv, bc, bb):
            W = t.W
            vb = nxt("vsb", 2)
            S.add("act", lambda e, vb=vb, bv=bv, W=W: e.copy(out=vsb[:, vb, 0:W], in_=banks[bv][:, 0:W]),
                  reads=[("ps", bv)], writes=[("vsb", vb)], name="vcopy")
            if t.is_H:
                S.add("dve", lambda e, vb=vb, bc=bc, j=j: e.tensor_tensor(
                    out=zhalo[:, j, :], in0=vsb[:, vb, 0:HALO], in1=banks[bc][:, 0:HALO], op=ALU.mult),
                    reads=[("vsb", vb), ("ps", bc)], writes=[("zhalo", j)], name="zH")
                return
            zb = nxt("zext", 2)
            tb = nxt("t1", 2)
            S.add("act", lambda e, zb=zb, j=j: e.copy(out=zext[:, zb, 0:HALO], in_=zhalo[:, j, :]),
                  reads=[("zhalo", j)], writes=[("zext", zb)], name="zhalo_in")
            S.add("dve", lambda e, zb=zb, vb=vb, bc=bc: e.tensor_tensor(
                out=zext[:, zb, HALO:HALO + T], in0=vsb[:, vb, :], in1=banks[bc][:, :], op=ALU.mult),
                reads=[("vsb", vb), ("ps", bc)], writes=[("zext", zb)], name="z")
            S.add("dve", lambda e, zb=zb, tb=tb, j=j: e.tensor_scalar(
                out=t1b[:, tb, :], in0=zext[:, zb, HALO - 2:HALO - 2 + T], scalar1=convw[:, j, 0:1], scalar2=None,
                op0=ALU.mult),
                reads=[("zext", zb), ("c_convw",)], writes=[("t1", tb)], name="c0")
            for k in (1, 2):
                S.add("dve", lambda e, zb=zb, tb=tb, j=j, k=k: e.scalar_tensor_tensor(
                    out=t1b[:, tb, :], in0=zext[:, zb, HALO - 2 + k:HALO - 2 + k + T], scalar=convw[:, j, k:k + 1],
                    in1=t1b[:, tb, :], op0=ALU.mult, op1=ALU.add),
                    reads=[("zext", zb), ("t1", tb), ("c_convw",)], writes=[("t1", tb)], name="c12")
            S.add("dve", lambda e, tb=tb, bb=bb, t=t, j=j: e.tensor_tensor(
                out=hid_ap(t, j), in0=t1b[:, tb, :], in1=banks[bb][:, :], op=ALU.mult),
                reads=[("t1", tb), ("ps", bb)], writes=[hid_reg(t, j)], name="ya")
            S.add("act", lambda e, zb=zb, j=j: e.copy(out=zhalo[:, j, :], in_=zext[:, zb, T:T + HALO]),
                  reads=[("zext", zb)], writes=[("zhalo", j)], name="zhalo_out")

        def pool_mm(t):
            for g in range(4):
                bank = next_bank()
                mm_group(bank, T, [(poolw[:, g, :], pooled[:, t.par, g, :])], [("pooled", t.par, g), ("poolw",)], name="poolmm")
                S.add("act", lambda e, t=t, g=g, bank=bank: e.mul(out=hid_ap(t, 4 + g), in_=banks[bank][:, :],
                                                                  mul=pscale[:, g:g + 1]),
                      reads=[("ps", bank), ("c_pscale",)], writes=[hid_reg(t, 4 + g)], name="yb")

        def w_in(tiles):
            def u_work(slot, t):
                for g in range(4):
                    bank = next_bank()
                    mm_group(bank, t.W, [(ring[:, slot, kc * 512 + g * 128: kc * 512 + g * 128 + 128], h_ap(t, kc))
                                         for kc in range(DC)],
                             [h_reg(t, kc) for kc in range(DC)] + [("ring", slot)], name="win_u",
                             fine_reads=([[h_reg(t, kc), ("ring", slot)] for kc in range(DC)] if g == 0 else None))
                    pooling(t, g, bank)

            def j_work(j, slot, t):
                bl = []
                for q in range(2 if t.is_H else 3):
                    bank = next_bank()
                    mm_group(bank, t.W, [(ring[:, slot, kc * 384 + q * 128: kc * 384 + q * 128 + 128], h_ap(t, kc))
                                         for kc in range(DC)],
                             [h_reg(t, kc) for kc in range(DC)] + [("ring", slot)], name="win_j")
                    bl.append(bank)
                conv(t, j, bl[0], bl[1], bl[2] if len(bl) > 2 else None)
                if j == 1 and not t.is_H:
                    pool_mm(t)

            slot_u = load_slab(("win", "u"))
            slot_0 = load_slab(("win", 0))
            for t in tiles:
                u_work(slot_u, t)
                j_work(0, slot_0, t)
            del slab_slot[("win", "u")]
            del slab_slot[("win", 0)]
            for j in range(1, 4):
                slot = load_slab(("win", j))
                for t in tiles:
                    j_work(j, slot, t)
                del slab_slot[("win", j)]

        def w_out(tiles, after):
            slots = [load_slab(("wo", 0)), load_slab(("wo", 1))]
            for t in tiles:
                for s in range(2):
                    slot = slots[s]
                    for oi in range(4):
                        o = 4 * s + oi
                        bank = next_bank()
                        mm_group(bank, T, [(ring[:, slot, kc * 512 + oi * 128: kc * 512 + oi * 128 + 128], hid_ap(t, kc))
                                           for kc in range(DC)],
                                 [hid_reg(t, kc) for kc in range(DC)] + [("ring", slot)], name="wout")
                        S.add("dve", lambda e, t=t, o=o, bank=bank: e.tensor_tensor(
                            out=x_ap(t, o), in0=banks[bank][:, :], in1=x_ap(t, o), op=ALU.add),
                            reads=[("ps", bank), x_reg(t, o)], writes=[x_reg(t, o)], name="xadd")
                        norm_accum(t, o, thr=(12 if o == DC - 1 else 40))
                after(t)
            del slab_slot[("wo", 0)]
            del slab_slot[("wo", 1)]
            expedite(step=2)

        S.add("dve", lambda e: e.memset(ones[:, :], 1.0 / D), writes=[("ones",)], name="ones")
        S.add("dve", lambda e: e.memset(epsb[:, :], EPS), writes=[("eps",)], name="eps")
        S.add("act", lambda e: e.activation(out=t16[:, 0:1], in_=epsb[:, 0:1], func=AF.Square), reads=[("eps",)], writes=[("t16",)], name="warm")

        tH = TileD("H", HALO, None, 0, -HALO, is_H=True)
        pairs = [
            [TileD("A0", T, 0, 0, 0, is_first=True), TileD("B0", T, 1, 1, T)],
            [TileD("A1", T, 2, 0, 2 * T), TileD("B1", T, 0, 1, 3 * T)],
        ]

        def load_x(t):
            if t.is_H:
                S.add("sp", lambda e: e.dma_start(out=xH[:, :, :], in_=xT[:, :, 0:HALO]),
                      writes=[("xH", c) for c in range(DC)], dma_key=("xH",), name="ldxH")
            else:
                assert not any(it[3] == ("final", t.xslot) for it in pending)
                for q in range(4):
                    xload_ops[t.name] = S.add("sp", lambda e, t=t, q=q: e.dma_start(
                        out=xs[:, t.xslot, 2 * q:2 * q + 2, :],
                        in_=xT[:, 2 * q:2 * q + 2, HALO + t.tok0:HALO + t.tok0 + T]),
                        writes=[("x", t.xslot, 2 * q), ("x", t.xslot, 2 * q + 1)], dma_key=("x", t.xslot, q), name="ldx",
                        )

        stores = []

        def finish_tile(t):
            norm_finish(t, G_FINAL, final=True)

        S.add("sp", lambda e: e.dma_start(out=gains[:, :, :], in_=gains_d[:, :, :]), writes=[("c_gains",)],
              dma_key=("c", 0), name="ldgains")
        load_x(tH)
        load_x(pairs[0][0])
        S.add("sp", lambda e: e.dma_start(out=convw[:, :, :], in_=convw_d[:, :, :]), writes=[("c_convw",)],
              dma_key=("c", 1), name="ldconvw")
        S.add("sp", lambda e: e.dma_start(out=pscale[:, :], in_=pscale_d[:, :]), writes=[("c_pscale",)],
              dma_key=("c", 2), name="ldpscale")
        S.add("sp", lambda e: e.dma_start(out=invc[:, :, :], in_=invc_d[:, :, :]), writes=[("c_invc",)],
              dma_key=("c", 3), name="ldinvc")
        load_x(pairs[0][1])

        def initial_norm(t, thr_a=10, thr_b=10):
            for c in range(DC):
                norm_accum(t, c, thr=(thr_b if c == DC - 1 else thr_a))
            norm_finish(t, G_FFN1, thr=thr_b, step=2)

        for p, (tA, tB) in enumerate(pairs):
            tiles = ([tH] if p == 0 else []) + [tA, tB]
            if p == 0:
                for t in tiles:
                    initial_norm(t)
                gateup(1, tiles, skew=3)
                S.add("pool", lambda e: e.dma_start(out=poolw[:, :, :], in_=poolw_d[:, :, :]), writes=[("poolw",)],
                      dma_key=("c", 4), name="ldpoolw")
                load_x(pairs[1][0])
            else:
                load_x(tB)
                gateup(1, tiles, skew=5, mid=lambda: initial_norm(tB, 30, 40))
            down(1, tiles, lambda t: norm_finish(t, G_MIX))
            w_in(tiles)
            w_out([tA, tB], lambda t: norm_finish(t, G_FFN2, thr=12, step=4))
            gateup(2, [tA, tB])
            if p == 0:
                initial_norm(pairs[1][0], 30, 40)
            down(2, [tA, tB], finish_tile, exp_step=16)
        flush(force=True)
        S.add("sp", lambda e: None,
              reads=[("out", n, q) for n in ("A0", "B0", "A1", "B1") for q in range(4)], name="final")

        S.plan()
        global _LAST_SCHED
        _LAST_SCHED = S

        with nc.Block() as block:
            def run(eng_name, e):
                for op in S.ops[eng_name]:
                    for dep in op.waits:
                        if dep.dma_key is not None:
                            e.wait_ge(dsem(dep.dma_key), 16 * dep.dma_cnt)
                        else:
                            e.wait_ge(sem[dep.eng], dep.sigidx)
                    ins = op.emit(e)
                    if ins is None:
                        continue
                    if op.dma_key is not None:
                        ins.then_inc(dsem(op.dma_key), 16)
                    elif op.signal:
                        ins.then_inc(sem[op.eng], 1)

            @block.tensor
            def _(e):
                run("pe", e)

            @block.scalar
            def _(e):
                run("act", e)

            @block.vector
            def _(e):
                run("dve", e)

            @block.gpsimd
            def _(e):
                run("pool", e)

            @block.sync
            def _(e):
                run("sp", e)
    return nc


_PROGRAM = None
_LAST_SCHED = None


def _prep_inputs(inp):
    x = np.asarray(inp["x"], np.float32)
    wst = _build_wstream(inp)
    gains = np.stack([np.asarray(inp["norm_ffn1"][0]), np.asarray(inp["norm_mix"][0]),
                      np.asarray(inp["norm_ffn2"][0]), np.asarray(inp["norm_final"])], axis=0)
    gains = np.ascontiguousarray(gains.reshape(4, DC, 128).transpose(2, 0, 1)).astype(np.float32)
    convw = np.ascontiguousarray(np.asarray(inp["conv_w"][0]).reshape(3, 4, 128).transpose(2, 1, 0)).astype(np.float32)
    pscale = np.ascontiguousarray(np.asarray(inp["pool_scale"][0]).reshape(4, 128).T).astype(np.float32)
    poolw = np.ascontiguousarray(np.asarray(inp["pool_w"][0]).transpose(1, 0, 2)).astype(np.float32)
    t1 = np.arange(1, HALO + 1, dtype=np.float32)
    invc_first = np.stack([1.0 / np.minimum(t1, float(w)) for w in WINDOWS], axis=0)
    invc_rest = np.stack([np.full(HALO, 1.0 / w, np.float32) for w in WINDOWS], axis=0)
    in_maps = []
    for core in range(NCORE):
        b, h = core // 2, core % 2
        xt = np.zeros((128, DC, HALO + TOK), np.float32)
        lo = h * TOK - HALO
        src = x[b, max(lo, 0):h * TOK + TOK, :]
        src = src.reshape(src.shape[0], DC, 128).transpose(2, 1, 0)
        xt[:, :, HALO + TOK - src.shape[2]:] = src
        ic = invc_first if h == 0 else invc_rest
        in_maps.append({
            "xT": xt, "wst": wst, "gains": gains, "convw": convw, "pscale": pscale,
            "invc": np.ascontiguousarray(np.broadcast_to(ic[None], (128, 4, HALO))).astype(np.float32),
            "poolw": poolw,
        })
    return in_maps


def kernel(**inputs):
    global _PROGRAM
    if _PROGRAM is None:
        _PROGRAM = build_program()
    in_maps = _prep_inputs(inputs)
    res = run_bass_kernel_spmd(_PROGRAM, in_maps, core_ids=list(range(NCORE)))
    out = np.empty((4, 4096, D), np.float32)
    for core in range(NCORE):
        b, h = core // 2, core % 2
        o = res.results[core]["outT"]
        out[b, h * TOK:(h + 1) * TOK, :] = o.transpose(2, 1, 0).reshape(TOK, D)
    return out
```

```python
import numpy as np
import concourse.bass as bass
import concourse.mybir as mybir
from concourse.bass_utils import run_bass_kernel_spmd

F32 = mybir.dt.float32
BF16 = mybir.dt.bfloat16
AF = mybir.ActivationFunctionType
ALU = mybir.AluOpType

D = 1024
DC = 8
DFF = 2816
FC = 22
T = 512
HALO = 16
TOK = 2048
NCORE = 8
EPS = 1e-6
WINDOWS = (2, 4, 8, 16)

NSLOT = 6
SLOT = 4096
NROT = 6
NSQ = 8
NSILU = 3

G_FFN1, G_MIX, G_FFN2, G_FINAL = 0, 1, 2, 3


def _slab_table():
    tab = {}
    off = 0

    def put(name, n):
        nonlocal off
        tab[name] = (off, n)
        off += n

    for k in (1, 2):
        for s in range(11):
            put(("gu", k, s), 8 * 2 * 256)
        for d in range(8):
            put(("dn", k, d), FC * 128)
    put(("win", "u"), 8 * 512)
    for j in range(4):
        put(("win", j), 8 * 384)
    for s in range(2):
        put(("wo", s), 8 * 512)
    return tab, off


SLABS, WCOLS = _slab_table()


def _build_wstream(inp):
    w = np.empty((128, WCOLS), np.float32)

    def kview(m):
        K, N = m.shape
        return m.reshape(K // 128, 128, N).transpose(1, 0, 2)

    for k in (1, 2):
        wg = kview(np.asarray(inp[f"ffn{k}_w_gate"][0]))
        wu = kview(np.asarray(inp[f"ffn{k}_w_up"][0]))
        wd = kview(np.asarray(inp[f"ffn{k}_w_down"][0]))
        for s in range(11):
            off, n = SLABS[("gu", k, s)]
            blk = np.stack([wg[:, :, 256 * s:256 * (s + 1)], wu[:, :, 256 * s:256 * (s + 1)]], axis=2)
            w[:, off:off + n] = blk.reshape(128, n)
        for d in range(8):
            off, n = SLABS[("dn", k, d)]
            w[:, off:off + n] = wd[:, :, 128 * d:128 * (d + 1)].reshape(128, n)
    win = kview(np.asarray(inp["w_in"][0]))
    off, n = SLABS[("win", "u")]
    w[:, off:off + n] = win[:, :, 1536:2048].reshape(128, n)
    for j in range(4):
        off, n = SLABS[("win", j)]
        blk = np.stack([win[:, :, 0 + 128 * j:128 * (j + 1)],
                        win[:, :, 1024 + 128 * j:1024 + 128 * (j + 1)],
                        win[:, :, 512 + 128 * j:512 + 128 * (j + 1)]],
                       axis=2)
        w[:, off:off + n] = blk.reshape(128, n)
    wo = kview(np.asarray(inp["w_out"][0]))
    for s in range(2):
        off, n = SLABS[("wo", s)]
        w[:, off:off + n] = wo[:, :, 512 * s:512 * (s + 1)].reshape(128, n)
    return w


ENGS = ("pe", "act", "dve", "pool", "sp")


class Op:
    __slots__ = ("eng", "emit", "deps", "pos", "signal", "sigidx", "dma_key", "dma_cnt", "name", "waits")

    def __init__(self, eng, emit, name):
        self.eng = eng
        self.emit = emit
        self.deps = {}
        self.signal = False
        self.sigidx = 0
        self.dma_key = None
        self.dma_cnt = 0
        self.name = name
        self.waits = []


class Sched:
    def __init__(self):
        self.ops = {e: [] for e in ENGS}
        self.last_writer = {}
        self.readers = {}
        self.dma_counts = {}

    def add(self, eng, emit, reads=(), writes=(), dma_key=None, name="", extra_deps=()):
        op = Op(eng, emit, name)
        for dep in extra_deps:
            op.deps[dep] = "raw"
        for r in reads:
            w = self.last_writer.get(r)
            if w is not None:
                op.deps[w] = "raw"
        for r in writes:
            w = self.last_writer.get(r)
            if w is not None and w not in op.deps:
                op.deps[w] = "waw"
            for rd in self.readers.get(r, ()):
                if rd not in op.deps:
                    op.deps[rd] = "war"
        op.deps.pop(op, None)
        for r in reads:
            self.readers.setdefault(r, []).append(op)
        for r in writes:
            self.last_writer[r] = op
            self.readers[r] = []
        if dma_key is not None:
            op.dma_key = dma_key
            self.dma_counts[dma_key] = self.dma_counts.get(dma_key, 0) + 1
            op.dma_cnt = self.dma_counts[dma_key]
        op.pos = len(self.ops[eng])
        self.ops[eng].append(op)
        return op

    def plan(self):
        for eng in ENGS:
            maxpos = {}
            maxdma = {}
            for op in self.ops[eng]:
                for dep, kind in sorted(op.deps.items(), key=lambda kv: -kv[0].pos):
                    if dep.dma_key is not None:
                        if maxdma.get(dep.dma_key, 0) >= dep.dma_cnt:
                            continue
                        maxdma[dep.dma_key] = dep.dma_cnt
                        op.waits.append(dep)
                        continue
                    if dep.eng == eng:
                        if eng == "pe" or kind != "raw":
                            continue
                    if maxpos.get(dep.eng, -1) >= dep.pos:
                        continue
                    maxpos[dep.eng] = dep.pos
                    dep.signal = True
                    op.waits.append(dep)
        for eng in ENGS:
            n = 0
            for op in self.ops[eng]:
                if op.signal:
                    n += 1
                    op.sigidx = n


class TileD:
    def __init__(self, name, W, xslot, par, tok0, is_H=False, is_first=False):
        self.name, self.W, self.xslot, self.par, self.tok0 = name, W, xslot, par, tok0
        self.is_H, self.is_first = is_H, is_first


def build_program():
    nc = bass.Bass("TRN2", target_bir_lowering=False)
    xT = nc.dram_tensor("xT", [128, DC, HALO + TOK], F32, kind="ExternalInput").ap()
    wst = nc.dram_tensor("wst", [128, WCOLS], F32, kind="ExternalInput").ap()
    gains_d = nc.dram_tensor("gains", [128, 4, DC], F32, kind="ExternalInput").ap()
    convw_d = nc.dram_tensor("convw", [128, 4, 3], F32, kind="ExternalInput").ap()
    pscale_d = nc.dram_tensor("pscale", [128, 4], F32, kind="ExternalInput").ap()
    invc_d = nc.dram_tensor("invc", [128, 4, HALO], F32, kind="ExternalInput").ap()
    poolw_d = nc.dram_tensor("poolw", [128, 4, 128], F32, kind="ExternalInput").ap()
    outT = nc.dram_tensor("outT", [128, DC, TOK], F32, kind="ExternalOutput").ap()

    S = Sched()
    from contextlib import ExitStack
    with ExitStack() as es:
        def sb(name, shape, dt):
            return es.enter_context(nc.sbuf_tensor(name, shape, dt))

        xs = sb("xs", [128, 3, DC, T], F32)
        xH = sb("xH", [128, DC, HALO], F32)
        hb = sb("hb", [128, 2, DC, T], BF16)
        hH = sb("hH", [128, DC, HALO], BF16)
        hid = sb("hid", [128, 2, FC, T], BF16)
        hidH = sb("hidH", [128, FC, HALO], BF16)
        ring = sb("ring", [128, NSLOT, SLOT], BF16)
        sq = sb("sq", [128, NSQ, T], BF16)
        sqH = sb("sqH", [128, DC, HALO], BF16)
        rstd = sb("rstd", [128, 2, T], F32)
        rstdH = sb("rstdH", [128, HALO], F32)
        silu = sb("silu", [128, NSILU, T], F32)
        gains = sb("gains_sb", [128, 4, DC], F32)
        convw = sb("convw_sb", [128, 4, 3], F32)
        pscale = sb("pscale_sb", [128, 4], F32)
        invc = sb("invc_sb", [128, 4, HALO], F32)
        poolw = sb("poolw_sb", [128, 4, 128], BF16)
        ones = sb("ones_sb", [128, 128], BF16)
        epsb = sb("eps_sb", [128, 1], F32)
        uext = sb("uext", [128, 2, HALO + T], F32)
        zext = sb("zext", [128, 2, HALO + T], F32)
        vsb = sb("vsb", [128, 2, T], F32)
        t1b = sb("t1b", [128, 2, T], F32)
        lvl = sb("lvl", [128, 2, HALO + T], F32)
        pooled = sb("pooled", [128, 2, 4, T], BF16)
        uhalo = sb("uhalo", [128, 4, HALO], F32)
        zhalo = sb("zhalo", [128, 4, HALO], F32)
        t16 = sb("t16", [128, HALO], F32)
        banks = [es.enter_context(nc.psum_tensor(f"ps{i}", [128, T], F32)) for i in range(8)]

        sem = {e: es.enter_context(nc.semaphore(f"sem_{e}")) for e in ("pe", "act", "dve")}
        dma_sems = {}

        def dsem(key):
            if key not in dma_sems:
                dma_sems[key] = es.enter_context(nc.semaphore("dma_" + "_".join(str(k) for k in key)))
            return dma_sems[key]

        ctr = {"bank": 0, "slot": 0, "sq": 0, "silu": 0, "uext": 0, "zext": 0, "vsb": 0, "t1": 0}

        def nxt(kind, n):
            v = ctr[kind]
            ctr[kind] = (v + 1) % n
            return v

        def next_bank():
            return nxt("bank", NROT)

        def x_ap(t, c):
            return xH[:, c, :] if t.is_H else xs[:, t.xslot, c, :]

        def x_reg(t, c):
            return ("xH", c) if t.is_H else ("x", t.xslot, c)

        def h_ap(t, c):
            return hH[:, c, :] if t.is_H else hb[:, t.par, c, :]

        def h_reg(t, c):
            return ("hH", c) if t.is_H else ("h", t.par, c)

        def hid_ap(t, f):
            return hidH[:, f, :] if t.is_H else hid[:, t.par, f, :]

        def hid_reg(t, f):
            return ("hidH", f) if t.is_H else ("hid", t.par, f)

        def rstd_ap(t):
            return rstdH[:, :] if t.is_H else rstd[:, t.par, :]

        def rstd_reg(t):
            return ("rstdH",) if t.is_H else ("rstd", t.par)

        slab_slot = {}
        nload = [0]
        xload_ops = {}

        def load_slab(name):
            off, n = SLABS[name]
            slot = nxt("slot", NSLOT)
            assert slot not in slab_slot.values(), (name, slot, slab_slot)
            slab_slot[name] = slot
            nload[0] += 1
            S.add("pool",
                  lambda e, slot=slot, off=off, n=n: e.dma_start(out=ring[:, slot, 0:n], in_=wst[:, off:off + n]),
                  writes=[("ring", slot)], dma_key=("ring", slot), name=f"ld{name}",
                  extra_deps=([xload_ops["A0"]] if nload[0] <= 1 else [xload_ops["B0"]] if nload[0] <= NSLOT else []))
            return slot

        pending = []
        sqstate = {0: [], 1: []}

        def defer(fn, thr, writes=(), tag=None):
            pending.append([thr, fn, set(writes), tag])

        def flush(force=False, passed=0, reads=()):
            rs = set(reads)
            for it in pending:
                it[0] -= passed
            if force or any(it[2] & rs for it in pending):
                items = list(pending)
                del pending[:]
                for it in items:
                    it[1]()
                return
            ready = [it for it in pending if it[0] <= 0]
            if ready:
                pending[:] = [it for it in pending if it[0] > 0]
                for it in ready:
                    it[1]()

        def expedite(base=8, step=2):
            for idx, it in enumerate(pending):
                it[0] = min(it[0], base + step * idx)

        def mm_group(bank, W, pairs, reads, name="", fine_reads=None):
            flush(reads=reads)
            n = len(pairs)
            if fine_reads is None:
                def emit(pe, bank=bank, W=W, pairs=pairs):
                    ins = None
                    for i, (l, r) in enumerate(pairs):
                        ins = pe.matmul(banks[bank][:, 0:W], lhsT=l, rhs=r, start=(i == 0), stop=(i == n - 1))
                    return ins
                S.add("pe", emit, reads=reads, writes=[("ps", bank)], name=name)
            else:
                for i, (l, r) in enumerate(pairs):
                    S.add("pe", lambda pe, i=i, l=l, r=r, bank=bank, W=W: pe.matmul(
                        banks[bank][:, 0:W], lhsT=l, rhs=r, start=(i == 0), stop=(i == n - 1)),
                        reads=fine_reads[i], writes=[("ps", bank)], name=name + "1")
            flush(passed=len(pairs))

        def norm_accum(t, c, thr=10):
            if t.is_H:
                S.add("act", lambda e, c=c: e.activation(out=sqH[:, c, :], in_=xH[:, c, :], func=AF.Square),
                      reads=[x_reg(t, c)], writes=[("sqH", c)], name="sqH")
                return
            i = nxt("sq", NSQ)
            if any(it[3] == ("sq", i) for it in pending):
                flush(force=True)
            S.add("act", lambda e, i=i, t=t, c=c: e.activation(out=sq[:, i, :], in_=x_ap(t, c), func=AF.Square),
                  reads=[x_reg(t, c)], writes=[("sq", i)], name="sq")
            st = sqstate[t.par]
            st.append(i)

            def sq_add(a, b):
                S.add("dve", lambda e, a=a, b=b: e.tensor_tensor(out=sq[:, a, :], in0=sq[:, a, :], in1=sq[:, b, :], op=ALU.add),
                      reads=[("sq", a), ("sq", b)], writes=[("sq", a)], name="sqadd")
            if c % 4 >= 1:
                sq_add(st[0], st[-1])
            if c % 4 == 3:
                bank = 6 + t.par
                i0_ = st[0]
                defer(lambda bank=bank, i=i0_, c=c: S.add(
                    "pe", lambda pe: pe.matmul(banks[bank][:, :], lhsT=ones[:, :], rhs=sq[:, i, :],
                                               start=(c == 3), stop=(c == DC - 1)),
                    reads=[("sq", i), ("ones",)], writes=[("ps", bank)], name="ss"), thr, tag=("sq", i0_))
                del st[:]

        def norm_rstd(t):
            W = t.W
            if t.is_H:
                bank = next_bank()
                mm_group(bank, W, [(ones[:, :], sqH[:, c, :]) for c in range(DC)],
                         reads=[("sqH", c) for c in range(DC)] + [("ones",)], name="ssH")
            else:
                bank = 6 + t.par
            S.add("act", lambda e, t=t, bank=bank, W=W: e.activation(out=rstd_ap(t), in_=banks[bank][:, 0:W], func=AF.Sqrt,
                                                                     bias=epsb[:, 0:1], scale=1.0),
                  reads=[("ps", bank), ("eps",)], writes=[rstd_reg(t)], name="sqrt")
            S.add("dve", lambda e, t=t: e.reciprocal(out=rstd_ap(t), in_=rstd_ap(t)),
                  reads=[rstd_reg(t)], writes=[rstd_reg(t)], name="recip")

        def norm_apply(t, gi, final, chunks):
            for c in chunks:
                if final:
                    S.add("dve", lambda e, t=t, c=c, gi=gi: e.scalar_tensor_tensor(
                        out=x_ap(t, c), in0=x_ap(t, c), scalar=gains[:, gi, c:c + 1], in1=rstd_ap(t),
                        op0=ALU.mult, op1=ALU.mult),
                        reads=[x_reg(t, c), rstd_reg(t), ("c_gains",)], writes=[x_reg(t, c)], name="fin")
                else:
                    S.add("dve", lambda e, t=t, c=c, gi=gi: e.scalar_tensor_tensor(
                        out=h_ap(t, c), in0=x_ap(t, c), scalar=gains[:, gi, c:c + 1], in1=rstd_ap(t),
                        op0=ALU.mult, op1=ALU.mult),
                        reads=[x_reg(t, c), rstd_reg(t), ("c_gains",)], writes=[h_reg(t, c)], name="napply")

        def store_chunks(t, q):
            S.add("sp", lambda e, t=t, q=q: e.dma_start(out=outT[:, 2 * q:2 * q + 2, t.tok0:t.tok0 + T],
                                                       in_=xs[:, t.xslot, 2 * q:2 * q + 2, :]),
                  reads=[("x", t.xslot, 2 * q), ("x", t.xslot, 2 * q + 1)], writes=[("out", t.name, q)],
                  dma_key=("st", t.xslot, q), name="store")

        def norm_finish(t, gi, final=False, thr=10, step=2):
            if t.is_H:
                norm_rstd(t)
                norm_apply(t, gi, final, range(DC))
                return
            defer(lambda: norm_rstd(t), thr)
            for q in range(4):
                def fn(q=q):
                    norm_apply(t, gi, final, (2 * q, 2 * q + 1))
                    if final:
                        store_chunks(t, q)
                wr = [] if final else [h_reg(t, 2 * q), h_reg(t, 2 * q + 1)]
                defer(fn, thr + step * (q + 1), writes=wr, tag=("final", t.xslot) if final else None)

        def gateup(k, tiles, skew=2, mid=None, mid_at=2):
            def work(s, slot, t):
                W = t.W
                for fi in range(2):
                    f = 2 * s + fi
                    hreads = [h_reg(t, kc) for kc in range(DC)] + [("ring", slot)]
                    fr = [[h_reg(t, kc), ("ring", slot)] for kc in range(DC)] if (s == 0 and fi == 0) else None
                    bg = next_bank()
                    mm_group(bg, W, [(ring[:, slot, kc * 512 + fi * 128: kc * 512 + fi * 128 + 128], h_ap(t, kc))
                                     for kc in range(DC)], hreads, name="gate", fine_reads=fr)
                    bu = next_bank()
                    mm_group(bu, W, [(ring[:, slot, kc * 512 + 256 + fi * 128: kc * 512 + 256 + fi * 128 + 128], h_ap(t, kc))
                                     for kc in range(DC)], hreads, name="up")
                    si = nxt("silu", NSILU)
                    S.add("act", lambda e, si=si, bg=bg, W=W: e.activation(out=silu[:, si, 0:W], in_=banks[bg][:, 0:W],
                                                                         func=AF.Silu),
                          reads=[("ps", bg)], writes=[("silu", si)], name="silu")
                    S.add("dve", lambda e, si=si, bu=bu, W=W, t=t, f=f: e.tensor_tensor(
                        out=hid_ap(t, f), in0=silu[:, si, 0:W], in1=banks[bu][:, 0:W], op=ALU.mult),
                        reads=[("silu", si), ("ps", bu)], writes=[hid_reg(t, f)], name="hmul")

            slots = {}
            for s in range(skew):
                nm = ("gu", k, s)
                slots[s] = slab_slot[nm] if nm in slab_slot else load_slab(nm)
            for ti, t in enumerate(tiles):
                for s in range(skew):
                    if mid is not None and ti == len(tiles) - 2 and s == mid_at:
                        mid()
                    work(s, slots[s], t)
            for s in range(skew):
                del slab_slot[("gu", k, s)]
            for s in range(skew, 11):
                slot = load_slab(("gu", k, s))
                for t in tiles:
                    work(s, slot, t)
                del slab_slot[("gu", k, s)]

        def down(k, tiles, after, exp_step=2):
            for half in range(2):
                for t in tiles:
                    W = t.W
                    for d in range(4 * half, 4 * half + 4):
                        name = ("dn", k, d)
                        if name not in slab_slot:
                            load_slab(name)
                        slot = slab_slot[name]
                        bank = next_bank()
                        mm_group(bank, W, [(ring[:, slot, fc * 128: fc * 128 + 128], hid_ap(t, fc)) for fc in range(FC)],
                                 [hid_reg(t, fc) for fc in range(FC)] + [("ring", slot)], name="down")
                        S.add("dve", lambda e, t=t, d=d, bank=bank, W=W: e.scalar_tensor_tensor(
                            out=x_ap(t, d), in0=banks[bank][:, 0:W], scalar=0.5, in1=x_ap(t, d),
                            op0=ALU.mult, op1=ALU.add),
                            reads=[("ps", bank), x_reg(t, d)], writes=[x_reg(t, d)], name="xupd")
                        norm_accum(t, d, thr=(10 if d == DC - 1 else 30))
                    if half == 1:
                        after(t)
                for d in range(4 * half, 4 * half + 4):
                    del slab_slot[("dn", k, d)]
            expedite(step=exp_step)

        def pooling(t, g, bank):
            W = t.W
            if t.is_H:
                S.add("act", lambda e, g=g, bank=bank: e.copy(out=uhalo[:, g, :], in_=banks[bank][:, 0:HALO]),
                      reads=[("ps", bank)], writes=[("uhalo", g)], name="uH")
                return
            ub = nxt("uext", 2)
            S.add("act", lambda e, ub=ub, bank=bank: e.copy(out=uext[:, ub, HALO:HALO + T], in_=banks[bank][:, :]),
                  reads=[("ps", bank)], writes=[("uext", ub)], name="ucopy")
            S.add("act", lambda e, ub=ub, g=g: e.copy(out=uext[:, ub, 0:HALO], in_=uhalo[:, g, :]),
                  reads=[("uhalo", g)], writes=[("uext", ub)], name="uhalo_in")
            m = g + 1
            start = {m: HALO}
            for i in range(m, 1, -1):
                start[i - 1] = start[i] - (1 << (i - 1))
            prev_ap, prev_reg = (lambda lo, hi, ub=ub: uext[:, ub, lo:hi]), ("uext", ub)
            for i in range(1, m + 1):
                li = i % 2
                sh = 1 << (i - 1)
                lo = start[i]
                S.add("dve", lambda e, prev_ap=prev_ap, li=li, lo=lo, sh=sh: e.tensor_tensor(
                    out=lvl[:, li, lo:HALO + T], in0=prev_ap(lo, HALO + T), in1=prev_ap(lo - sh, HALO + T - sh), op=ALU.add),
                    reads=[prev_reg], writes=[("lvl", li)], name="padd")
                prev_ap, prev_reg = (lambda lo, hi, li=li: lvl[:, li, lo:hi]), ("lvl", li)
            w = float(1 << m)
            S.add("dve", lambda e, prev_ap=prev_ap, ub=ub, g=g, w=w, t=t: e.scalar_tensor_tensor(
                out=pooled[:, t.par, g, :], in0=prev_ap(HALO, HALO + T), scalar=1.0 / w, in1=uext[:, ub, HALO:HALO + T],
                op0=ALU.mult, op1=ALU.subtract),
                reads=[prev_reg, ("uext", ub)], writes=[("pooled", t.par, g)], name="pooled")
            if t.is_first:
                S.add("dve", lambda e, prev_ap=prev_ap, g=g: e.tensor_tensor(
                    out=t16[:, :], in0=prev_ap(HALO, 2 * HALO), in1=invc[:, g, :], op=ALU.mult),
                    reads=[prev_reg, ("c_invc",)], writes=[("t16",)], name="fix1")
                S.add("dve", lambda e, ub=ub, g=g, t=t: e.tensor_tensor(
                    out=pooled[:, t.par, g, 0:HALO], in0=t16[:, :], in1=uext[:, ub, HALO:2 * HALO], op=ALU.subtract),
                    reads=[("t16",), ("uext", ub)], writes=[("pooled", t.par, g)], name="fix2")
            S.add("act", lambda e, ub=ub, g=g: e.copy(out=uhalo[:, g, :], in_=uext[:, ub, T:T + HALO]),
                  reads=[("uext", ub)], writes=[("uhalo", g)], name="uhalo_out")

        def conv(t, j, bv, bc, bb):
            W = t.W
            vb = nxt("vsb", 2)
            S.add("act", lambda e, vb=vb, bv=bv, W=W: e.copy(out=vsb[:, vb, 0:W], in_=banks[bv][:, 0:W]),
                  reads=[("ps", bv)], writes=[("vsb", vb)], name="vcopy")
            if t.is_H:
                S.add("dve", lambda e, vb=vb, bc=bc, j=j: e.tensor_tensor(
                    out=zhalo[:, j, :], in0=vsb[:, vb, 0:HALO], in1=banks[bc][:, 0:HALO], op=ALU.mult),
                    reads=[("vsb", vb), ("ps", bc)], writes=[("zhalo", j)], name="zH")
                return
            zb = nxt("zext", 2)
            tb = nxt("t1", 2)
            S.add("act", lambda e, zb=zb, j=j: e.copy(out=zext[:, zb, 0:HALO], in_=zhalo[:, j, :]),
                  reads=[("zhalo", j)], writes=[("zext", zb)], name="zhalo_in")
            S.add("dve", lambda e, zb=zb, vb=vb, bc=bc: e.tensor_tensor(
                out=zext[:, zb, HALO:HALO + T], in0=vsb[:, vb, :], in1=banks[bc][:, :], op=ALU.mult),
                reads=[("vsb", vb), ("ps", bc)], writes=[("zext", zb)], name="z")
            si = nxt("silu", NSILU)
            S.add("act", lambda e, si=si, bb=bb: e.copy(out=silu[:, si, :], in_=banks[bb][:, :]),
                  reads=[("ps", bb)], writes=[("silu", si)], name="bcopy")
            S.add("dve", lambda e, zb=zb, tb=tb, j=j: e.tensor_scalar(
                out=t1b[:, tb, :], in0=zext[:, zb, HALO - 2:HALO - 2 + T], scalar1=convw[:, j, 0:1], scalar2=None,
                op0=ALU.mult),
                reads=[("zext", zb), ("c_convw",)], writes=[("t1", tb)], name="c0")
            for k in (1, 2):
                S.add("dve", lambda e, zb=zb, tb=tb, j=j, k=k: e.scalar_tensor_tensor(
                    out=t1b[:, tb, :], in0=zext[:, zb, HALO - 2 + k:HALO - 2 + k + T], scalar=convw[:, j, k:k + 1],
                    in1=t1b[:, tb, :], op0=ALU.mult, op1=ALU.add),
                    reads=[("zext", zb), ("t1", tb), ("c_convw",)], writes=[("t1", tb)], name="c12")
            S.add("dve", lambda e, tb=tb, si=si, t=t, j=j: e.tensor_tensor(
                out=hid_ap(t, j), in0=t1b[:, tb, :], in1=silu[:, si, :], op=ALU.mult),
                reads=[("t1", tb), ("silu", si)], writes=[hid_reg(t, j)], name="ya")
            S.add("dve", lambda e, zb=zb, j=j: e.tensor_copy(out=zhalo[:, j, :], in_=zext[:, zb, T:T + HALO]),
                  reads=[("zext", zb)], writes=[("zhalo", j)], name="zhalo_out")

        def pool_mm(t, gs):
            for g in gs:
                bank = next_bank()
                mm_group(bank, T, [(poolw[:, g, :], pooled[:, t.par, g, :])], [("pooled", t.par, g), ("poolw",)], name="poolmm")
                S.add("act", lambda e, t=t, g=g, bank=bank: e.mul(out=hid_ap(t, 4 + g), in_=banks[bank][:, :],
                                                                  mul=pscale[:, g:g + 1]),
                      reads=[("ps", bank), ("c_pscale",)], writes=[hid_reg(t, 4 + g)], name="yb")

        def w_in(tiles):
            def u_work(slot, t):
                for g in range(4):
                    bank = next_bank()
                    mm_group(bank, t.W, [(ring[:, slot, kc * 512 + g * 128: kc * 512 + g * 128 + 128], h_ap(t, kc))
                                         for kc in range(DC)],
                             [h_reg(t, kc) for kc in range(DC)] + [("ring", slot)], name="win_u",
                             fine_reads=([[h_reg(t, kc), ("ring", slot)] for kc in range(DC)] if g == 0 else None))
                    pooling(t, g, bank)

            def j_work(j, slot, t):
                bl = []
                for q in range(2 if t.is_H else 3):
                    bank = next_bank()
                    mm_group(bank, t.W, [(ring[:, slot, kc * 384 + q * 128: kc * 384 + q * 128 + 128], h_ap(t, kc))
                                         for kc in range(DC)],
                             [h_reg(t, kc) for kc in range(DC)] + [("ring", slot)], name="win_j")
                    bl.append(bank)
                conv(t, j, bl[0], bl[1], bl[2] if len(bl) > 2 else None)
                if j in (1, 2) and not t.is_H:
                    pool_mm(t, (0, 1) if j == 1 else (2, 3))

            slot_u = load_slab(("win", "u"))
            slot_0 = load_slab(("win", 0))
            for t in tiles:
                u_work(slot_u, t)
                j_work(0, slot_0, t)
            del slab_slot[("win", "u")]
            del slab_slot[("win", 0)]
            for j in range(1, 4):
                slot = load_slab(("win", j))
                for t in tiles:
                    j_work(j, slot, t)
                del slab_slot[("win", j)]

        def w_out(tiles, after):
            slots = [load_slab(("wo", 0)), load_slab(("wo", 1))]
            for t in tiles:
                for s in range(2):
                    slot = slots[s]
                    for oi in range(4):
                        o = 4 * s + oi
                        bank = next_bank()
                        mm_group(bank, T, [(ring[:, slot, kc * 512 + oi * 128: kc * 512 + oi * 128 + 128], hid_ap(t, kc))
                                           for kc in range(DC)],
                                 [hid_reg(t, kc) for kc in range(DC)] + [("ring", slot)], name="wout")
                        S.add("dve", lambda e, t=t, o=o, bank=bank: e.tensor_tensor(
                            out=x_ap(t, o), in0=banks[bank][:, :], in1=x_ap(t, o), op=ALU.add),
                            reads=[("ps", bank), x_reg(t, o)], writes=[x_reg(t, o)], name="xadd")
                        norm_accum(t, o, thr=(12 if o == DC - 1 else 40))
                after(t)
            del slab_slot[("wo", 0)]
            del slab_slot[("wo", 1)]
            expedite(base=24, step=2)

        S.add("dve", lambda e: e.memset(ones[:, :], 1.0 / D), writes=[("ones",)], name="ones")
        S.add("dve", lambda e: e.memset(epsb[:, :], EPS), writes=[("eps",)], name="eps")
        S.add("act", lambda e: e.activation(out=t16[:, 0:1], in_=epsb[:, 0:1], func=AF.Square), reads=[("eps",)], writes=[("t16",)], name="warm")

        tH = TileD("H", HALO, None, 0, -HALO, is_H=True)
        pairs = [
            [TileD("A0", T, 0, 0, 0, is_first=True), TileD("B0", T, 1, 1, T)],
            [TileD("A1", T, 2, 0, 2 * T), TileD("B1", T, 0, 1, 3 * T)],
        ]

        def load_x(t):
            if t.is_H:
                S.add("sp", lambda e: e.dma_start(out=xH[:, :, :], in_=xT[:, :, 0:HALO]),
                      writes=[("xH", c) for c in range(DC)], dma_key=("xH",), name="ldxH")
            else:
                assert not any(it[3] == ("final", t.xslot) for it in pending)
                for q in range(4):
                    xload_ops[t.name] = S.add("sp", lambda e, t=t, q=q: e.dma_start(
                        out=xs[:, t.xslot, 2 * q:2 * q + 2, :],
                        in_=xT[:, 2 * q:2 * q + 2, HALO + t.tok0:HALO + t.tok0 + T]),
                        writes=[("x", t.xslot, 2 * q), ("x", t.xslot, 2 * q + 1)], dma_key=("x", t.xslot, q), name="ldx",
                        )

        stores = []

        def finish_tile(t):
            norm_finish(t, G_FINAL, final=True)

        S.add("sp", lambda e: e.dma_start(out=gains[:, :, :], in_=gains_d[:, :, :]), writes=[("c_gains",)],
              dma_key=("c", 0), name="ldgains")
        load_x(tH)
        load_x(pairs[0][0])
        S.add("sp", lambda e: e.dma_start(out=convw[:, :, :], in_=convw_d[:, :, :]), writes=[("c_convw",)],
              dma_key=("c", 1), name="ldconvw")
        S.add("sp", lambda e: e.dma_start(out=pscale[:, :], in_=pscale_d[:, :]), writes=[("c_pscale",)],
              dma_key=("c", 2), name="ldpscale")
        S.add("sp", lambda e: e.dma_start(out=invc[:, :, :], in_=invc_d[:, :, :]), writes=[("c_invc",)],
              dma_key=("c", 3), name="ldinvc")
        load_x(pairs[0][1])

        def initial_norm(t, thr_a=10, thr_b=10):
            for c in range(DC):
                norm_accum(t, c, thr=(thr_b if c == DC - 1 else thr_a))
            norm_finish(t, G_FFN1, thr=thr_b, step=2)

        for p, (tA, tB) in enumerate(pairs):
            tiles = ([tH] if p == 0 else []) + [tA, tB]
            if p == 0:
                for t in tiles:
                    initial_norm(t)
                gateup(1, tiles, skew=3)
                S.add("pool", lambda e: e.dma_start(out=poolw[:, :, :], in_=poolw_d[:, :, :]), writes=[("poolw",)],
                      dma_key=("c", 4), name="ldpoolw")
                load_x(pairs[1][0])
            else:
                load_x(tB)
                gateup(1, tiles, skew=5, mid=lambda: initial_norm(tB, 30, 40))
            down(1, tiles, lambda t: norm_finish(t, G_MIX))
            w_in(tiles)
            w_out([tA, tB], lambda t: norm_finish(t, G_FFN2, thr=12, step=4))
            gateup(2, [tA, tB], skew=3)
            if p == 0:
                initial_norm(pairs[1][0], 30, 40)
            down(2, [tA, tB], finish_tile, exp_step=16)
        flush(force=True)
        S.add("sp", lambda e: None,
              reads=[("out", n, q) for n in ("A0", "B0", "A1", "B1") for q in range(4)], name="final")

        S.plan()
        global _LAST_SCHED
        _LAST_SCHED = S

        with nc.Block() as block:
            def run(eng_name, e):
                for op in S.ops[eng_name]:
                    for dep in op.waits:
                        if dep.dma_key is not None:
                            e.wait_ge(dsem(dep.dma_key), 16 * dep.dma_cnt)
                        else:
                            e.wait_ge(sem[dep.eng], dep.sigidx)
                    ins = op.emit(e)
                    if ins is None:
                        continue
                    if op.dma_key is not None:
                        ins.then_inc(dsem(op.dma_key), 16)
                    elif op.signal:
                        ins.then_inc(sem[op.eng], 1)

            @block.tensor
            def _(e):
                run("pe", e)

            @block.scalar
            def _(e):
                run("act", e)

            @block.vector
            def _(e):
                run("dve", e)

            @block.gpsimd
            def _(e):
                run("pool", e)

            @block.sync
            def _(e):
                run("sp", e)
    return nc


_PROGRAM = None
_LAST_SCHED = None


def _prep_inputs(inp):
    x = np.asarray(inp["x"], np.float32)
    wst = _build_wstream(inp)
    gains = np.stack([np.asarray(inp["norm_ffn1"][0]), np.asarray(inp["norm_mix"][0]),
                      np.asarray(inp["norm_ffn2"][0]), np.asarray(inp["norm_final"])], axis=0)
    gains = np.ascontiguousarray(gains.reshape(4, DC, 128).transpose(2, 0, 1)).astype(np.float32)
    convw = np.ascontiguousarray(np.asarray(inp["conv_w"][0]).reshape(3, 4, 128).transpose(2, 1, 0)).astype(np.float32)
    pscale = np.ascontiguousarray(np.asarray(inp["pool_scale"][0]).reshape(4, 128).T).astype(np.float32)
    poolw = np.ascontiguousarray(np.asarray(inp["pool_w"][0]).transpose(1, 0, 2)).astype(np.float32)
    t1 = np.arange(1, HALO + 1, dtype=np.float32)
    invc_first = np.stack([1.0 / np.minimum(t1, float(w)) for w in WINDOWS], axis=0)
    invc_rest = np.stack([np.full(HALO, 1.0 / w, np.float32) for w in WINDOWS], axis=0)
    in_maps = []
    for core in range(NCORE):
        b, h = core // 2, core % 2
        xt = np.zeros((128, DC, HALO + TOK), np.float32)
        lo = h * TOK - HALO
        src = x[b, max(lo, 0):h * TOK + TOK, :]
        src = src.reshape(src.shape[0], DC, 128).transpose(2, 1, 0)
        xt[:, :, HALO + TOK - src.shape[2]:] = src
        ic = invc_first if h == 0 else invc_rest
        in_maps.append({
            "xT": xt, "wst": wst, "gains": gains, "convw": convw, "pscale": pscale,
            "invc": np.ascontiguousarray(np.broadcast_to(ic[None], (128, 4, HALO))).astype(np.float32),
            "poolw": poolw,
        })
    return in_maps


def kernel(**inputs):
    global _PROGRAM
    if _PROGRAM is None:
        _PROGRAM = build_program()
    in_maps = _prep_inputs(inputs)
    res = run_bass_kernel_spmd(_PROGRAM, in_maps, core_ids=list(range(NCORE)))
    out = np.empty((4, 4096, D), np.float32)
    for core in range(NCORE):
        b, h = core // 2, core % 2
        o = res.results[core]["outT"]
        out[b, h * TOK:(h + 1) * TOK, :] = o.transpose(2, 1, 0).reshape(TOK, D)
    return out
```

```python
import numpy as np
import concourse.bass as bass
import concourse.mybir as mybir
from concourse.bass_utils import run_bass_kernel_spmd

F32 = mybir.dt.float32
BF16 = mybir.dt.bfloat16
AF = mybir.ActivationFunctionType
ALU = mybir.AluOpType

D = 1024
DC = 8
DFF = 2816
FC = 22
T = 512
HALO = 16
TOK = 2048
NCORE = 8
EPS = 1e-6
WINDOWS = (2, 4, 8, 16)

NSLOT = 6
SLOT = 4096
NROT = 6
NSQ = 8
NSILU = 3

G_FFN1, G_MIX, G_FFN2, G_FINAL = 0, 1, 2, 3


def _slab_table():
    tab = {}
    off = 0

    def put(name, n):
        nonlocal off
        tab[name] = (off, n)
        off += n

    for k in (1, 2):
        for s in range(11):
            put(("gu", k, s), 8 * 2 * 256)
        for d in range(8):
            put(("dn", k, d), FC * 128)
    put(("win", "u"), 8 * 512)
    for j in range(4):
        put(("win", j), 8 * 384)
    for s in range(2):
        put(("wo", s), 8 * 512)
    return tab, off


SLABS, WCOLS = _slab_table()


def _build_wstream(inp):
    w = np.empty((128, WCOLS), np.float32)

    def kview(m):
        K, N = m.shape
        return m.reshape(K // 128, 128, N).transpose(1, 0, 2)

    for k in (1, 2):
        wg = kview(np.asarray(inp[f"ffn{k}_w_gate"][0]))
        wu = kview(np.asarray(inp[f"ffn{k}_w_up"][0]))
        wd = kview(np.asarray(inp[f"ffn{k}_w_down"][0]))
        for s in range(11):
            off, n = SLABS[("gu", k, s)]
            blk = np.stack([wg[:, :, 256 * s:256 * (s + 1)], wu[:, :, 256 * s:256 * (s + 1)]], axis=2)
            w[:, off:off + n] = blk.reshape(128, n)
        for d in range(8):
            off, n = SLABS[("dn", k, d)]
            w[:, off:off + n] = wd[:, :, 128 * d:128 * (d + 1)].reshape(128, n)
    win = kview(np.asarray(inp["w_in"][0]))
    off, n = SLABS[("win", "u")]
    w[:, off:off + n] = win[:, :, 1536:2048].reshape(128, n)
    for j in range(4):
        off, n = SLABS[("win", j)]
        blk = np.stack([win[:, :, 0 + 128 * j:128 * (j + 1)],
                        win[:, :, 1024 + 128 * j:1024 + 128 * (j + 1)],
                        win[:, :, 512 + 128 * j:512 + 128 * (j + 1)]],
                       axis=2)
        w[:, off:off + n] = blk.reshape(128, n)
    wo = kview(np.asarray(inp["w_out"][0]))
    for s in range(2):
        off, n = SLABS[("wo", s)]
        w[:, off:off + n] = wo[:, :, 512 * s:512 * (s + 1)].reshape(128, n)
    return w


ENGS = ("pe", "act", "dve", "pool", "sp")


class Op:
    __slots__ = ("eng", "emit", "deps", "pos", "signal", "sigidx", "dma_key", "dma_cnt", "name", "waits")

    def __init__(self, eng, emit, name):
        self.eng = eng
        self.emit = emit
        self.deps = {}
        self.signal = False
        self.sigidx = 0
        self.dma_key = None
        self.dma_cnt = 0
        self.name = name
        self.waits = []


class Sched:
    def __init__(self):
        self.ops = {e: [] for e in ENGS}
        self.last_writer = {}
        self.readers = {}
        self.dma_counts = {}

    def add(self, eng, emit, reads=(), writes=(), dma_key=None, name="", extra_deps=()):
        op = Op(eng, emit, name)
        for dep in extra_deps:
            op.deps[dep] = "raw"
        for r in reads:
            w = self.last_writer.get(r)
            if w is not None:
                op.deps[w] = "raw"
        for r in writes:
            w = self.last_writer.get(r)
            if w is not None and w not in op.deps:
                op.deps[w] = "waw"
            for rd in self.readers.get(r, ()):
                if rd not in op.deps:
                    op.deps[rd] = "war"
        op.deps.pop(op, None)
        for r in reads:
            self.readers.setdefault(r, []).append(op)
        for r in writes:
            self.last_writer[r] = op
            self.readers[r] = []
        if dma_key is not None:
            op.dma_key = dma_key
            self.dma_counts[dma_key] = self.dma_counts.get(dma_key, 0) + 1
            op.dma_cnt = self.dma_counts[dma_key]
        op.pos = len(self.ops[eng])
        self.ops[eng].append(op)
        return op

    def plan(self):
        for eng in ENGS:
            maxpos = {}
            maxdma = {}
            for op in self.ops[eng]:
                for dep, kind in sorted(op.deps.items(), key=lambda kv: -kv[0].pos):
                    if dep.dma_key is not None:
                        if maxdma.get(dep.dma_key, 0) >= dep.dma_cnt:
                            continue
                        maxdma[dep.dma_key] = dep.dma_cnt
                        op.waits.append(dep)
                        continue
                    if dep.eng == eng:
                        if eng == "pe" or kind != "raw":
                            continue
                    if maxpos.get(dep.eng, -1) >= dep.pos:
                        continue
                    maxpos[dep.eng] = dep.pos
                    dep.signal = True
                    op.waits.append(dep)
        for eng in ENGS:
            n = 0
            for op in self.ops[eng]:
                if op.signal:
                    n += 1
                    op.sigidx = n


class TileD:
    def __init__(self, name, W, xslot, par, tok0, is_H=False, is_first=False):
        self.name, self.W, self.xslot, self.par, self.tok0 = name, W, xslot, par, tok0
        self.is_H, self.is_first = is_H, is_first


def build_program():
    nc = bass.Bass("TRN2", target_bir_lowering=False)
    xT = nc.dram_tensor("xT", [128, DC, HALO + TOK], F32, kind="ExternalInput").ap()
    wst = nc.dram_tensor("wst", [128, WCOLS], F32, kind="ExternalInput").ap()
    gains_d = nc.dram_tensor("gains", [128, 4, DC], F32, kind="ExternalInput").ap()
    convw_d = nc.dram_tensor("convw", [128, 4, 3], F32, kind="ExternalInput").ap()
    pscale_d = nc.dram_tensor("pscale", [128, 4], F32, kind="ExternalInput").ap()
    invc_d = nc.dram_tensor("invc", [128, 4, HALO], F32, kind="ExternalInput").ap()
    poolw_d = nc.dram_tensor("poolw", [128, 4, 128], F32, kind="ExternalInput").ap()
    outT = nc.dram_tensor("outT", [128, DC, TOK], F32, kind="ExternalOutput").ap()

    S = Sched()
    from contextlib import ExitStack
    with ExitStack() as es:
        def sb(name, shape, dt):
            return es.enter_context(nc.sbuf_tensor(name, shape, dt))

        xs = sb("xs", [128, 3, DC, T], F32)
        xH = sb("xH", [128, DC, HALO], F32)
        hb = sb("hb", [128, 2, DC, T], BF16)
        hH = sb("hH", [128, DC, HALO], BF16)
        hid = sb("hid", [128, 2, FC, T], BF16)
        hidH = sb("hidH", [128, FC, HALO], BF16)
        ring = sb("ring", [128, NSLOT, SLOT], BF16)
        sq = sb("sq", [128, NSQ, T], BF16)
        sqH = sb("sqH", [128, DC, HALO], BF16)
        rstd = sb("rstd", [128, 2, T], F32)
        rstdH = sb("rstdH", [128, HALO], F32)
        silu = sb("silu", [128, NSILU, T], F32)
        gains = sb("gains_sb", [128, 4, DC], F32)
        convw = sb("convw_sb", [128, 4, 3], F32)
        pscale = sb("pscale_sb", [128, 4], F32)
        invc = sb("invc_sb", [128, 4, HALO], F32)
        poolw = sb("poolw_sb", [128, 4, 128], BF16)
        ones = sb("ones_sb", [128, 128], BF16)
        epsb = sb("eps_sb", [128, 1], F32)
        uext = sb("uext", [128, 2, HALO + T], F32)
        zext = sb("zext", [128, 2, HALO + T], F32)
        vsb = sb("vsb", [128, 2, T], F32)
        t1b = sb("t1b", [128, 2, T], F32)
        lvl = sb("lvl", [128, 2, HALO + T], F32)
        pooled = sb("pooled", [128, 2, 4, T], BF16)
        uhalo = sb("uhalo", [128, 4, HALO], F32)
        zhalo = sb("zhalo", [128, 4, HALO], F32)
        t16 = sb("t16", [128, HALO], F32)
        banks = [es.enter_context(nc.psum_tensor(f"ps{i}", [128, T], F32)) for i in range(8)]

        sem = {e: es.enter_context(nc.semaphore(f"sem_{e}")) for e in ("pe", "act", "dve")}
        dma_sems = {}

        def dsem(key):
            if key not in dma_sems:
                dma_sems[key] = es.enter_context(nc.semaphore("dma_" + "_".join(str(k) for k in key)))
            return dma_sems[key]

        ctr = {"bank": 0, "slot": 0, "sq": 0, "silu": 0, "uext": 0, "zext": 0, "vsb": 0, "t1": 0}

        def nxt(kind, n):
            v = ctr[kind]
            ctr[kind] = (v + 1) % n
            return v

        def next_bank():
            return nxt("bank", NROT)

        def x_ap(t, c):
            return xH[:, c, :] if t.is_H else xs[:, t.xslot, c, :]

        def x_reg(t, c):
            return ("xH", c) if t.is_H else ("x", t.xslot, c)

        def h_ap(t, c):
            return hH[:, c, :] if t.is_H else hb[:, t.par, c, :]

        def h_reg(t, c):
            return ("hH", c) if t.is_H else ("h", t.par, c)

        def hid_ap(t, f):
            return hidH[:, f, :] if t.is_H else hid[:, t.par, f, :]

        def hid_reg(t, f):
            return ("hidH", f) if t.is_H else ("hid", t.par, f)

        def rstd_ap(t):
            return rstdH[:, :] if t.is_H else rstd[:, t.par, :]

        def rstd_reg(t):
            return ("rstdH",) if t.is_H else ("rstd", t.par)

        slab_slot = {}
        nload = [0]
        xload_ops = {}

        def load_slab(name):
            off, n = SLABS[name]
            slot = nxt("slot", NSLOT)
            assert slot not in slab_slot.values(), (name, slot, slab_slot)
            slab_slot[name] = slot
            nload[0] += 1
            S.add("pool",
                  lambda e, slot=slot, off=off, n=n: e.dma_start(out=ring[:, slot, 0:n], in_=wst[:, off:off + n]),
                  writes=[("ring", slot)], dma_key=("ring", slot), name=f"ld{name}",
                  extra_deps=([xload_ops["A0"]] if nload[0] <= 1 else [xload_ops["B0"]] if nload[0] <= NSLOT else []))
            return slot

        pending = []
        sqstate = {0: [], 1: []}

        def defer(fn, thr, writes=(), tag=None):
            pending.append([thr, fn, set(writes), tag])

        def flush(force=False, passed=0, reads=()):
            rs = set(reads)
            for it in pending:
                it[0] -= passed
            if force or any(it[2] & rs for it in pending):
                items = list(pending)
                del pending[:]
                for it in items:
                    it[1]()
                return
            ready = [it for it in pending if it[0] <= 0]
            if ready:
                pending[:] = [it for it in pending if it[0] > 0]
                for it in ready:
                    it[1]()

        def expedite(base=8, step=2):
            for idx, it in enumerate(pending):
                it[0] = min(it[0], base + step * idx)

        def mm_group(bank, W, pairs, reads, name="", fine_reads=None):
            flush(reads=reads)
            n = len(pairs)
            if fine_reads is None:
                def emit(pe, bank=bank, W=W, pairs=pairs):
                    ins = None
                    for i, (l, r) in enumerate(pairs):
                        ins = pe.matmul(banks[bank][:, 0:W], lhsT=l, rhs=r, start=(i == 0), stop=(i == n - 1))
                    return ins
                S.add("pe", emit, reads=reads, writes=[("ps", bank)], name=name)
            else:
                for i, (l, r) in enumerate(pairs):
                    S.add("pe", lambda pe, i=i, l=l, r=r, bank=bank, W=W: pe.matmul(
                        banks[bank][:, 0:W], lhsT=l, rhs=r, start=(i == 0), stop=(i == n - 1)),
                        reads=fine_reads[i], writes=[("ps", bank)], name=name + "1")
            flush(passed=len(pairs))

        def norm_accum(t, c, thr=10):
            if t.is_H:
                S.add("act", lambda e, c=c: e.activation(out=sqH[:, c, :], in_=xH[:, c, :], func=AF.Square),
                      reads=[x_reg(t, c)], writes=[("sqH", c)], name="sqH")
                return
            i = nxt("sq", NSQ)
            if any(it[3] == ("sq", i) for it in pending):
                flush(force=True)
            S.add("act", lambda e, i=i, t=t, c=c: e.activation(out=sq[:, i, :], in_=x_ap(t, c), func=AF.Square),
                  reads=[x_reg(t, c)], writes=[("sq", i)], name="sq")
            st = sqstate[t.par]
            st.append(i)

            def sq_add(a, b):
                S.add("dve", lambda e, a=a, b=b: e.tensor_tensor(out=sq[:, a, :], in0=sq[:, a, :], in1=sq[:, b, :], op=ALU.add),
                      reads=[("sq", a), ("sq", b)], writes=[("sq", a)], name="sqadd")
            if c % 4 >= 1:
                sq_add(st[0], st[-1])
            if c % 4 == 3:
                bank = 6 + t.par
                i0_ = st[0]
                defer(lambda bank=bank, i=i0_, c=c: S.add(
                    "pe", lambda pe: pe.matmul(banks[bank][:, :], lhsT=ones[:, :], rhs=sq[:, i, :],
                                               start=(c == 3), stop=(c == DC - 1)),
                    reads=[("sq", i), ("ones",)], writes=[("ps", bank)], name="ss"), thr, tag=("sq", i0_))
                del st[:]

        def norm_rstd(t):
            W = t.W
            if t.is_H:
                bank = next_bank()
                mm_group(bank, W, [(ones[:, :], sqH[:, c, :]) for c in range(DC)],
                         reads=[("sqH", c) for c in range(DC)] + [("ones",)], name="ssH")
            else:
                bank = 6 + t.par
            S.add("act", lambda e, t=t, bank=bank, W=W: e.activation(out=rstd_ap(t), in_=banks[bank][:, 0:W], func=AF.Sqrt,
                                                                     bias=epsb[:, 0:1], scale=1.0),
                  reads=[("ps", bank), ("eps",)], writes=[rstd_reg(t)], name="sqrt")
            S.add("dve", lambda e, t=t: e.reciprocal(out=rstd_ap(t), in_=rstd_ap(t)),
                  reads=[rstd_reg(t)], writes=[rstd_reg(t)], name="recip")

        def norm_apply(t, gi, final, chunks):
            for c in chunks:
                if final:
                    S.add("dve", lambda e, t=t, c=c, gi=gi: e.scalar_tensor_tensor(
                        out=x_ap(t, c), in0=x_ap(t, c), scalar=gains[:, gi, c:c + 1], in1=rstd_ap(t),
                        op0=ALU.mult, op1=ALU.mult),
                        reads=[x_reg(t, c), rstd_reg(t), ("c_gains",)], writes=[x_reg(t, c)], name="fin")
                else:
                    S.add("dve", lambda e, t=t, c=c, gi=gi: e.scalar_tensor_tensor(
                        out=h_ap(t, c), in0=x_ap(t, c), scalar=gains[:, gi, c:c + 1], in1=rstd_ap(t),
                        op0=ALU.mult, op1=ALU.mult),
                        reads=[x_reg(t, c), rstd_reg(t), ("c_gains",)], writes=[h_reg(t, c)], name="napply")

        def store_chunks(t, q):
            S.add("sp", lambda e, t=t, q=q: e.dma_start(out=outT[:, 2 * q:2 * q + 2, t.tok0:t.tok0 + T],
                                                       in_=xs[:, t.xslot, 2 * q:2 * q + 2, :]),
                  reads=[("x", t.xslot, 2 * q), ("x", t.xslot, 2 * q + 1)], writes=[("out", t.name, q)],
                  dma_key=("st", t.xslot, q), name="store")

        def norm_finish(t, gi, final=False, thr=10, step=2):
            if t.is_H:
                norm_rstd(t)
                norm_apply(t, gi, final, range(DC))
                return
            defer(lambda: norm_rstd(t), thr)
            for q in range(4):
                def fn(q=q):
                    norm_apply(t, gi, final, (2 * q, 2 * q + 1))
                    if final:
                        store_chunks(t, q)
                wr = [] if final else [h_reg(t, 2 * q), h_reg(t, 2 * q + 1)]
                defer(fn, thr + step * (q + 1), writes=wr, tag=("final", t.xslot) if final else None)

        def gateup(k, tiles, skew=2, mid=None, mid_at=2):
            def work(s, slot, t):
                W = t.W
                for fi in range(2):
                    f = 2 * s + fi
                    hreads = [h_reg(t, kc) for kc in range(DC)] + [("ring", slot)]
                    fr = [[h_reg(t, kc), ("ring", slot)] for kc in range(DC)] if (s == 0 and fi == 0) else None
                    bg = next_bank()
                    mm_group(bg, W, [(ring[:, slot, kc * 512 + fi * 128: kc * 512 + fi * 128 + 128], h_ap(t, kc))
                                     for kc in range(DC)], hreads, name="gate", fine_reads=fr)
                    bu = next_bank()
                    mm_group(bu, W, [(ring[:, slot, kc * 512 + 256 + fi * 128: kc * 512 + 256 + fi * 128 + 128], h_ap(t, kc))
                                     for kc in range(DC)], hreads, name="up")
                    si = nxt("silu", NSILU)
                    S.add("act", lambda e, si=si, bg=bg, W=W: e.activation(out=silu[:, si, 0:W], in_=banks[bg][:, 0:W],
                                                                         func=AF.Silu),
                          reads=[("ps", bg)], writes=[("silu", si)], name="silu")
                    S.add("dve", lambda e, si=si, bu=bu, W=W, t=t, f=f: e.tensor_tensor(
                        out=hid_ap(t, f), in0=silu[:, si, 0:W], in1=banks[bu][:, 0:W], op=ALU.mult),
                        reads=[("silu", si), ("ps", bu)], writes=[hid_reg(t, f)], name="hmul")

            slots = {}
            for s in range(skew):
                nm = ("gu", k, s)
                slots[s] = slab_slot[nm] if nm in slab_slot else load_slab(nm)
            lead, last = tiles[:-1], tiles[-1]
            for s in range(skew):
                if mid is not None and s == mid_at:
                    mid()
                for t in lead:
                    work(s, slots[s], t)
            for s in range(skew):
                work(s, slots[s], last)
            for s in range(skew):
                del slab_slot[("gu", k, s)]
            for s in range(skew, 11):
                slot = load_slab(("gu", k, s))
                for t in tiles:
                    work(s, slot, t)
                del slab_slot[("gu", k, s)]

        def down(k, tiles, after, exp_step=2):
            for half in range(2):
                for t in tiles:
                    W = t.W
                    for d in range(4 * half, 4 * half + 4):
                        name = ("dn", k, d)
                        if name not in slab_slot:
                            load_slab(name)
                        slot = slab_slot[name]
                        bank = next_bank()
                        mm_group(bank, W, [(ring[:, slot, fc * 128: fc * 128 + 128], hid_ap(t, fc)) for fc in range(FC)],
                                 [hid_reg(t, fc) for fc in range(FC)] + [("ring", slot)], name="down")
                        S.add("dve", lambda e, t=t, d=d, bank=bank, W=W: e.scalar_tensor_tensor(
                            out=x_ap(t, d), in0=banks[bank][:, 0:W], scalar=0.5, in1=x_ap(t, d),
                            op0=ALU.mult, op1=ALU.add),
                            reads=[("ps", bank), x_reg(t, d)], writes=[x_reg(t, d)], name="xupd")
                        norm_accum(t, d, thr=(10 if d == DC - 1 else 30))
                    if half == 1:
                        after(t)
                for d in range(4 * half, 4 * half + 4):
                    del slab_slot[("dn", k, d)]
            expedite(step=exp_step)

        def pooling(t, g, bank):
            W = t.W
            if t.is_H:
                S.add("act", lambda e, g=g, bank=bank: e.copy(out=uhalo[:, g, :], in_=banks[bank][:, 0:HALO]),
                      reads=[("ps", bank)], writes=[("uhalo", g)], name="uH")
                return
            ub = nxt("uext", 2)
            S.add("act", lambda e, ub=ub, bank=bank: e.copy(out=uext[:, ub, HALO:HALO + T], in_=banks[bank][:, :]),
                  reads=[("ps", bank)], writes=[("uext", ub)], name="ucopy")
            S.add("act", lambda e, ub=ub, g=g: e.copy(out=uext[:, ub, 0:HALO], in_=uhalo[:, g, :]),
                  reads=[("uhalo", g)], writes=[("uext", ub)], name="uhalo_in")
            m = g + 1
            start = {m: HALO}
            for i in range(m, 1, -1):
                start[i - 1] = start[i] - (1 << (i - 1))
            prev_ap, prev_reg = (lambda lo, hi, ub=ub: uext[:, ub, lo:hi]), ("uext", ub)
            for i in range(1, m + 1):
                li = i % 2
                sh = 1 << (i - 1)
                lo = start[i]
                S.add("dve", lambda e, prev_ap=prev_ap, li=li, lo=lo, sh=sh: e.tensor_tensor(
                    out=lvl[:, li, lo:HALO + T], in0=prev_ap(lo, HALO + T), in1=prev_ap(lo - sh, HALO + T - sh), op=ALU.add),
                    reads=[prev_reg], writes=[("lvl", li)], name="padd")
                prev_ap, prev_reg = (lambda lo, hi, li=li: lvl[:, li, lo:hi]), ("lvl", li)
            w = float(1 << m)
            S.add("dve", lambda e, prev_ap=prev_ap, ub=ub, g=g, w=w, t=t: e.scalar_tensor_tensor(
                out=pooled[:, t.par, g, :], in0=prev_ap(HALO, HALO + T), scalar=1.0 / w, in1=uext[:, ub, HALO:HALO + T],
                op0=ALU.mult, op1=ALU.subtract),
                reads=[prev_reg, ("uext", ub)], writes=[("pooled", t.par, g)], name="pooled")
            if t.is_first:
                S.add("dve", lambda e, prev_ap=prev_ap, g=g: e.tensor_tensor(
                    out=t16[:, :], in0=prev_ap(HALO, 2 * HALO), in1=invc[:, g, :], op=ALU.mult),
                    reads=[prev_reg, ("c_invc",)], writes=[("t16",)], name="fix1")
                S.add("dve", lambda e, ub=ub, g=g, t=t: e.tensor_tensor(
                    out=pooled[:, t.par, g, 0:HALO], in0=t16[:, :], in1=uext[:, ub, HALO:2 * HALO], op=ALU.subtract),
                    reads=[("t16",), ("uext", ub)], writes=[("pooled", t.par, g)], name="fix2")
            S.add("act", lambda e, ub=ub, g=g: e.copy(out=uhalo[:, g, :], in_=uext[:, ub, T:T + HALO]),
                  reads=[("uext", ub)], writes=[("uhalo", g)], name="uhalo_out")

        def conv(t, j, bv, bc, bb):
            W = t.W
            vb = nxt("vsb", 2)
            S.add("act", lambda e, vb=vb, bv=bv, W=W: e.copy(out=vsb[:, vb, 0:W], in_=banks[bv][:, 0:W]),
                  reads=[("ps", bv)], writes=[("vsb", vb)], name="vcopy")
            if t.is_H:
                S.add("dve", lambda e, vb=vb, bc=bc, j=j: e.tensor_tensor(
                    out=zhalo[:, j, :], in0=vsb[:, vb, 0:HALO], in1=banks[bc][:, 0:HALO], op=ALU.mult),
                    reads=[("vsb", vb), ("ps", bc)], writes=[("zhalo", j)], name="zH")
                return
            zb = nxt("zext", 2)
            tb = nxt("t1", 2)
            S.add("act", lambda e, zb=zb, j=j: e.copy(out=zext[:, zb, 0:HALO], in_=zhalo[:, j, :]),
                  reads=[("zhalo", j)], writes=[("zext", zb)], name="zhalo_in")
            S.add("dve", lambda e, zb=zb, vb=vb, bc=bc: e.tensor_tensor(
                out=zext[:, zb, HALO:HALO + T], in0=vsb[:, vb, :], in1=banks[bc][:, :], op=ALU.mult),
                reads=[("vsb", vb), ("ps", bc)], writes=[("zext", zb)], name="z")
            si = nxt("silu", NSILU)
            S.add("act", lambda e, si=si, bb=bb: e.copy(out=silu[:, si, :], in_=banks[bb][:, :]),
                  reads=[("ps", bb)], writes=[("silu", si)], name="bcopy")
            S.add("dve", lambda e, zb=zb, tb=tb, j=j: e.tensor_scalar(
                out=t1b[:, tb, :], in0=zext[:, zb, HALO - 2:HALO - 2 + T], scalar1=convw[:, j, 0:1], scalar2=None,
                op0=ALU.mult),
                reads=[("zext", zb), ("c_convw",)], writes=[("t1", tb)], name="c0")
            for k in (1, 2):
                S.add("dve", lambda e, zb=zb, tb=tb, j=j, k=k: e.scalar_tensor_tensor(
                    out=t1b[:, tb, :], in0=zext[:, zb, HALO - 2 + k:HALO - 2 + k + T], scalar=convw[:, j, k:k + 1],
                    in1=t1b[:, tb, :], op0=ALU.mult, op1=ALU.add),
                    reads=[("zext", zb), ("t1", tb), ("c_convw",)], writes=[("t1", tb)], name="c12")
            S.add("dve", lambda e, tb=tb, si=si, t=t, j=j: e.tensor_tensor(
                out=hid_ap(t, j), in0=t1b[:, tb, :], in1=silu[:, si, :], op=ALU.mult),
                reads=[("t1", tb), ("silu", si)], writes=[hid_reg(t, j)], name="ya")
            S.add("dve", lambda e, zb=zb, j=j: e.tensor_copy(out=zhalo[:, j, :], in_=zext[:, zb, T:T + HALO]),
                  reads=[("zext", zb)], writes=[("zhalo", j)], name="zhalo_out")

        def pool_mm(t, gs):
            for g in gs:
                bank = next_bank()
                mm_group(bank, T, [(poolw[:, g, :], pooled[:, t.par, g, :])], [("pooled", t.par, g), ("poolw",)], name="poolmm")
                S.add("act", lambda e, t=t, g=g, bank=bank: e.mul(out=hid_ap(t, 4 + g), in_=banks[bank][:, :],
                                                                  mul=pscale[:, g:g + 1]),
                      reads=[("ps", bank), ("c_pscale",)], writes=[hid_reg(t, 4 + g)], name="yb")

        def w_in(tiles):
            def u_work(slot, t):
                for g in range(4):
                    bank = next_bank()
                    mm_group(bank, t.W, [(ring[:, slot, kc * 512 + g * 128: kc * 512 + g * 128 + 128], h_ap(t, kc))
                                         for kc in range(DC)],
                             [h_reg(t, kc) for kc in range(DC)] + [("ring", slot)], name="win_u",
                             fine_reads=([[h_reg(t, kc), ("ring", slot)] for kc in range(DC)] if g == 0 else None))
                    pooling(t, g, bank)

            def j_work(j, slot, t):
                bl = []
                for q in range(2 if t.is_H else 3):
                    bank = next_bank()
                    mm_group(bank, t.W, [(ring[:, slot, kc * 384 + q * 128: kc * 384 + q * 128 + 128], h_ap(t, kc))
                                         for kc in range(DC)],
                             [h_reg(t, kc) for kc in range(DC)] + [("ring", slot)], name="win_j")
                    bl.append(bank)
                conv(t, j, bl[0], bl[1], bl[2] if len(bl) > 2 else None)
                if j in (1, 2) and not t.is_H:
                    pool_mm(t, (0, 1) if j == 1 else (2, 3))

            slot_u = load_slab(("win", "u"))
            slot_0 = load_slab(("win", 0))
            for t in tiles:
                u_work(slot_u, t)
                j_work(0, slot_0, t)
            del slab_slot[("win", "u")]
            del slab_slot[("win", 0)]
            for j in range(1, 4):
                slot = load_slab(("win", j))
                for t in tiles:
                    j_work(j, slot, t)
                del slab_slot[("win", j)]

        def w_out(tiles, after):
            slots = [load_slab(("wo", 0)), load_slab(("wo", 1))]
            for t in tiles:
                for s in range(2):
                    slot = slots[s]
                    for oi in range(4):
                        o = 4 * s + oi
                        bank = next_bank()
                        mm_group(bank, T, [(ring[:, slot, kc * 512 + oi * 128: kc * 512 + oi * 128 + 128], hid_ap(t, kc))
                                           for kc in range(DC)],
                                 [hid_reg(t, kc) for kc in range(DC)] + [("ring", slot)], name="wout")
                        S.add("dve", lambda e, t=t, o=o, bank=bank: e.tensor_tensor(
                            out=x_ap(t, o), in0=banks[bank][:, :], in1=x_ap(t, o), op=ALU.add),
                            reads=[("ps", bank), x_reg(t, o)], writes=[x_reg(t, o)], name="xadd")
                        norm_accum(t, o, thr=(12 if o == DC - 1 else 40))
                after(t)
            del slab_slot[("wo", 0)]
            del slab_slot[("wo", 1)]
            expedite(base=24, step=2)

        S.add("dve", lambda e: e.memset(ones[:, :], 1.0 / D), writes=[("ones",)], name="ones")
        S.add("dve", lambda e: e.memset(epsb[:, :], EPS), writes=[("eps",)], name="eps")
        S.add("act", lambda e: e.activation(out=t16[:, 0:1], in_=epsb[:, 0:1], func=AF.Square), reads=[("eps",)], writes=[("t16",)], name="warm")

        tH = TileD("H", HALO, None, 0, -HALO, is_H=True)
        pairs = [
            [TileD("A0", T, 0, 0, 0, is_first=True), TileD("B0", T, 1, 1, T)],
            [TileD("A1", T, 2, 0, 2 * T), TileD("B1", T, 0, 1, 3 * T)],
        ]

        def load_x(t):
            if t.is_H:
                S.add("sp", lambda e: e.dma_start(out=xH[:, :, :], in_=xT[:, :, 0:HALO]),
                      writes=[("xH", c) for c in range(DC)], dma_key=("xH",), name="ldxH")
            else:
                assert not any(it[3] == ("final", t.xslot) for it in pending)
                for q in range(4):
                    xload_ops[t.name] = S.add("sp", lambda e, t=t, q=q: e.dma_start(
                        out=xs[:, t.xslot, 2 * q:2 * q + 2, :],
                        in_=xT[:, 2 * q:2 * q + 2, HALO + t.tok0:HALO + t.tok0 + T]),
                        writes=[("x", t.xslot, 2 * q), ("x", t.xslot, 2 * q + 1)], dma_key=("x", t.xslot, q), name="ldx",
                        )

        stores = []

        def finish_tile(t):
            norm_finish(t, G_FINAL, final=True)

        S.add("sp", lambda e: e.dma_start(out=gains[:, :, :], in_=gains_d[:, :, :]), writes=[("c_gains",)],
              dma_key=("c", 0), name="ldgains")
        load_x(tH)
        load_x(pairs[0][0])
        S.add("sp", lambda e: e.dma_start(out=convw[:, :, :], in_=convw_d[:, :, :]), writes=[("c_convw",)],
              dma_key=("c", 1), name="ldconvw")
        S.add("sp", lambda e: e.dma_start(out=pscale[:, :], in_=pscale_d[:, :]), writes=[("c_pscale",)],
              dma_key=("c", 2), name="ldpscale")
        S.add("sp", lambda e: e.dma_start(out=invc[:, :, :], in_=invc_d[:, :, :]), writes=[("c_invc",)],
              dma_key=("c", 3), name="ldinvc")
        load_x(pairs[0][1])

        def initial_norm(t, thr_a=10, thr_b=10):
            for c in range(DC):
                norm_accum(t, c, thr=(thr_b if c == DC - 1 else thr_a))
            norm_finish(t, G_FFN1, thr=thr_b, step=2)

        for p, (tA, tB) in enumerate(pairs):
            tiles = ([tH] if p == 0 else []) + [tA, tB]
            if p == 0:
                for t in tiles:
                    initial_norm(t)
                gateup(1, tiles, skew=3)
                S.add("pool", lambda e: e.dma_start(out=poolw[:, :, :], in_=poolw_d[:, :, :]), writes=[("poolw",)],
                      dma_key=("c", 4), name="ldpoolw")
                load_x(pairs[1][0])
            else:
                load_x(tB)
                gateup(1, tiles, skew=5, mid=lambda: initial_norm(tB, 30, 40))
            down(1, tiles, lambda t: norm_finish(t, G_MIX))
            w_in(tiles)
            w_out([tA, tB], lambda t: norm_finish(t, G_FFN2, thr=12, step=4))
            gateup(2, [tA, tB], skew=3)
            if p == 0:
                initial_norm(pairs[1][0], 30, 40)
            down(2, [tA, tB], finish_tile, exp_step=16)
        flush(force=True)
        S.add("sp", lambda e: None,
              reads=[("out", n, q) for n in ("A0", "B0", "A1", "B1") for q in range(4)], name="final")

        S.plan()
        global _LAST_SCHED
        _LAST_SCHED = S

        with nc.Block() as block:
            def run(eng_name, e):
                for op in S.ops[eng_name]:
                    for dep in op.waits:
                        if dep.dma_key is not None:
                            e.wait_ge(dsem(dep.dma_key), 16 * dep.dma_cnt)
                        else:
                            e.wait_ge(sem[dep.eng], dep.sigidx)
                    ins = op.emit(e)
                    if ins is None:
                        continue
                    if op.dma_key is not None:
                        ins.then_inc(dsem(op.dma_key), 16)
                    elif op.signal:
                        ins.then_inc(sem[op.eng], 1)

            @block.tensor
            def _(e):
                run("pe", e)

            @block.scalar
            def _(e):
                run("act", e)

            @block.vector
            def _(e):
                run("dve", e)

            @block.gpsimd
            def _(e):
                run("pool", e)

            @block.sync
            def _(e):
                run("sp", e)
    return nc


_PROGRAM = None
_LAST_SCHED = None


def _prep_inputs(inp):
    x = np.asarray(inp["x"], np.float32)
    wst = _build_wstream(inp)
    gains = np.stack([np.asarray(inp["norm_ffn1"][0]), np.asarray(inp["norm_mix"][0]),
                      np.asarray(inp["norm_ffn2"][0]), np.asarray(inp["norm_final"])], axis=0)
    gains = np.ascontiguousarray(gains.reshape(4, DC, 128).transpose(2, 0, 1)).astype(np.float32)
    convw = np.ascontiguousarray(np.asarray(inp["conv_w"][0]).reshape(3, 4, 128).transpose(2, 1, 0)).astype(np.float32)
    pscale = np.ascontiguousarray(np.asarray(inp["pool_scale"][0]).reshape(4, 128).T).astype(np.float32)
    poolw = np.ascontiguousarray(np.asarray(inp["pool_w"][0]).transpose(1, 0, 2)).astype(np.float32)
    t1 = np.arange(1, HALO + 1, dtype=np.float32)
    invc_first = np.stack([1.0 / np.minimum(t1, float(w)) for w in WINDOWS], axis=0)
    invc_rest = np.stack([np.full(HALO, 1.0 / w, np.float32) for w in WINDOWS], axis=0)
    in_maps = []
    for core in range(NCORE):
        b, h = core // 2, core % 2
        xt = np.zeros((128, DC, HALO + TOK), np.float32)
        lo = h * TOK - HALO
        src = x[b, max(lo, 0):h * TOK + TOK, :]
        src = src.reshape(src.shape[0], DC, 128).transpose(2, 1, 0)
        xt[:, :, HALO + TOK - src.shape[2]:] = src
        ic = invc_first if h == 0 else invc_rest
        in_maps.append({
            "xT": xt, "wst": wst, "gains": gains, "convw": convw, "pscale": pscale,
            "invc": np.ascontiguousarray(np.broadcast_to(ic[None], (128, 4, HALO))).astype(np.float32),
            "poolw": poolw,
        })
    return in_maps


def kernel(**inputs):
    global _PROGRAM
    if _PROGRAM is None:
        _PROGRAM = build_program()
    in_maps = _prep_inputs(inputs)
    res = run_bass_kernel_spmd(_PROGRAM, in_maps, core_ids=list(range(NCORE)))
    out = np.empty((4, 4096, D), np.float32)
    for core in range(NCORE):
        b, h = core // 2, core % 2
        o = res.results[core]["outT"]
        out[b, h * TOK:(h + 1) * TOK, :] = o.transpose(2, 1, 0).reshape(TOK, D)
    return out
```

```python
import numpy as np
import concourse.bass as bass
import concourse.mybir as mybir
from concourse.bass_utils import run_bass_kernel_spmd

F32 = mybir.dt.float32
BF16 = mybir.dt.bfloat16
AF = mybir.ActivationFunctionType
ALU = mybir.AluOpType

D = 1024
DC = 8
DFF = 2816
FC = 22
T = 512
HALO = 16
TOK = 2048
NCORE = 8
EPS = 1e-6
WINDOWS = (2, 4, 8, 16)

NSLOT = 6
SLOT = 4096
NROT = 6
NSQ = 8
NSILU = 3

G_FFN1, G_MIX, G_FFN2, G_FINAL = 0, 1, 2, 3


def _slab_table():
    tab = {}
    off = 0

    def put(name, n):
        nonlocal off
        tab[name] = (off, n)
        off += n

    for k in (1, 2):
        for s in range(11):
            put(("gu", k, s), 8 * 2 * 256)
        for d in range(8):
            put(("dn", k, d), FC * 128)
    put(("win", "u"), 8 * 512)
    for j in range(4):
        put(("win", j), 8 * 384)
    for s in range(2):
        put(("wo", s), 8 * 512)
    return tab, off


SLABS, WCOLS = _slab_table()


def _build_wstream(inp):
    w = np.empty((128, WCOLS), np.float32)

    def kview(m):
        K, N = m.shape
        return m.reshape(K // 128, 128, N).transpose(1, 0, 2)

    for k in (1, 2):
        wg = kview(np.asarray(inp[f"ffn{k}_w_gate"][0]))
        wu = kview(np.asarray(inp[f"ffn{k}_w_up"][0]))
        wd = kview(np.asarray(inp[f"ffn{k}_w_down"][0]))
        for s in range(11):
            off, n = SLABS[("gu", k, s)]
            blk = np.stack([wg[:, :, 256 * s:256 * (s + 1)], wu[:, :, 256 * s:256 * (s + 1)]], axis=2)
            w[:, off:off + n] = blk.reshape(128, n)
        for d in range(8):
            off, n = SLABS[("dn", k, d)]
            w[:, off:off + n] = wd[:, :, 128 * d:128 * (d + 1)].reshape(128, n)
    win = kview(np.asarray(inp["w_in"][0]))
    off, n = SLABS[("win", "u")]
    w[:, off:off + n] = win[:, :, 1536:2048].reshape(128, n)
    for j in range(4):
        off, n = SLABS[("win", j)]
        blk = np.stack([win[:, :, 0 + 128 * j:128 * (j + 1)],
                        win[:, :, 1024 + 128 * j:1024 + 128 * (j + 1)],
                        win[:, :, 512 + 128 * j:512 + 128 * (j + 1)]],
                       axis=2)
        w[:, off:off + n] = blk.reshape(128, n)
    wo = kview(np.asarray(inp["w_out"][0]))
    for s in range(2):
        off, n = SLABS[("wo", s)]
        w[:, off:off + n] = wo[:, :, 512 * s:512 * (s + 1)].reshape(128, n)
    return w


ENGS = ("pe", "act", "dve", "pool", "sp")


class Op:
    __slots__ = ("eng", "emit", "deps", "pos", "signal", "sigidx", "dma_key", "dma_cnt", "name", "waits")

    def __init__(self, eng, emit, name):
        self.eng = eng
        self.emit = emit
        self.deps = {}
        self.signal = False
        self.sigidx = 0
        self.dma_key = None
        self.dma_cnt = 0
        self.name = name
        self.waits = []


class Sched:
    def __init__(self):
        self.ops = {e: [] for e in ENGS}
        self.last_writer = {}
        self.readers = {}
        self.dma_counts = {}

    def add(self, eng, emit, reads=(), writes=(), dma_key=None, name="", extra_deps=()):
        op = Op(eng, emit, name)
        for dep in extra_deps:
            op.deps[dep] = "raw"
        for r in reads:
            w = self.last_writer.get(r)
            if w is not None:
                op.deps[w] = "raw"
        for r in writes:
            w = self.last_writer.get(r)
            if w is not None and w not in op.deps:
                op.deps[w] = "waw"
            for rd in self.readers.get(r, ()):
                if rd not in op.deps:
                    op.deps[rd] = "war"
        op.deps.pop(op, None)
        for r in reads:
            self.readers.setdefault(r, []).append(op)
        for r in writes:
            self.last_writer[r] = op
            self.readers[r] = []
        if dma_key is not None:
            op.dma_key = dma_key
            self.dma_counts[dma_key] = self.dma_counts.get(dma_key, 0) + 1
            op.dma_cnt = self.dma_counts[dma_key]
        op.pos = len(self.ops[eng])
        self.ops[eng].append(op)
        return op

    def plan(self):
        for eng in ENGS:
            maxpos = {}
            maxdma = {}
            for op in self.ops[eng]:
                for dep, kind in sorted(op.deps.items(), key=lambda kv: -kv[0].pos):
                    if dep.dma_key is not None:
                        if maxdma.get(dep.dma_key, 0) >= dep.dma_cnt:
                            continue
                        maxdma[dep.dma_key] = dep.dma_cnt
                        op.waits.append(dep)
                        continue
                    if dep.eng == eng:
                        if eng == "pe" or kind != "raw":
                            continue
                    if maxpos.get(dep.eng, -1) >= dep.pos:
                        continue
                    maxpos[dep.eng] = dep.pos
                    dep.signal = True
                    op.waits.append(dep)
        for eng in ENGS:
            n = 0
            for op in self.ops[eng]:
                if op.signal:
                    n += 1
                    op.sigidx = n


class TileD:
    def __init__(self, name, W, xslot, par, tok0, is_H=False, is_first=False):
        self.name, self.W, self.xslot, self.par, self.tok0 = name, W, xslot, par, tok0
        self.is_H, self.is_first = is_H, is_first


def build_program():
    nc = bass.Bass("TRN2", target_bir_lowering=False)
    xT = nc.dram_tensor("xT", [128, DC, HALO + TOK], F32, kind="ExternalInput").ap()
    wst = nc.dram_tensor("wst", [128, WCOLS], F32, kind="ExternalInput").ap()
    gains_d = nc.dram_tensor("gains", [128, 4, DC], F32, kind="ExternalInput").ap()
    convw_d = nc.dram_tensor("convw", [128, 4, 3], F32, kind="ExternalInput").ap()
    pscale_d = nc.dram_tensor("pscale", [128, 4], F32, kind="ExternalInput").ap()
    invc_d = nc.dram_tensor("invc", [128, 4, HALO], F32, kind="ExternalInput").ap()
    poolw_d = nc.dram_tensor("poolw", [128, 4, 128], F32, kind="ExternalInput").ap()
    outT = nc.dram_tensor("outT", [128, DC, TOK], F32, kind="ExternalOutput").ap()

    S = Sched()
    from contextlib import ExitStack
    with ExitStack() as es:
        def sb(name, shape, dt):
            return es.enter_context(nc.sbuf_tensor(name, shape, dt))

        xs = sb("xs", [128, 3, DC, T], F32)
        xH = sb("xH", [128, DC, HALO], F32)
        hb = sb("hb", [128, 2, DC, T], BF16)
        hH = sb("hH", [128, DC, HALO], BF16)
        hid = sb("hid", [128, 2, FC, T], BF16)
        hidH = sb("hidH", [128, FC, HALO], BF16)
        ring = sb("ring", [128, NSLOT, SLOT], BF16)
        sq = sb("sq", [128, NSQ, T], BF16)
        sqH = sb("sqH", [128, DC, HALO], BF16)
        rstd = sb("rstd", [128, 2, T], F32)
        rstdH = sb("rstdH", [128, HALO], F32)
        silu = sb("silu", [128, NSILU, T], F32)
        gains = sb("gains_sb", [128, 4, DC], F32)
        convw = sb("convw_sb", [128, 4, 3], F32)
        pscale = sb("pscale_sb", [128, 4], F32)
        invc = sb("invc_sb", [128, 4, HALO], F32)
        poolw = sb("poolw_sb", [128, 4, 128], BF16)
        ones = sb("ones_sb", [128, 128], BF16)
        epsb = sb("eps_sb", [128, 1], F32)
        uext = sb("uext", [128, 2, HALO + T], F32)
        zext = sb("zext", [128, 2, HALO + T], F32)
        vsb = sb("vsb", [128, 2, T], F32)
        t1b = sb("t1b", [128, 2, T], F32)
        lvl = sb("lvl", [128, 2, HALO + T], F32)
        pooled = sb("pooled", [128, 2, 4, T], BF16)
        uhalo = sb("uhalo", [128, 4, HALO], F32)
        zhalo = sb("zhalo", [128, 4, HALO], F32)
        t16 = sb("t16", [128, HALO], F32)
        banks = [es.enter_context(nc.psum_tensor(f"ps{i}", [128, T], F32)) for i in range(8)]

        sem = {e: es.enter_context(nc.semaphore(f"sem_{e}")) for e in ("pe", "act", "dve")}
        dma_sems = {}

        def dsem(key):
            if key not in dma_sems:
                dma_sems[key] = es.enter_context(nc.semaphore("dma_" + "_".join(str(k) for k in key)))
            return dma_sems[key]

        ctr = {"bank": 0, "slot": 0, "sq": 0, "silu": 0, "uext": 0, "zext": 0, "vsb": 0, "t1": 0}

        def nxt(kind, n):
            v = ctr[kind]
            ctr[kind] = (v + 1) % n
            return v

        def next_bank():
            return nxt("bank", NROT)

        def x_ap(t, c):
            return xH[:, c, :] if t.is_H else xs[:, t.xslot, c, :]

        def x_reg(t, c):
            return ("xH", c) if t.is_H else ("x", t.xslot, c)

        def h_ap(t, c):
            return hH[:, c, :] if t.is_H else hb[:, t.par, c, :]

        def h_reg(t, c):
            return ("hH", c) if t.is_H else ("h", t.par, c)

        def hid_ap(t, f):
            return hidH[:, f, :] if t.is_H else hid[:, t.par, f, :]

        def hid_reg(t, f):
            return ("hidH", f) if t.is_H else ("hid", t.par, f)

        def rstd_ap(t):
            return rstdH[:, :] if t.is_H else rstd[:, t.par, :]

        def rstd_reg(t):
            return ("rstdH",) if t.is_H else ("rstd", t.par)

        slab_slot = {}
        nload = [0]
        xload_ops = {}

        def load_slab(name):
            off, n = SLABS[name]
            slot = nxt("slot", NSLOT)
            assert slot not in slab_slot.values(), (name, slot, slab_slot)
            slab_slot[name] = slot
            nload[0] += 1
            S.add("pool",
                  lambda e, slot=slot, off=off, n=n: e.dma_start(out=ring[:, slot, 0:n], in_=wst[:, off:off + n]),
                  writes=[("ring", slot)], dma_key=("ring", slot), name=f"ld{name}",
                  extra_deps=([xload_ops["A0"]] if nload[0] <= 1 else [xload_ops["B0"]] if nload[0] <= NSLOT else []))
            return slot

        pending = []
        sqstate = {0: [], 1: []}

        def defer(fn, thr, writes=(), tag=None):
            pending.append([thr, fn, set(writes), tag])

        def flush(force=False, passed=0, reads=()):
            rs = set(reads)
            for it in pending:
                it[0] -= passed
            if force or any(it[2] & rs for it in pending):
                items = list(pending)
                del pending[:]
                for it in items:
                    it[1]()
                return
            ready = [it for it in pending if it[0] <= 0]
            if ready:
                pending[:] = [it for it in pending if it[0] > 0]
                for it in ready:
                    it[1]()

        def expedite(base=8, step=2):
            for idx, it in enumerate(pending):
                it[0] = min(it[0], base + step * idx)

        def mm_group(bank, W, pairs, reads, name="", fine_reads=None):
            flush(reads=reads)
            n = len(pairs)
            if fine_reads is None:
                def emit(pe, bank=bank, W=W, pairs=pairs):
                    ins = None
                    for i, (l, r) in enumerate(pairs):
                        ins = pe.matmul(banks[bank][:, 0:W], lhsT=l, rhs=r, start=(i == 0), stop=(i == n - 1))
                    return ins
                S.add("pe", emit, reads=reads, writes=[("ps", bank)], name=name)
            else:
                for i, (l, r) in enumerate(pairs):
                    S.add("pe", lambda pe, i=i, l=l, r=r, bank=bank, W=W: pe.matmul(
                        banks[bank][:, 0:W], lhsT=l, rhs=r, start=(i == 0), stop=(i == n - 1)),
                        reads=fine_reads[i], writes=[("ps", bank)], name=name + "1")
            flush(passed=len(pairs))

        def norm_accum(t, c, thr=10):
            if t.is_H:
                S.add("act", lambda e, c=c: e.activation(out=sqH[:, c, :], in_=xH[:, c, :], func=AF.Square),
                      reads=[x_reg(t, c)], writes=[("sqH", c)], name="sqH")
                return
            i = nxt("sq", NSQ)
            if any(it[3] == ("sq", i) for it in pending):
                flush(force=True)
            S.add("act", lambda e, i=i, t=t, c=c: e.activation(out=sq[:, i, :], in_=x_ap(t, c), func=AF.Square),
                  reads=[x_reg(t, c)], writes=[("sq", i)], name="sq")
            st = sqstate[t.par]
            st.append(i)

            def sq_add(a, b):
                S.add("dve", lambda e, a=a, b=b: e.tensor_tensor(out=sq[:, a, :], in0=sq[:, a, :], in1=sq[:, b, :], op=ALU.add),
                      reads=[("sq", a), ("sq", b)], writes=[("sq", a)], name="sqadd")
            if c % 4 >= 1:
                sq_add(st[0], st[-1])
            if c % 4 == 3:
                bank = 6 + t.par
                i0_ = st[0]
                defer(lambda bank=bank, i=i0_, c=c: S.add(
                    "pe", lambda pe: pe.matmul(banks[bank][:, :], lhsT=ones[:, :], rhs=sq[:, i, :],
                                               start=(c == 3), stop=(c == DC - 1)),
                    reads=[("sq", i), ("ones",)], writes=[("ps", bank)], name="ss"), thr, tag=("sq", i0_))
                del st[:]

        def norm_rstd(t):
            W = t.W
            if t.is_H:
                bank = next_bank()
                mm_group(bank, W, [(ones[:, :], sqH[:, c, :]) for c in range(DC)],
                         reads=[("sqH", c) for c in range(DC)] + [("ones",)], name="ssH")
            else:
                bank = 6 + t.par
            S.add("act", lambda e, t=t, bank=bank, W=W: e.activation(out=rstd_ap(t), in_=banks[bank][:, 0:W], func=AF.Sqrt,
                                                                     bias=epsb[:, 0:1], scale=1.0),
                  reads=[("ps", bank), ("eps",)], writes=[rstd_reg(t)], name="sqrt")
            S.add("dve", lambda e, t=t: e.reciprocal(out=rstd_ap(t), in_=rstd_ap(t)),
                  reads=[rstd_reg(t)], writes=[rstd_reg(t)], name="recip")

        def norm_apply(t, gi, final, chunks):
            for c in chunks:
                if final:
                    S.add("dve", lambda e, t=t, c=c, gi=gi: e.scalar_tensor_tensor(
                        out=x_ap(t, c), in0=x_ap(t, c), scalar=gains[:, gi, c:c + 1], in1=rstd_ap(t),
                        op0=ALU.mult, op1=ALU.mult),
                        reads=[x_reg(t, c), rstd_reg(t), ("c_gains",)], writes=[x_reg(t, c)], name="fin")
                else:
                    S.add("dve", lambda e, t=t, c=c, gi=gi: e.scalar_tensor_tensor(
                        out=h_ap(t, c), in0=x_ap(t, c), scalar=gains[:, gi, c:c + 1], in1=rstd_ap(t),
                        op0=ALU.mult, op1=ALU.mult),
                        reads=[x_reg(t, c), rstd_reg(t), ("c_gains",)], writes=[h_reg(t, c)], name="napply")

        def store_chunks(t, q):
            S.add("sp", lambda e, t=t, q=q: e.dma_start(out=outT[:, 2 * q:2 * q + 2, t.tok0:t.tok0 + T],
                                                       in_=xs[:, t.xslot, 2 * q:2 * q + 2, :]),
                  reads=[("x", t.xslot, 2 * q), ("x", t.xslot, 2 * q + 1)], writes=[("out", t.name, q)],
                  dma_key=("st", t.xslot, q), name="store")

        def norm_finish(t, gi, final=False, thr=10, step=2):
            if t.is_H:
                norm_rstd(t)
                norm_apply(t, gi, final, range(DC))
                return
            defer(lambda: norm_rstd(t), thr)
            for q in range(4):
                def fn(q=q):
                    norm_apply(t, gi, final, (2 * q, 2 * q + 1))
                    if final:
                        store_chunks(t, q)
                wr = [] if final else [h_reg(t, 2 * q), h_reg(t, 2 * q + 1)]
                defer(fn, thr + step * (q + 1), writes=wr, tag=("final", t.xslot) if final else None)

        def gateup(k, tiles, skew=2, mid=None, mid_at=2):
            def work(s, slot, t):
                W = t.W
                for fi in range(2):
                    f = 2 * s + fi
                    hreads = [h_reg(t, kc) for kc in range(DC)] + [("ring", slot)]
                    fr = [[h_reg(t, kc), ("ring", slot)] for kc in range(DC)] if (s == 0 and fi == 0) else None
                    bg = next_bank()
                    mm_group(bg, W, [(ring[:, slot, kc * 512 + fi * 128: kc * 512 + fi * 128 + 128], h_ap(t, kc))
                                     for kc in range(DC)], hreads, name="gate", fine_reads=fr)
                    bu = next_bank()
                    mm_group(bu, W, [(ring[:, slot, kc * 512 + 256 + fi * 128: kc * 512 + 256 + fi * 128 + 128], h_ap(t, kc))
                                     for kc in range(DC)], hreads, name="up")
                    si = nxt("silu", NSILU)
                    S.add("act", lambda e, si=si, bg=bg, W=W: e.activation(out=silu[:, si, 0:W], in_=banks[bg][:, 0:W],
                                                                         func=AF.Silu),
                          reads=[("ps", bg)], writes=[("silu", si)], name="silu")
                    S.add("dve", lambda e, si=si, bu=bu, W=W, t=t, f=f: e.tensor_tensor(
                        out=hid_ap(t, f), in0=silu[:, si, 0:W], in1=banks[bu][:, 0:W], op=ALU.mult),
                        reads=[("silu", si), ("ps", bu)], writes=[hid_reg(t, f)], name="hmul")

            slots = {}
            for s in range(skew):
                nm = ("gu", k, s)
                slots[s] = slab_slot[nm] if nm in slab_slot else load_slab(nm)
            lead, last = tiles[:-1], tiles[-1]
            for s in range(skew):
                if mid is not None and s == mid_at:
                    mid()
                for t in lead:
                    work(s, slots[s], t)
            for s in range(skew):
                work(s, slots[s], last)
            for s in range(skew):
                del slab_slot[("gu", k, s)]
            for s in range(skew, 11):
                slot = load_slab(("gu", k, s))
                for t in tiles:
                    work(s, slot, t)
                del slab_slot[("gu", k, s)]

        def down(k, tiles, after, exp_step=2):
            for half in range(2):
                for t in tiles:
                    W = t.W
                    for d in range(4 * half, 4 * half + 4):
                        name = ("dn", k, d)
                        if name not in slab_slot:
                            load_slab(name)
                        slot = slab_slot[name]
                        bank = next_bank()
                        mm_group(bank, W, [(ring[:, slot, fc * 128: fc * 128 + 128], hid_ap(t, fc)) for fc in range(FC)],
                                 [hid_reg(t, fc) for fc in range(FC)] + [("ring", slot)], name="down")
                        S.add("dve", lambda e, t=t, d=d, bank=bank, W=W: e.scalar_tensor_tensor(
                            out=x_ap(t, d), in0=banks[bank][:, 0:W], scalar=0.5, in1=x_ap(t, d),
                            op0=ALU.mult, op1=ALU.add),
                            reads=[("ps", bank), x_reg(t, d)], writes=[x_reg(t, d)], name="xupd")
                        norm_accum(t, d, thr=(10 if d == DC - 1 else 30))
                    if half == 1:
                        after(t)
                for d in range(4 * half, 4 * half + 4):
                    del slab_slot[("dn", k, d)]
            expedite(step=exp_step)

        def pooling(t, g, bank):
            W = t.W
            if t.is_H:
                S.add("act", lambda e, g=g, bank=bank: e.copy(out=uhalo[:, g, :], in_=banks[bank][:, 0:HALO]),
                      reads=[("ps", bank)], writes=[("uhalo", g)], name="uH")
                return
            ub = nxt("uext", 2)
            S.add("act", lambda e, ub=ub, bank=bank: e.copy(out=uext[:, ub, HALO:HALO + T], in_=banks[bank][:, :]),
                  reads=[("ps", bank)], writes=[("uext", ub)], name="ucopy")
            S.add("act", lambda e, ub=ub, g=g: e.copy(out=uext[:, ub, 0:HALO], in_=uhalo[:, g, :]),
                  reads=[("uhalo", g)], writes=[("uext", ub)], name="uhalo_in")
            m = g + 1
            start = {m: HALO}
            for i in range(m, 1, -1):
                start[i - 1] = start[i] - (1 << (i - 1))
            prev_ap, prev_reg = (lambda lo, hi, ub=ub: uext[:, ub, lo:hi]), ("uext", ub)
            for i in range(1, m + 1):
                li = i % 2
                sh = 1 << (i - 1)
                lo = start[i]
                S.add("dve", lambda e, prev_ap=prev_ap, li=li, lo=lo, sh=sh: e.tensor_tensor(
                    out=lvl[:, li, lo:HALO + T], in0=prev_ap(lo, HALO + T), in1=prev_ap(lo - sh, HALO + T - sh), op=ALU.add),
                    reads=[prev_reg], writes=[("lvl", li)], name="padd")
                prev_ap, prev_reg = (lambda lo, hi, li=li: lvl[:, li, lo:hi]), ("lvl", li)
            w = float(1 << m)
            S.add("dve", lambda e, prev_ap=prev_ap, ub=ub, g=g, w=w, t=t: e.scalar_tensor_tensor(
                out=pooled[:, t.par, g, :], in0=prev_ap(HALO, HALO + T), scalar=1.0 / w, in1=uext[:, ub, HALO:HALO + T],
                op0=ALU.mult, op1=ALU.subtract),
                reads=[prev_reg, ("uext", ub)], writes=[("pooled", t.par, g)], name="pooled")
            if t.is_first:
                S.add("dve", lambda e, prev_ap=prev_ap, g=g: e.tensor_tensor(
                    out=t16[:, :], in0=prev_ap(HALO, 2 * HALO), in1=invc[:, g, :], op=ALU.mult),
                    reads=[prev_reg, ("c_invc",)], writes=[("t16",)], name="fix1")
                S.add("dve", lambda e, ub=ub, g=g, t=t: e.tensor_tensor(
                    out=pooled[:, t.par, g, 0:HALO], in0=t16[:, :], in1=uext[:, ub, HALO:2 * HALO], op=ALU.subtract),
                    reads=[("t16",), ("uext", ub)], writes=[("pooled", t.par, g)], name="fix2")
            S.add("act", lambda e, ub=ub, g=g: e.copy(out=uhalo[:, g, :], in_=uext[:, ub, T:T + HALO]),
                  reads=[("uext", ub)], writes=[("uhalo", g)], name="uhalo_out")

        def conv(t, j, bv, bc, bb):
            W = t.W
            vb = nxt("vsb", 2)
            S.add("act", lambda e, vb=vb, bv=bv, W=W: e.copy(out=vsb[:, vb, 0:W], in_=banks[bv][:, 0:W]),
                  reads=[("ps", bv)], writes=[("vsb", vb)], name="vcopy")
            if t.is_H:
                S.add("dve", lambda e, vb=vb, bc=bc, j=j: e.tensor_tensor(
                    out=zhalo[:, j, :], in0=vsb[:, vb, 0:HALO], in1=banks[bc][:, 0:HALO], op=ALU.mult),
                    reads=[("vsb", vb), ("ps", bc)], writes=[("zhalo", j)], name="zH")
                return
            zb = nxt("zext", 2)
            tb = nxt("t1", 2)
            S.add("act", lambda e, zb=zb, j=j: e.copy(out=zext[:, zb, 0:HALO], in_=zhalo[:, j, :]),
                  reads=[("zhalo", j)], writes=[("zext", zb)], name="zhalo_in")
            S.add("dve", lambda e, zb=zb, vb=vb, bc=bc: e.tensor_tensor(
                out=zext[:, zb, HALO:HALO + T], in0=vsb[:, vb, :], in1=banks[bc][:, :], op=ALU.mult),
                reads=[("vsb", vb), ("ps", bc)], writes=[("zext", zb)], name="z")
            si = nxt("silu", NSILU)
            S.add("act", lambda e, si=si, bb=bb: e.copy(out=silu[:, si, :], in_=banks[bb][:, :]),
                  reads=[("ps", bb)], writes=[("silu", si)], name="bcopy")
            S.add("dve", lambda e, zb=zb, tb=tb, j=j: e.tensor_scalar(
                out=t1b[:, tb, :], in0=zext[:, zb, HALO - 2:HALO - 2 + T], scalar1=convw[:, j, 0:1], scalar2=None,
                op0=ALU.mult),
                reads=[("zext", zb), ("c_convw",)], writes=[("t1", tb)], name="c0")
            for k in (1, 2):
                S.add("dve", lambda e, zb=zb, tb=tb, j=j, k=k: e.scalar_tensor_tensor(
                    out=t1b[:, tb, :], in0=zext[:, zb, HALO - 2 + k:HALO - 2 + k + T], scalar=convw[:, j, k:k + 1],
                    in1=t1b[:, tb, :], op0=ALU.mult, op1=ALU.add),
                    reads=[("zext", zb), ("t1", tb), ("c_convw",)], writes=[("t1", tb)], name="c12")
            S.add("dve", lambda e, tb=tb, si=si, t=t, j=j: e.tensor_tensor(
                out=hid_ap(t, j), in0=t1b[:, tb, :], in1=silu[:, si, :], op=ALU.mult),
                reads=[("t1", tb), ("silu", si)], writes=[hid_reg(t, j)], name="ya")
            S.add("dve", lambda e, zb=zb, j=j: e.tensor_copy(out=zhalo[:, j, :], in_=zext[:, zb, T:T + HALO]),
                  reads=[("zext", zb)], writes=[("zhalo", j)], name="zhalo_out")

        def pool_mm(t, gs):
            for g in gs:
                bank = next_bank()
                mm_group(bank, T, [(poolw[:, g, :], pooled[:, t.par, g, :])], [("pooled", t.par, g), ("poolw",)], name="poolmm")
                S.add("act", lambda e, t=t, g=g, bank=bank: e.mul(out=hid_ap(t, 4 + g), in_=banks[bank][:, :],
                                                                  mul=pscale[:, g:g + 1]),
                      reads=[("ps", bank), ("c_pscale",)], writes=[hid_reg(t, 4 + g)], name="yb")

        late_pool = []

        def w_in(tiles):
            def u_work(slot, t, gs):
                for g in gs:
                    bank = next_bank()
                    mm_group(bank, t.W, [(ring[:, slot, kc * 512 + g * 128: kc * 512 + g * 128 + 128], h_ap(t, kc))
                                         for kc in range(DC)],
                             [h_reg(t, kc) for kc in range(DC)] + [("ring", slot)], name="win_u",
                             fine_reads=([[h_reg(t, kc), ("ring", slot)] for kc in range(DC)] if g == 0 else None))
                    pooling(t, g, bank)

            def j_work(j, slot, t):
                bl = []
                for q in range(2 if t.is_H else 3):
                    bank = next_bank()
                    mm_group(bank, t.W, [(ring[:, slot, kc * 384 + q * 128: kc * 384 + q * 128 + 128], h_ap(t, kc))
                                         for kc in range(DC)],
                             [h_reg(t, kc) for kc in range(DC)] + [("ring", slot)], name="win_j")
                    bl.append(bank)
                conv(t, j, bl[0], bl[1], bl[2] if len(bl) > 2 else None)
                if j >= 1 and not t.is_H:
                    pool_mm(t, (j - 1,))

            slot_u = load_slab(("win", "u"))
            slots_j = [load_slab(("win", j)) for j in range(4)]
            lead, last = tiles[:-1], tiles[-1]
            for j in range(4):
                for t in lead:
                    u_work(slot_u, t, (j,))
                    j_work(j, slots_j[j], t)
            for j in range(4):
                u_work(slot_u, last, (j,))
                j_work(j, slots_j[j], last)
                if j == 0:
                    for t in lead:
                        if not t.is_H:
                            pool_mm(t, (3,))
            late_pool.append(last)
            del slab_slot[("win", "u")]
            for j in range(4):
                del slab_slot[("win", j)]

        def w_out(tiles, after):
            slots = [load_slab(("wo", 0)), load_slab(("wo", 1))]
            for t in tiles:
                for s in range(2):
                    slot = slots[s]
                    for oi in range(4):
                        o = 4 * s + oi
                        bank = next_bank()
                        mm_group(bank, T, [(ring[:, slot, kc * 512 + oi * 128: kc * 512 + oi * 128 + 128], hid_ap(t, kc))
                                           for kc in range(DC)],
                                 [hid_reg(t, kc) for kc in range(DC)] + [("ring", slot)], name="wout")
                        S.add("dve", lambda e, t=t, o=o, bank=bank: e.tensor_tensor(
                            out=x_ap(t, o), in0=banks[bank][:, :], in1=x_ap(t, o), op=ALU.add),
                            reads=[("ps", bank), x_reg(t, o)], writes=[x_reg(t, o)], name="xadd")
                        norm_accum(t, o, thr=(12 if o == DC - 1 else 40))
                        if o == 1 and late_pool:
                            pool_mm(late_pool.pop(), (3,))
                after(t)
            del slab_slot[("wo", 0)]
            del slab_slot[("wo", 1)]
            expedite(base=24, step=2)

        S.add("dve", lambda e: e.memset(ones[:, :], 1.0 / D), writes=[("ones",)], name="ones")
        S.add("dve", lambda e: e.memset(epsb[:, :], EPS), writes=[("eps",)], name="eps")
        S.add("act", lambda e: e.activation(out=t16[:, 0:1], in_=epsb[:, 0:1], func=AF.Square), reads=[("eps",)], writes=[("t16",)], name="warm")

        tH = TileD("H", HALO, None, 0, -HALO, is_H=True)
        pairs = [
            [TileD("A0", T, 0, 0, 0, is_first=True), TileD("B0", T, 1, 1, T)],
            [TileD("A1", T, 2, 0, 2 * T), TileD("B1", T, 0, 1, 3 * T)],
        ]

        def load_x(t):
            if t.is_H:
                S.add("sp", lambda e: e.dma_start(out=xH[:, :, :], in_=xT[:, :, 0:HALO]),
                      writes=[("xH", c) for c in range(DC)], dma_key=("xH",), name="ldxH")
            else:
                assert not any(it[3] == ("final", t.xslot) for it in pending)
                for q in range(4):
                    xload_ops[t.name] = S.add("sp", lambda e, t=t, q=q: e.dma_start(
                        out=xs[:, t.xslot, 2 * q:2 * q + 2, :],
                        in_=xT[:, 2 * q:2 * q + 2, HALO + t.tok0:HALO + t.tok0 + T]),
                        writes=[("x", t.xslot, 2 * q), ("x", t.xslot, 2 * q + 1)], dma_key=("x", t.xslot, q), name="ldx",
                        )

        stores = []

        def finish_tile(t):
            norm_finish(t, G_FINAL, final=True)

        S.add("sp", lambda e: e.dma_start(out=gains[:, :, :], in_=gains_d[:, :, :]), writes=[("c_gains",)],
              dma_key=("c", 0), name="ldgains")
        load_x(tH)
        load_x(pairs[0][0])
        S.add("sp", lambda e: e.dma_start(out=convw[:, :, :], in_=convw_d[:, :, :]), writes=[("c_convw",)],
              dma_key=("c", 1), name="ldconvw")
        S.add("sp", lambda e: e.dma_start(out=pscale[:, :], in_=pscale_d[:, :]), writes=[("c_pscale",)],
              dma_key=("c", 2), name="ldpscale")
        S.add("sp", lambda e: e.dma_start(out=invc[:, :, :], in_=invc_d[:, :, :]), writes=[("c_invc",)],
              dma_key=("c", 3), name="ldinvc")
        load_x(pairs[0][1])

        def initial_norm(t, thr_a=10, thr_b=10):
            for c in range(DC):
                norm_accum(t, c, thr=(thr_b if c == DC - 1 else thr_a))
            norm_finish(t, G_FFN1, thr=thr_b, step=2)

        for p, (tA, tB) in enumerate(pairs):
            tiles = ([tH] if p == 0 else []) + [tA, tB]
            if p == 0:
                for t in tiles:
                    initial_norm(t)
                gateup(1, tiles, skew=3)
                S.add("pool", lambda e: e.dma_start(out=poolw[:, :, :], in_=poolw_d[:, :, :]), writes=[("poolw",)],
                      dma_key=("c", 4), name="ldpoolw")
                load_x(pairs[1][0])
            else:
                load_x(tB)
                gateup(1, tiles, skew=5, mid=lambda: initial_norm(tB, 30, 40))
            down(1, tiles, lambda t: norm_finish(t, G_MIX))
            w_in(tiles)
            w_out([tA, tB], lambda t: norm_finish(t, G_FFN2, thr=12, step=4))
            gateup(2, [tA, tB], skew=3)
            if p == 0:
                initial_norm(pairs[1][0], 30, 40)
            down(2, [tA, tB], finish_tile, exp_step=16)
        flush(force=True)
        S.add("sp", lambda e: None,
              reads=[("out", n, q) for n in ("A0", "B0", "A1", "B1") for q in range(4)], name="final")

        S.plan()
        global _LAST_SCHED
        _LAST_SCHED = S

        with nc.Block() as block:
            def run(eng_name, e):
                for op in S.ops[eng_name]:
                    for dep in op.waits:
                        if dep.dma_key is not None:
                            e.wait_ge(dsem(dep.dma_key), 16 * dep.dma_cnt)
                        else:
                            e.wait_ge(sem[dep.eng], dep.sigidx)
                    ins = op.emit(e)
                    if ins is None:
                        continue
                    if op.dma_key is not None:
                        ins.then_inc(dsem(op.dma_key), 16)
                    elif op.signal:
                        ins.then_inc(sem[op.eng], 1)

            @block.tensor
            def _(e):
                run("pe", e)

            @block.scalar
            def _(e):
                run("act", e)

            @block.vector
            def _(e):
                run("dve", e)

            @block.gpsimd
            def _(e):
                run("pool", e)

            @block.sync
            def _(e):
                run("sp", e)
    return nc


_PROGRAM = None
_LAST_SCHED = None


def _prep_inputs(inp):
    x = np.asarray(inp["x"], np.float32)
    wst = _build_wstream(inp)
    gains = np.stack([np.asarray(inp["norm_ffn1"][0]), np.asarray(inp["norm_mix"][0]),
                      np.asarray(inp["norm_ffn2"][0]), np.asarray(inp["norm_final"])], axis=0)
    gains = np.ascontiguousarray(gains.reshape(4, DC, 128).transpose(2, 0, 1)).astype(np.float32)
    convw = np.ascontiguousarray(np.asarray(inp["conv_w"][0]).reshape(3, 4, 128).transpose(2, 1, 0)).astype(np.float32)
    pscale = np.ascontiguousarray(np.asarray(inp["pool_scale"][0]).reshape(4, 128).T).astype(np.float32)
    poolw = np.ascontiguousarray(np.asarray(inp["pool_w"][0]).transpose(1, 0, 2)).astype(np.float32)
    t1 = np.arange(1, HALO + 1, dtype=np.float32)
    invc_first = np.stack([1.0 / np.minimum(t1, float(w)) for w in WINDOWS], axis=0)
    invc_rest = np.stack([np.full(HALO, 1.0 / w, np.float32) for w in WINDOWS], axis=0)
    in_maps = []
    for core in range(NCORE):
        b, h = core // 2, core % 2
        xt = np.zeros((128, DC, HALO + TOK), np.float32)
        lo = h * TOK - HALO
        src = x[b, max(lo, 0):h * TOK + TOK, :]
        src = src.reshape(src.shape[0], DC, 128).transpose(2, 1, 0)
        xt[:, :, HALO + TOK - src.shape[2]:] = src
        ic = invc_first if h == 0 else invc_rest
        in_maps.append({
            "xT": xt, "wst": wst, "gains": gains, "convw": convw, "pscale": pscale,
            "invc": np.ascontiguousarray(np.broadcast_to(ic[None], (128, 4, HALO))).astype(np.float32),
            "poolw": poolw,
        })
    return in_maps


def kernel(**inputs):
    global _PROGRAM
    if _PROGRAM is None:
        _PROGRAM = build_program()
    in_maps = _prep_inputs(inputs)
    res = run_bass_kernel_spmd(_PROGRAM, in_maps, core_ids=list(range(NCORE)))
    out = np.empty((4, 4096, D), np.float32)
    for core in range(NCORE):
        b, h = core // 2, core % 2
        o = res.results[core]["outT"]
        out[b, h * TOK:(h + 1) * TOK, :] = o.transpose(2, 1, 0).reshape(TOK, D)
    return out
```

```python
import numpy as np
import concourse.bass as bass
import concourse.mybir as mybir
from concourse.bass_utils import run_bass_kernel_spmd

F32 = mybir.dt.float32
BF16 = mybir.dt.bfloat16
AF = mybir.ActivationFunctionType
ALU = mybir.AluOpType

D = 1024
DC = 8
DFF = 2816
FC = 22
T = 512
HALO = 16
TOK = 2048
NCORE = 8
EPS = 1e-6
WINDOWS = (2, 4, 8, 16)

NSLOT = 6
SLOT = 4096
NROT = 6
NSQ = 8
NSILU = 3

G_FFN1, G_MIX, G_FFN2, G_FINAL = 0, 1, 2, 3


def _slab_table():
    tab = {}
    off = 0

    def put(name, n):
        nonlocal off
        tab[name] = (off, n)
        off += n

    for k in (1, 2):
        for s in range(11):
            put(("gu", k, s), 8 * 2 * 256)
        for d in range(8):
            put(("dn", k, d), FC * 128)
    put(("win", "u"), 8 * 512)
    for j in range(4):
        put(("win", j), 8 * 384)
    for s in range(2):
        put(("wo", s), 8 * 512)
    return tab, off


SLABS, WCOLS = _slab_table()


def _build_wstream(inp):
    w = np.empty((128, WCOLS), np.float32)

    def kview(m):
        K, N = m.shape
        return m.reshape(K // 128, 128, N).transpose(1, 0, 2)

    for k in (1, 2):
        wg = kview(np.asarray(inp[f"ffn{k}_w_gate"][0]))
        wu = kview(np.asarray(inp[f"ffn{k}_w_up"][0]))
        wd = kview(np.asarray(inp[f"ffn{k}_w_down"][0]))
        for s in range(11):
            off, n = SLABS[("gu", k, s)]
            blk = np.stack([wg[:, :, 256 * s:256 * (s + 1)], wu[:, :, 256 * s:256 * (s + 1)]], axis=2)
            w[:, off:off + n] = blk.reshape(128, n)
        for d in range(8):
            off, n = SLABS[("dn", k, d)]
            w[:, off:off + n] = wd[:, :, 128 * d:128 * (d + 1)].reshape(128, n)
    win = kview(np.asarray(inp["w_in"][0]))
    off, n = SLABS[("win", "u")]
    w[:, off:off + n] = win[:, :, 1536:2048].reshape(128, n)
    for j in range(4):
        off, n = SLABS[("win", j)]
        blk = np.stack([win[:, :, 0 + 128 * j:128 * (j + 1)],
                        win[:, :, 1024 + 128 * j:1024 + 128 * (j + 1)],
                        win[:, :, 512 + 128 * j:512 + 128 * (j + 1)]],
                       axis=2)
        w[:, off:off + n] = blk.reshape(128, n)
    wo = kview(np.asarray(inp["w_out"][0]))
    for s in range(2):
        off, n = SLABS[("wo", s)]
        w[:, off:off + n] = wo[:, :, 512 * s:512 * (s + 1)].reshape(128, n)
    return w


ENGS = ("pe", "act", "dve", "pool", "sp")


class Op:
    __slots__ = ("eng", "emit", "deps", "pos", "signal", "sigidx", "dma_key", "dma_cnt", "name", "waits")

    def __init__(self, eng, emit, name):
        self.eng = eng
        self.emit = emit
        self.deps = {}
        self.signal = False
        self.sigidx = 0
        self.dma_key = None
        self.dma_cnt = 0
        self.name = name
        self.waits = []


class Sched:
    def __init__(self):
        self.ops = {e: [] for e in ENGS}
        self.last_writer = {}
        self.readers = {}
        self.dma_counts = {}

    def add(self, eng, emit, reads=(), writes=(), dma_key=None, name="", extra_deps=()):
        op = Op(eng, emit, name)
        for dep in extra_deps:
            op.deps[dep] = "raw"
        for r in reads:
            w = self.last_writer.get(r)
            if w is not None:
                op.deps[w] = "raw"
        for r in writes:
            w = self.last_writer.get(r)
            if w is not None and w not in op.deps:
                op.deps[w] = "waw"
            for rd in self.readers.get(r, ()):
                if rd not in op.deps:
                    op.deps[rd] = "war"
        op.deps.pop(op, None)
        for r in reads:
            self.readers.setdefault(r, []).append(op)
        for r in writes:
            self.last_writer[r] = op
            self.readers[r] = []
        if dma_key is not None:
            op.dma_key = dma_key
            self.dma_counts[dma_key] = self.dma_counts.get(dma_key, 0) + 1
            op.dma_cnt = self.dma_counts[dma_key]
        op.pos = len(self.ops[eng])
        self.ops[eng].append(op)
        return op

    def plan(self):
        for eng in ENGS:
            maxpos = {}
            maxdma = {}
            for op in self.ops[eng]:
                for dep, kind in sorted(op.deps.items(), key=lambda kv: -kv[0].pos):
                    if dep.dma_key is not None:
                        if maxdma.get(dep.dma_key, 0) >= dep.dma_cnt:
                            continue
                        maxdma[dep.dma_key] = dep.dma_cnt
                        op.waits.append(dep)
                        continue
                    if dep.eng == eng:
                        if eng == "pe" or kind != "raw":
                            continue
                    if maxpos.get(dep.eng, -1) >= dep.pos:
                        continue
                    maxpos[dep.eng] = dep.pos
                    dep.signal = True
                    op.waits.append(dep)
        for eng in ENGS:
            n = 0
            for op in self.ops[eng]:
                if op.signal:
                    n += 1
                    op.sigidx = n


class TileD:
    def __init__(self, name, W, xslot, par, tok0, is_H=False, is_first=False):
        self.name, self.W, self.xslot, self.par, self.tok0 = name, W, xslot, par, tok0
        self.is_H, self.is_first = is_H, is_first


def build_program():
    nc = bass.Bass("TRN2", target_bir_lowering=False)
    xT = nc.dram_tensor("xT", [128, DC, HALO + TOK], F32, kind="ExternalInput").ap()
    wst = nc.dram_tensor("wst", [128, WCOLS], F32, kind="ExternalInput").ap()
    gains_d = nc.dram_tensor("gains", [128, 4, DC], F32, kind="ExternalInput").ap()
    convw_d = nc.dram_tensor("convw", [128, 4, 3], F32, kind="ExternalInput").ap()
    pscale_d = nc.dram_tensor("pscale", [128, 4], F32, kind="ExternalInput").ap()
    invc_d = nc.dram_tensor("invc", [128, 4, HALO], F32, kind="ExternalInput").ap()
    poolw_d = nc.dram_tensor("poolw", [128, 4, 128], F32, kind="ExternalInput").ap()
    outT = nc.dram_tensor("outT", [128, DC, TOK], F32, kind="ExternalOutput").ap()

    S = Sched()
    from contextlib import ExitStack
    with ExitStack() as es:
        def sb(name, shape, dt):
            return es.enter_context(nc.sbuf_tensor(name, shape, dt))

        xs = sb("xs", [128, 3, DC, T], F32)
        xH = sb("xH", [128, DC, HALO], F32)
        hb = sb("hb", [128, 2, DC, T], BF16)
        hH = sb("hH", [128, DC, HALO], BF16)
        hid = sb("hid", [128, 2, FC, T], BF16)
        hidH = sb("hidH", [128, FC, HALO], BF16)
        ring = sb("ring", [128, NSLOT, SLOT], BF16)
        sq = sb("sq", [128, NSQ, T], BF16)
        sqH = sb("sqH", [128, DC, HALO], BF16)
        rstd = sb("rstd", [128, 2, T], F32)
        rstdH = sb("rstdH", [128, HALO], F32)
        silu = sb("silu", [128, NSILU, T], F32)
        gains = sb("gains_sb", [128, 4, DC], F32)
        convw = sb("convw_sb", [128, 4, 3], F32)
        pscale = sb("pscale_sb", [128, 4], F32)
        invc = sb("invc_sb", [128, 4, HALO], F32)
        poolw = sb("poolw_sb", [128, 4, 128], BF16)
        ones = sb("ones_sb", [128, 128], BF16)
        epsb = sb("eps_sb", [128, 1], F32)
        uext = sb("uext", [128, 2, HALO + T], F32)
        zext = sb("zext", [128, 2, HALO + T], F32)
        vsb = sb("vsb", [128, 2, T], F32)
        t1b = sb("t1b", [128, 2, T], F32)
        lvl = sb("lvl", [128, 2, HALO + T], F32)
        pooled = sb("pooled", [128, 2, 4, T], BF16)
        uhalo = sb("uhalo", [128, 4, HALO], F32)
        zhalo = sb("zhalo", [128, 4, HALO], F32)
        t16 = sb("t16", [128, HALO], F32)
        banks = [es.enter_context(nc.psum_tensor(f"ps{i}", [128, T], F32)) for i in range(8)]

        sem = {e: es.enter_context(nc.semaphore(f"sem_{e}")) for e in ("pe", "act", "dve")}
        dma_sems = {}

        def dsem(key):
            if key not in dma_sems:
                dma_sems[key] = es.enter_context(nc.semaphore("dma_" + "_".join(str(k) for k in key)))
            return dma_sems[key]

        ctr = {"bank": 0, "slot": 0, "sq": 0, "silu": 0, "uext": 0, "zext": 0, "vsb": 0, "t1": 0}

        def nxt(kind, n):
            v = ctr[kind]
            ctr[kind] = (v + 1) % n
            return v

        def next_bank():
            return nxt("bank", NROT)

        def x_ap(t, c):
            return xH[:, c, :] if t.is_H else xs[:, t.xslot, c, :]

        def x_reg(t, c):
            return ("xH", c) if t.is_H else ("x", t.xslot, c)

        def h_ap(t, c):
            return hH[:, c, :] if t.is_H else hb[:, t.par, c, :]

        def h_reg(t, c):
            return ("hH", c) if t.is_H else ("h", t.par, c)

        def hid_ap(t, f):
            return hidH[:, f, :] if t.is_H else hid[:, t.par, f, :]

        def hid_reg(t, f):
            return ("hidH", f) if t.is_H else ("hid", t.par, f)

        def rstd_ap(t):
            return rstdH[:, :] if t.is_H else rstd[:, t.par, :]

        def rstd_reg(t):
            return ("rstdH",) if t.is_H else ("rstd", t.par)

        slab_slot = {}
        nload = [0]
        xload_ops = {}

        def load_slab(name):
            off, n = SLABS[name]
            slot = nxt("slot", NSLOT)
            assert slot not in slab_slot.values(), (name, slot, slab_slot)
            slab_slot[name] = slot
            nload[0] += 1
            S.add("pool",
                  lambda e, slot=slot, off=off, n=n: e.dma_start(out=ring[:, slot, 0:n], in_=wst[:, off:off + n]),
                  writes=[("ring", slot)], dma_key=("ring", slot), name=f"ld{name}",
                  extra_deps=([xload_ops["A0"]] if nload[0] <= 1 else [xload_ops["B0"]] if nload[0] <= NSLOT else []))
            return slot

        pending = []
        sqstate = {0: [], 1: []}

        def defer(fn, thr, writes=(), tag=None):
            pending.append([thr, fn, set(writes), tag])

        def flush(force=False, passed=0, reads=()):
            rs = set(reads)
            for it in pending:
                it[0] -= passed
            if force or any(it[2] & rs for it in pending):
                items = list(pending)
                del pending[:]
                for it in items:
                    it[1]()
                return
            ready = [it for it in pending if it[0] <= 0]
            if ready:
                pending[:] = [it for it in pending if it[0] > 0]
                for it in ready:
                    it[1]()

        def expedite(base=8, step=2):
            for idx, it in enumerate(pending):
                it[0] = min(it[0], base + step * idx)

        def mm_group(bank, W, pairs, reads, name="", fine_reads=None):
            flush(reads=reads)
            n = len(pairs)
            if fine_reads is None:
                def emit(pe, bank=bank, W=W, pairs=pairs):
                    ins = None
                    for i, (l, r) in enumerate(pairs):
                        ins = pe.matmul(banks[bank][:, 0:W], lhsT=l, rhs=r, start=(i == 0), stop=(i == n - 1))
                    return ins
                S.add("pe", emit, reads=reads, writes=[("ps", bank)], name=name)
            else:
                for i, (l, r) in enumerate(pairs):
                    S.add("pe", lambda pe, i=i, l=l, r=r, bank=bank, W=W: pe.matmul(
                        banks[bank][:, 0:W], lhsT=l, rhs=r, start=(i == 0), stop=(i == n - 1)),
                        reads=fine_reads[i], writes=[("ps", bank)], name=name + "1")
            flush(passed=len(pairs))

        def norm_accum(t, c, thr=10):
            if t.is_H:
                S.add("act", lambda e, c=c: e.activation(out=sqH[:, c, :], in_=xH[:, c, :], func=AF.Square),
                      reads=[x_reg(t, c)], writes=[("sqH", c)], name="sqH")
                return
            i = nxt("sq", NSQ)
            if any(it[3] == ("sq", i) for it in pending):
                flush(force=True)
            S.add("act", lambda e, i=i, t=t, c=c: e.activation(out=sq[:, i, :], in_=x_ap(t, c), func=AF.Square),
                  reads=[x_reg(t, c)], writes=[("sq", i)], name="sq")
            st = sqstate[t.par]
            st.append(i)

            def sq_add(a, b):
                S.add("dve", lambda e, a=a, b=b: e.tensor_tensor(out=sq[:, a, :], in0=sq[:, a, :], in1=sq[:, b, :], op=ALU.add),
                      reads=[("sq", a), ("sq", b)], writes=[("sq", a)], name="sqadd")
            if c % 4 >= 1:
                sq_add(st[0], st[-1])
            if c % 4 == 3:
                bank = 6 + t.par
                i0_ = st[0]
                defer(lambda bank=bank, i=i0_, c=c: S.add(
                    "pe", lambda pe: pe.matmul(banks[bank][:, :], lhsT=ones[:, :], rhs=sq[:, i, :],
                                               start=(c == 3), stop=(c == DC - 1)),
                    reads=[("sq", i), ("ones",)], writes=[("ps", bank)], name="ss"), thr, tag=("sq", i0_))
                del st[:]

        def norm_rstd(t):
            W = t.W
            if t.is_H:
                bank = next_bank()
                mm_group(bank, W, [(ones[:, :], sqH[:, c, :]) for c in range(DC)],
                         reads=[("sqH", c) for c in range(DC)] + [("ones",)], name="ssH")
            else:
                bank = 6 + t.par
            S.add("act", lambda e, t=t, bank=bank, W=W: e.activation(out=rstd_ap(t), in_=banks[bank][:, 0:W], func=AF.Ln,
                                                                     bias=epsb[:, 0:1], scale=1.0),
                  reads=[("ps", bank), ("eps",)], writes=[rstd_reg(t)], name="ln")
            S.add("act", lambda e, t=t: e.activation(out=rstd_ap(t), in_=rstd_ap(t), func=AF.Exp, scale=-0.5),
                  reads=[rstd_reg(t)], writes=[rstd_reg(t)], name="exp")

        def norm_apply(t, gi, final, chunks):
            for c in chunks:
                if final:
                    S.add("dve", lambda e, t=t, c=c, gi=gi: e.scalar_tensor_tensor(
                        out=x_ap(t, c), in0=x_ap(t, c), scalar=gains[:, gi, c:c + 1], in1=rstd_ap(t),
                        op0=ALU.mult, op1=ALU.mult),
                        reads=[x_reg(t, c), rstd_reg(t), ("c_gains",)], writes=[x_reg(t, c)], name="fin")
                else:
                    S.add("dve", lambda e, t=t, c=c, gi=gi: e.scalar_tensor_tensor(
                        out=h_ap(t, c), in0=x_ap(t, c), scalar=gains[:, gi, c:c + 1], in1=rstd_ap(t),
                        op0=ALU.mult, op1=ALU.mult),
                        reads=[x_reg(t, c), rstd_reg(t), ("c_gains",)], writes=[h_reg(t, c)], name="napply")

        def store_chunks(t, q):
            S.add("sp", lambda e, t=t, q=q: e.dma_start(out=outT[:, 2 * q:2 * q + 2, t.tok0:t.tok0 + T],
                                                       in_=xs[:, t.xslot, 2 * q:2 * q + 2, :]),
                  reads=[("x", t.xslot, 2 * q), ("x", t.xslot, 2 * q + 1)], writes=[("out", t.name, q)],
                  dma_key=("st", t.xslot, q), name="store")

        def norm_finish(t, gi, final=False, thr=10, step=2):
            if t.is_H:
                norm_rstd(t)
                norm_apply(t, gi, final, range(DC))
                return
            defer(lambda: norm_rstd(t), thr)
            for q in range(4):
                def fn(q=q):
                    norm_apply(t, gi, final, (2 * q, 2 * q + 1))
                    if final:
                        store_chunks(t, q)
                wr = [] if final else [h_reg(t, 2 * q), h_reg(t, 2 * q + 1)]
                defer(fn, thr + step * (q + 1), writes=wr, tag=("final", t.xslot) if final else None)

        def gateup(k, tiles, skew=2, mid=None, mid_at=2):
            def work(s, slot, t):
                W = t.W
                for fi in range(2):
                    f = 2 * s + fi
                    hreads = [h_reg(t, kc) for kc in range(DC)] + [("ring", slot)]
                    fr = [[h_reg(t, kc), ("ring", slot)] for kc in range(DC)] if (s == 0 and fi == 0) else None
                    bg = next_bank()
                    mm_group(bg, W, [(ring[:, slot, kc * 512 + fi * 128: kc * 512 + fi * 128 + 128], h_ap(t, kc))
                                     for kc in range(DC)], hreads, name="gate", fine_reads=fr)
                    bu = next_bank()
                    mm_group(bu, W, [(ring[:, slot, kc * 512 + 256 + fi * 128: kc * 512 + 256 + fi * 128 + 128], h_ap(t, kc))
                                     for kc in range(DC)], hreads, name="up")
                    si = nxt("silu", NSILU)
                    S.add("act", lambda e, si=si, bg=bg, W=W: e.activation(out=silu[:, si, 0:W], in_=banks[bg][:, 0:W],
                                                                         func=AF.Silu),
                          reads=[("ps", bg)], writes=[("silu", si)], name="silu")
                    S.add("dve", lambda e, si=si, bu=bu, W=W, t=t, f=f: e.tensor_tensor(
                        out=hid_ap(t, f), in0=silu[:, si, 0:W], in1=banks[bu][:, 0:W], op=ALU.mult),
                        reads=[("silu", si), ("ps", bu)], writes=[hid_reg(t, f)], name="hmul")

            slots = {}
            for s in range(skew):
                nm = ("gu", k, s)
                slots[s] = slab_slot[nm] if nm in slab_slot else load_slab(nm)
            lead, last = tiles[:-1], tiles[-1]
            for s in range(skew):
                if mid is not None and s == mid_at:
                    mid()
                for t in lead:
                    work(s, slots[s], t)
            for s in range(skew):
                work(s, slots[s], last)
            for s in range(skew):
                del slab_slot[("gu", k, s)]
            for s in range(skew, 11):
                slot = load_slab(("gu", k, s))
                for t in tiles:
                    work(s, slot, t)
                del slab_slot[("gu", k, s)]

        def down(k, tiles, after, exp_step=2):
            for half in range(2):
                for t in tiles:
                    W = t.W
                    for d in range(4 * half, 4 * half + 4):
                        name = ("dn", k, d)
                        if name not in slab_slot:
                            load_slab(name)
                        slot = slab_slot[name]
                        bank = next_bank()
                        mm_group(bank, W, [(ring[:, slot, fc * 128: fc * 128 + 128], hid_ap(t, fc)) for fc in range(FC)],
                                 [hid_reg(t, fc) for fc in range(FC)] + [("ring", slot)], name="down")
                        S.add("dve", lambda e, t=t, d=d, bank=bank, W=W: e.scalar_tensor_tensor(
                            out=x_ap(t, d), in0=banks[bank][:, 0:W], scalar=0.5, in1=x_ap(t, d),
                            op0=ALU.mult, op1=ALU.add),
                            reads=[("ps", bank), x_reg(t, d)], writes=[x_reg(t, d)], name="xupd")
                        norm_accum(t, d, thr=(10 if d == DC - 1 else 30))
                    if half == 1:
                        after(t)
                for d in range(4 * half, 4 * half + 4):
                    del slab_slot[("dn", k, d)]
            expedite(step=exp_step)

        def pooling(t, g, bank):
            W = t.W
            if t.is_H:
                S.add("act", lambda e, g=g, bank=bank: e.copy(out=uhalo[:, g, :], in_=banks[bank][:, 0:HALO]),
                      reads=[("ps", bank)], writes=[("uhalo", g)], name="uH")
                return
            ub = nxt("uext", 2)
            S.add("act", lambda e, ub=ub, bank=bank: e.copy(out=uext[:, ub, HALO:HALO + T], in_=banks[bank][:, :]),
                  reads=[("ps", bank)], writes=[("uext", ub)], name="ucopy")
            S.add("act", lambda e, ub=ub, g=g: e.copy(out=uext[:, ub, 0:HALO], in_=uhalo[:, g, :]),
                  reads=[("uhalo", g)], writes=[("uext", ub)], name="uhalo_in")
            m = g + 1
            start = {m: HALO}
            for i in range(m, 1, -1):
                start[i - 1] = start[i] - (1 << (i - 1))
            prev_ap, prev_reg = (lambda lo, hi, ub=ub: uext[:, ub, lo:hi]), ("uext", ub)
            for i in range(1, m + 1):
                li = i % 2
                sh = 1 << (i - 1)
                lo = start[i]
                S.add("dve", lambda e, prev_ap=prev_ap, li=li, lo=lo, sh=sh: e.tensor_tensor(
                    out=lvl[:, li, lo:HALO + T], in0=prev_ap(lo, HALO + T), in1=prev_ap(lo - sh, HALO + T - sh), op=ALU.add),
                    reads=[prev_reg], writes=[("lvl", li)], name="padd")
                prev_ap, prev_reg = (lambda lo, hi, li=li: lvl[:, li, lo:hi]), ("lvl", li)
            w = float(1 << m)
            S.add("dve", lambda e, prev_ap=prev_ap, ub=ub, g=g, w=w, t=t: e.scalar_tensor_tensor(
                out=pooled[:, t.par, g, :], in0=prev_ap(HALO, HALO + T), scalar=1.0 / w, in1=uext[:, ub, HALO:HALO + T],
                op0=ALU.mult, op1=ALU.subtract),
                reads=[prev_reg, ("uext", ub)], writes=[("pooled", t.par, g)], name="pooled")
            if t.is_first:
                S.add("dve", lambda e, prev_ap=prev_ap, g=g: e.tensor_tensor(
                    out=t16[:, :], in0=prev_ap(HALO, 2 * HALO), in1=invc[:, g, :], op=ALU.mult),
                    reads=[prev_reg, ("c_invc",)], writes=[("t16",)], name="fix1")
                S.add("dve", lambda e, ub=ub, g=g, t=t: e.tensor_tensor(
                    out=pooled[:, t.par, g, 0:HALO], in0=t16[:, :], in1=uext[:, ub, HALO:2 * HALO], op=ALU.subtract),
                    reads=[("t16",), ("uext", ub)], writes=[("pooled", t.par, g)], name="fix2")
            S.add("act", lambda e, ub=ub, g=g: e.copy(out=uhalo[:, g, :], in_=uext[:, ub, T:T + HALO]),
                  reads=[("uext", ub)], writes=[("uhalo", g)], name="uhalo_out")

        def conv(t, j, bv, bc, bb):
            W = t.W
            vb = nxt("vsb", 2)
            S.add("act", lambda e, vb=vb, bv=bv, W=W: e.copy(out=vsb[:, vb, 0:W], in_=banks[bv][:, 0:W]),
                  reads=[("ps", bv)], writes=[("vsb", vb)], name="vcopy")
            if t.is_H:
                S.add("dve", lambda e, vb=vb, bc=bc, j=j: e.tensor_tensor(
                    out=zhalo[:, j, :], in0=vsb[:, vb, 0:HALO], in1=banks[bc][:, 0:HALO], op=ALU.mult),
                    reads=[("vsb", vb), ("ps", bc)], writes=[("zhalo", j)], name="zH")
                return
            zb = nxt("zext", 2)
            tb = nxt("t1", 2)
            S.add("act", lambda e, zb=zb, j=j: e.copy(out=zext[:, zb, 0:HALO], in_=zhalo[:, j, :]),
                  reads=[("zhalo", j)], writes=[("zext", zb)], name="zhalo_in")
            S.add("dve", lambda e, zb=zb, vb=vb, bc=bc: e.tensor_tensor(
                out=zext[:, zb, HALO:HALO + T], in0=vsb[:, vb, :], in1=banks[bc][:, :], op=ALU.mult),
                reads=[("vsb", vb), ("ps", bc)], writes=[("zext", zb)], name="z")
            si = nxt("silu", NSILU)
            S.add("act", lambda e, si=si, bb=bb: e.copy(out=silu[:, si, :], in_=banks[bb][:, :]),
                  reads=[("ps", bb)], writes=[("silu", si)], name="bcopy")
            S.add("dve", lambda e, zb=zb, tb=tb, j=j: e.tensor_scalar(
                out=t1b[:, tb, :], in0=zext[:, zb, HALO - 2:HALO - 2 + T], scalar1=convw[:, j, 0:1], scalar2=None,
                op0=ALU.mult),
                reads=[("zext", zb), ("c_convw",)], writes=[("t1", tb)], name="c0")
            for k in (1, 2):
                S.add("dve", lambda e, zb=zb, tb=tb, j=j, k=k: e.scalar_tensor_tensor(
                    out=t1b[:, tb, :], in0=zext[:, zb, HALO - 2 + k:HALO - 2 + k + T], scalar=convw[:, j, k:k + 1],
                    in1=t1b[:, tb, :], op0=ALU.mult, op1=ALU.add),
                    reads=[("zext", zb), ("t1", tb), ("c_convw",)], writes=[("t1", tb)], name="c12")
            S.add("dve", lambda e, tb=tb, si=si, t=t, j=j: e.tensor_tensor(
                out=hid_ap(t, j), in0=t1b[:, tb, :], in1=silu[:, si, :], op=ALU.mult),
                reads=[("t1", tb), ("silu", si)], writes=[hid_reg(t, j)], name="ya")
            S.add("dve", lambda e, zb=zb, j=j: e.tensor_copy(out=zhalo[:, j, :], in_=zext[:, zb, T:T + HALO]),
                  reads=[("zext", zb)], writes=[("zhalo", j)], name="zhalo_out")

        def pool_mm(t, gs):
            for g in gs:
                bank = next_bank()
                mm_group(bank, T, [(poolw[:, g, :], pooled[:, t.par, g, :])], [("pooled", t.par, g), ("poolw",)], name="poolmm")
                S.add("act", lambda e, t=t, g=g, bank=bank: e.mul(out=hid_ap(t, 4 + g), in_=banks[bank][:, :],
                                                                  mul=pscale[:, g:g + 1]),
                      reads=[("ps", bank), ("c_pscale",)], writes=[hid_reg(t, 4 + g)], name="yb")

        late_pool = []

        def w_in(tiles):
            def u_work(slot, t, gs):
                for g in gs:
                    bank = next_bank()
                    mm_group(bank, t.W, [(ring[:, slot, kc * 512 + g * 128: kc * 512 + g * 128 + 128], h_ap(t, kc))
                                         for kc in range(DC)],
                             [h_reg(t, kc) for kc in range(DC)] + [("ring", slot)], name="win_u",
                             fine_reads=([[h_reg(t, kc), ("ring", slot)] for kc in range(DC)] if g == 0 else None))
                    pooling(t, g, bank)

            def j_work(j, slot, t):
                bl = []
                for q in range(2 if t.is_H else 3):
                    bank = next_bank()
                    mm_group(bank, t.W, [(ring[:, slot, kc * 384 + q * 128: kc * 384 + q * 128 + 128], h_ap(t, kc))
                                         for kc in range(DC)],
                             [h_reg(t, kc) for kc in range(DC)] + [("ring", slot)], name="win_j")
                    bl.append(bank)
                conv(t, j, bl[0], bl[1], bl[2] if len(bl) > 2 else None)
                if j >= 1 and not t.is_H:
                    pool_mm(t, (j - 1,))

            slot_u = load_slab(("win", "u"))
            slots_j = [load_slab(("win", j)) for j in range(4)]
            lead, last = tiles[:-1], tiles[-1]
            for j in range(4):
                for t in lead:
                    u_work(slot_u, t, (j,))
                    j_work(j, slots_j[j], t)
            for j in range(4):
                u_work(slot_u, last, (j,))
                j_work(j, slots_j[j], last)
                if j == 0:
                    for t in lead:
                        if not t.is_H:
                            pool_mm(t, (3,))
            late_pool.append(last)
            del slab_slot[("win", "u")]
            for j in range(4):
                del slab_slot[("win", j)]

        def w_out(tiles, after):
            slots = [load_slab(("wo", 0)), load_slab(("wo", 1))]
            for t in tiles:
                for s in range(2):
                    slot = slots[s]
                    for oi in range(4):
                        o = 4 * s + oi
                        bank = next_bank()
                        mm_group(bank, T, [(ring[:, slot, kc * 512 + oi * 128: kc * 512 + oi * 128 + 128], hid_ap(t, kc))
                                           for kc in range(DC)],
                                 [hid_reg(t, kc) for kc in range(DC)] + [("ring", slot)], name="wout")
                        S.add("dve", lambda e, t=t, o=o, bank=bank: e.tensor_tensor(
                            out=x_ap(t, o), in0=banks[bank][:, :], in1=x_ap(t, o), op=ALU.add),
                            reads=[("ps", bank), x_reg(t, o)], writes=[x_reg(t, o)], name="xadd")
                        norm_accum(t, o, thr=(12 if o == DC - 1 else 40))
                        if o == 1 and late_pool:
                            pool_mm(late_pool.pop(), (3,))
                after(t)
            del slab_slot[("wo", 0)]
            del slab_slot[("wo", 1)]
            expedite(base=24, step=2)

        S.add("dve", lambda e: e.memset(ones[:, :], 1.0 / D), writes=[("ones",)], name="ones")
        S.add("dve", lambda e: e.memset(epsb[:, :], EPS), writes=[("eps",)], name="eps")
        S.add("act", lambda e: e.activation(out=t16[:, 0:1], in_=epsb[:, 0:1], func=AF.Square), reads=[("eps",)], writes=[("t16",)], name="warm")

        tH = TileD("H", HALO, None, 0, -HALO, is_H=True)
        pairs = [
            [TileD("A0", T, 0, 0, 0, is_first=True), TileD("B0", T, 1, 1, T)],
            [TileD("A1", T, 2, 0, 2 * T), TileD("B1", T, 0, 1, 3 * T)],
        ]

        def load_x(t):
            if t.is_H:
                S.add("sp", lambda e: e.dma_start(out=xH[:, :, :], in_=xT[:, :, 0:HALO]),
                      writes=[("xH", c) for c in range(DC)], dma_key=("xH",), name="ldxH")
            else:
                assert not any(it[3] == ("final", t.xslot) for it in pending)
                for q in range(4):
                    xload_ops[t.name] = S.add("sp", lambda e, t=t, q=q: e.dma_start(
                        out=xs[:, t.xslot, 2 * q:2 * q + 2, :],
                        in_=xT[:, 2 * q:2 * q + 2, HALO + t.tok0:HALO + t.tok0 + T]),
                        writes=[("x", t.xslot, 2 * q), ("x", t.xslot, 2 * q + 1)], dma_key=("x", t.xslot, q), name="ldx",
                        )

        stores = []

        def finish_tile(t):
            norm_finish(t, G_FINAL, final=True)

        S.add("sp", lambda e: e.dma_start(out=gains[:, :, :], in_=gains_d[:, :, :]), writes=[("c_gains",)],
              dma_key=("c", 0), name="ldgains")
        load_x(tH)
        load_x(pairs[0][0])
        S.add("sp", lambda e: e.dma_start(out=convw[:, :, :], in_=convw_d[:, :, :]), writes=[("c_convw",)],
              dma_key=("c", 1), name="ldconvw")
        S.add("sp", lambda e: e.dma_start(out=pscale[:, :], in_=pscale_d[:, :]), writes=[("c_pscale",)],
              dma_key=("c", 2), name="ldpscale")
        S.add("sp", lambda e: e.dma_start(out=invc[:, :, :], in_=invc_d[:, :, :]), writes=[("c_invc",)],
              dma_key=("c", 3), name="ldinvc")
        load_x(pairs[0][1])

        def initial_norm(t, thr_a=10, thr_b=10):
            for c in range(DC):
                norm_accum(t, c, thr=(thr_b if c == DC - 1 else thr_a))
            norm_finish(t, G_FFN1, thr=thr_b, step=2)

        for p, (tA, tB) in enumerate(pairs):
            tiles = ([tH] if p == 0 else []) + [tA, tB]
            if p == 0:
                for t in tiles:
                    initial_norm(t)
                gateup(1, tiles, skew=3)
                S.add("pool", lambda e: e.dma_start(out=poolw[:, :, :], in_=poolw_d[:, :, :]), writes=[("poolw",)],
                      dma_key=("c", 4), name="ldpoolw")
                load_x(pairs[1][0])
            else:
                load_x(tB)
                gateup(1, tiles, skew=5, mid=lambda: initial_norm(tB, 30, 40))
            down(1, tiles, lambda t: norm_finish(t, G_MIX))
            w_in(tiles)
            w_out([tA, tB], lambda t: norm_finish(t, G_FFN2, thr=12, step=4))
            gateup(2, [tA, tB], skew=3)
            if p == 0:
                initial_norm(pairs[1][0], 30, 40)
            down(2, [tA, tB], finish_tile, exp_step=16)
        flush(force=True)
        S.add("sp", lambda e: None,
              reads=[("out", n, q) for n in ("A0", "B0", "A1", "B1") for q in range(4)], name="final")

        S.plan()
        global _LAST_SCHED
        _LAST_SCHED = S

        with nc.Block() as block:
            def run(eng_name, e):
                for op in S.ops[eng_name]:
                    for dep in op.waits:
                        if dep.dma_key is not None:
                            e.wait_ge(dsem(dep.dma_key), 16 * dep.dma_cnt)
                        else:
                            e.wait_ge(sem[dep.eng], dep.sigidx)
                    ins = op.emit(e)
                    if ins is None:
                        continue
                    if op.dma_key is not None:
                        ins.then_inc(dsem(op.dma_key), 16)
                    elif op.signal:
                        ins.then_inc(sem[op.eng], 1)

            @block.tensor
            def _(e):
                run("pe", e)

            @block.scalar
            def _(e):
                run("act", e)

            @block.vector
            def _(e):
                run("dve", e)

            @block.gpsimd
            def _(e):
                run("pool", e)

            @block.sync
            def _(e):
                run("sp", e)
    return nc


_PROGRAM = None
_LAST_SCHED = None


def _prep_inputs(inp):
    x = np.asarray(inp["x"], np.float32)
    wst = _build_wstream(inp)
    gains = np.stack([np.asarray(inp["norm_ffn1"][0]), np.asarray(inp["norm_mix"][0]),
                      np.asarray(inp["norm_ffn2"][0]), np.asarray(inp["norm_final"])], axis=0)
    gains = np.ascontiguousarray(gains.reshape(4, DC, 128).transpose(2, 0, 1)).astype(np.float32)
    convw = np.ascontiguousarray(np.asarray(inp["conv_w"][0]).reshape(3, 4, 128).transpose(2, 1, 0)).astype(np.float32)
    pscale = np.ascontiguousarray(np.asarray(inp["pool_scale"][0]).reshape(4, 128).T).astype(np.float32)
    poolw = np.ascontiguousarray(np.asarray(inp["pool_w"][0]).transpose(1, 0, 2)).astype(np.float32)
    t1 = np.arange(1, HALO + 1, dtype=np.float32)
    invc_first = np.stack([1.0 / np.minimum(t1, float(w)) for w in WINDOWS], axis=0)
    invc_rest = np.stack([np.full(HALO, 1.0 / w, np.float32) for w in WINDOWS], axis=0)
    in_maps = []
    for core in range(NCORE):
        b, h = core // 2, core % 2
        xt = np.zeros((128, DC, HALO + TOK), np.float32)
        lo = h * TOK - HALO
        src = x[b, max(lo, 0):h * TOK + TOK, :]
        src = src.reshape(src.shape[0], DC, 128).transpose(2, 1, 0)
        xt[:, :, HALO + TOK - src.shape[2]:] = src
        ic = invc_first if h == 0 else invc_rest
        in_maps.append({
            "xT": xt, "wst": wst, "gains": gains, "convw": convw, "pscale": pscale,
            "invc": np.ascontiguousarray(np.broadcast_to(ic[None], (128, 4, HALO))).astype(np.float32),
            "poolw": poolw,
        })
    return in_maps


def kernel(**inputs):
    global _PROGRAM
    if _PROGRAM is None:
        _PROGRAM = build_program()
    in_maps = _prep_inputs(inputs)
    res = run_bass_kernel_spmd(_PROGRAM, in_maps, core_ids=list(range(NCORE)))
    out = np.empty((4, 4096, D), np.float32)
    for core in range(NCORE):
        b, h = core // 2, core % 2
        o = res.results[core]["outT"]
        out[b, h * TOK:(h + 1) * TOK, :] = o.transpose(2, 1, 0).reshape(TOK, D)
    return out
```

```python
import numpy as np
import concourse.bass as bass
import concourse.mybir as mybir
from concourse.bass_utils import run_bass_kernel_spmd

F32 = mybir.dt.float32
BF16 = mybir.dt.bfloat16
AF = mybir.ActivationFunctionType
ALU = mybir.AluOpType

D = 1024
DC = 8
DFF = 2816
FC = 22
T = 512
HALO = 16
TOK = 2048
NCORE = 8
EPS = 1e-6
WINDOWS = (2, 4, 8, 16)

NSLOT = 6
SLOT = 4096
NROT = 6
NSQ = 8
NSILU = 3

G_FFN1, G_MIX, G_FFN2, G_FINAL = 0, 1, 2, 3


def _slab_table():
    tab = {}
    off = 0

    def put(name, n):
        nonlocal off
        tab[name] = (off, n)
        off += n

    for k in (1, 2):
        for s in range(11):
            put(("gu", k, s), 8 * 2 * 256)
        for d in range(8):
            put(("dn", k, d), FC * 128)
    put(("win", "u"), 8 * 512)
    for j in range(4):
        put(("win", j), 8 * 384)
    for s in range(2):
        put(("wo", s), 8 * 512)
    return tab, off


SLABS, WCOLS = _slab_table()


def _build_wstream(inp):
    w = np.empty((128, WCOLS), np.float32)

    def kview(m):
        K, N = m.shape
        return m.reshape(K // 128, 128, N).transpose(1, 0, 2)

    for k in (1, 2):
        wg = kview(np.asarray(inp[f"ffn{k}_w_gate"][0]))
        wu = kview(np.asarray(inp[f"ffn{k}_w_up"][0]))
        wd = kview(np.asarray(inp[f"ffn{k}_w_down"][0]))
        for s in range(11):
            off, n = SLABS[("gu", k, s)]
            blk = np.stack([wg[:, :, 256 * s:256 * (s + 1)], wu[:, :, 256 * s:256 * (s + 1)]], axis=2)
            w[:, off:off + n] = blk.reshape(128, n)
        for d in range(8):
            off, n = SLABS[("dn", k, d)]
            w[:, off:off + n] = wd[:, :, 128 * d:128 * (d + 1)].reshape(128, n)
    win = kview(np.asarray(inp["w_in"][0]))
    off, n = SLABS[("win", "u")]
    w[:, off:off + n] = win[:, :, 1536:2048].reshape(128, n)
    for j in range(4):
        off, n = SLABS[("win", j)]
        blk = np.stack([win[:, :, 0 + 128 * j:128 * (j + 1)],
                        win[:, :, 1024 + 128 * j:1024 + 128 * (j + 1)],
                        win[:, :, 512 + 128 * j:512 + 128 * (j + 1)]],
                       axis=2)
        w[:, off:off + n] = blk.reshape(128, n)
    wo = kview(np.asarray(inp["w_out"][0]))
    for s in range(2):
        off, n = SLABS[("wo", s)]
        w[:, off:off + n] = wo[:, :, 512 * s:512 * (s + 1)].reshape(128, n)
    return w


ENGS = ("pe", "act", "dve", "pool", "sp")


class Op:
    __slots__ = ("eng", "emit", "deps", "pos", "signal", "sigidx", "dma_key", "dma_cnt", "name", "waits")

    def __init__(self, eng, emit, name):
        self.eng = eng
        self.emit = emit
        self.deps = {}
        self.signal = False
        self.sigidx = 0
        self.dma_key = None
        self.dma_cnt = 0
        self.name = name
        self.waits = []


class Sched:
    def __init__(self):
        self.ops = {e: [] for e in ENGS}
        self.last_writer = {}
        self.readers = {}
        self.dma_counts = {}

    def add(self, eng, emit, reads=(), writes=(), dma_key=None, name="", extra_deps=()):
        op = Op(eng, emit, name)
        for dep in extra_deps:
            op.deps[dep] = "raw"
        for r in reads:
            w = self.last_writer.get(r)
            if w is not None:
                op.deps[w] = "raw"
        for r in writes:
            w = self.last_writer.get(r)
            if w is not None and w not in op.deps:
                op.deps[w] = "waw"
            for rd in self.readers.get(r, ()):
                if rd not in op.deps:
                    op.deps[rd] = "war"
        op.deps.pop(op, None)
        for r in reads:
            self.readers.setdefault(r, []).append(op)
        for r in writes:
            self.last_writer[r] = op
            self.readers[r] = []
        if dma_key is not None:
            op.dma_key = dma_key
            self.dma_counts[dma_key] = self.dma_counts.get(dma_key, 0) + 1
            op.dma_cnt = self.dma_counts[dma_key]
        op.pos = len(self.ops[eng])
        self.ops[eng].append(op)
        return op

    def plan(self):
        for eng in ENGS:
            maxpos = {}
            maxdma = {}
            for op in self.ops[eng]:
                for dep, kind in sorted(op.deps.items(), key=lambda kv: -kv[0].pos):
                    if dep.dma_key is not None:
                        if maxdma.get(dep.dma_key, 0) >= dep.dma_cnt:
                            continue
                        maxdma[dep.dma_key] = dep.dma_cnt
                        op.waits.append(dep)
                        continue
                    if dep.eng == eng:
                        if eng == "pe" or kind != "raw":
                            continue
                    if maxpos.get(dep.eng, -1) >= dep.pos:
                        continue
                    maxpos[dep.eng] = dep.pos
                    dep.signal = True
                    op.waits.append(dep)
        for eng in ENGS:
            n = 0
            for op in self.ops[eng]:
                if op.signal:
                    n += 1
                    op.sigidx = n


class TileD:
    def __init__(self, name, W, xslot, par, tok0, is_H=False, is_first=False):
        self.name, self.W, self.xslot, self.par, self.tok0 = name, W, xslot, par, tok0
        self.is_H, self.is_first = is_H, is_first


def build_program():
    nc = bass.Bass("TRN2", target_bir_lowering=False)
    xT = nc.dram_tensor("xT", [128, DC, HALO + TOK], F32, kind="ExternalInput").ap()
    wst = nc.dram_tensor("wst", [128, WCOLS], F32, kind="ExternalInput").ap()
    gains_d = nc.dram_tensor("gains", [128, 4, DC], F32, kind="ExternalInput").ap()
    convw_d = nc.dram_tensor("convw", [128, 4, 3], F32, kind="ExternalInput").ap()
    pscale_d = nc.dram_tensor("pscale", [128, 4], F32, kind="ExternalInput").ap()
    invc_d = nc.dram_tensor("invc", [128, 4, HALO], F32, kind="ExternalInput").ap()
    poolw_d = nc.dram_tensor("poolw", [128, 4, 128], F32, kind="ExternalInput").ap()
    outT = nc.dram_tensor("outT", [128, DC, TOK], F32, kind="ExternalOutput").ap()

    S = Sched()
    from contextlib import ExitStack
    with ExitStack() as es:
        def sb(name, shape, dt):
            return es.enter_context(nc.sbuf_tensor(name, shape, dt))

        xs = sb("xs", [128, 3, DC, T], F32)
        xH = sb("xH", [128, DC, HALO], F32)
        hb = sb("hb", [128, 2, DC, T], BF16)
        hH = sb("hH", [128, DC, HALO], BF16)
        hid = sb("hid", [128, 2, FC, T], BF16)
        hidH = sb("hidH", [128, FC, HALO], BF16)
        ring = sb("ring", [128, NSLOT, SLOT], BF16)
        sq = sb("sq", [128, NSQ, T], BF16)
        sqH = sb("sqH", [128, DC, HALO], BF16)
        rstd = sb("rstd", [128, 2, T], F32)
        rstdH = sb("rstdH", [128, HALO], F32)
        silu = sb("silu", [128, NSILU, T], F32)
        gains = sb("gains_sb", [128, 4, DC], F32)
        convw = sb("convw_sb", [128, 4, 3], F32)
        pscale = sb("pscale_sb", [128, 4], F32)
        invc = sb("invc_sb", [128, 4, HALO], F32)
        poolw = sb("poolw_sb", [128, 4, 128], BF16)
        ones = sb("ones_sb", [128, 128], BF16)
        epsb = sb("eps_sb", [128, 1], F32)
        uext = sb("uext", [128, 2, HALO + T], F32)
        zext = sb("zext", [128, 2, HALO + T], F32)
        vsb = sb("vsb", [128, 2, T], F32)
        t1b = sb("t1b", [128, 2, T], F32)
        lvl = sb("lvl", [128, 2, HALO + T], F32)
        pooled = sb("pooled", [128, 2, 4, T], BF16)
        uhalo = sb("uhalo", [128, 4, HALO], F32)
        zhalo = sb("zhalo", [128, 4, HALO], F32)
        t16 = sb("t16", [128, HALO], F32)
        banks = [es.enter_context(nc.psum_tensor(f"ps{i}", [128, T], F32)) for i in range(8)]

        sem = {e: es.enter_context(nc.semaphore(f"sem_{e}")) for e in ("pe", "act", "dve")}
        dma_sems = {}

        def dsem(key):
            if key not in dma_sems:
                dma_sems[key] = es.enter_context(nc.semaphore("dma_" + "_".join(str(k) for k in key)))
            return dma_sems[key]

        ctr = {"bank": 0, "slot": 0, "sq": 0, "silu": 0, "uext": 0, "zext": 0, "vsb": 0, "t1": 0}

        def nxt(kind, n):
            v = ctr[kind]
            ctr[kind] = (v + 1) % n
            return v

        def next_bank():
            return nxt("bank", NROT)

        def x_ap(t, c):
            return xH[:, c, :] if t.is_H else xs[:, t.xslot, c, :]

        def x_reg(t, c):
            return ("xH", c) if t.is_H else ("x", t.xslot, c)

        def h_ap(t, c):
            return hH[:, c, :] if t.is_H else hb[:, t.par, c, :]

        def h_reg(t, c):
            return ("hH", c) if t.is_H else ("h", t.par, c)

        def hid_ap(t, f):
            return hidH[:, f, :] if t.is_H else hid[:, t.par, f, :]

        def hid_reg(t, f):
            return ("hidH", f) if t.is_H else ("hid", t.par, f)

        def rstd_ap(t):
            return rstdH[:, :] if t.is_H else rstd[:, t.par, :]

        def rstd_reg(t):
            return ("rstdH",) if t.is_H else ("rstd", t.par)

        slab_slot = {}
        nload = [0]
        xload_ops = {}

        def load_slab(name):
            off, n = SLABS[name]
            slot = nxt("slot", NSLOT)
            assert slot not in slab_slot.values(), (name, slot, slab_slot)
            slab_slot[name] = slot
            nload[0] += 1
            S.add("pool",
                  lambda e, slot=slot, off=off, n=n: e.dma_start(out=ring[:, slot, 0:n], in_=wst[:, off:off + n]),
                  writes=[("ring", slot)], dma_key=("ring", slot), name=f"ld{name}",
                  extra_deps=([xload_ops["A0q1"]] if nload[0] <= 1 else [xload_ops["B0"]] if nload[0] <= NSLOT else []))
            return slot

        pending = []
        sqstate = {0: [], 1: []}

        def defer(fn, thr, writes=(), tag=None):
            pending.append([thr, fn, set(writes), tag])

        def flush(force=False, passed=0, reads=()):
            rs = set(reads)
            for it in pending:
                it[0] -= passed
            if force or any(it[2] & rs for it in pending):
                items = list(pending)
                del pending[:]
                for it in items:
                    it[1]()
                return
            ready = [it for it in pending if it[0] <= 0]
            if ready:
                pending[:] = [it for it in pending if it[0] > 0]
                for it in ready:
                    it[1]()

        def expedite(base=8, step=2):
            for idx, it in enumerate(pending):
                it[0] = min(it[0], base + step * idx)

        def mm_group(bank, W, pairs, reads, name="", fine_reads=None):
            flush(reads=reads)
            n = len(pairs)
            if fine_reads is None:
                def emit(pe, bank=bank, W=W, pairs=pairs):
                    ins = None
                    for i, (l, r) in enumerate(pairs):
                        ins = pe.matmul(banks[bank][:, 0:W], lhsT=l, rhs=r, start=(i == 0), stop=(i == n - 1))
                    return ins
                S.add("pe", emit, reads=reads, writes=[("ps", bank)], name=name)
            else:
                for i, (l, r) in enumerate(pairs):
                    S.add("pe", lambda pe, i=i, l=l, r=r, bank=bank, W=W: pe.matmul(
                        banks[bank][:, 0:W], lhsT=l, rhs=r, start=(i == 0), stop=(i == n - 1)),
                        reads=fine_reads[i], writes=[("ps", bank)], name=name + "1")
            flush(passed=len(pairs))

        def norm_accum(t, c, thr=10):
            if t.is_H:
                S.add("act", lambda e, c=c: e.activation(out=sqH[:, c, :], in_=xH[:, c, :], func=AF.Square),
                      reads=[x_reg(t, c)], writes=[("sqH", c)], name="sqH")
                return
            i = nxt("sq", NSQ)
            if any(it[3] == ("sq", i) for it in pending):
                flush(force=True)
            S.add("act", lambda e, i=i, t=t, c=c: e.activation(out=sq[:, i, :], in_=x_ap(t, c), func=AF.Square),
                  reads=[x_reg(t, c)], writes=[("sq", i)], name="sq")
            st = sqstate[t.par]
            st.append(i)

            def sq_add(a, b):
                S.add("dve", lambda e, a=a, b=b: e.tensor_tensor(out=sq[:, a, :], in0=sq[:, a, :], in1=sq[:, b, :], op=ALU.add),
                      reads=[("sq", a), ("sq", b)], writes=[("sq", a)], name="sqadd")
            if c % 4 >= 1:
                sq_add(st[0], st[-1])
            if c % 4 == 3:
                bank = 6 + t.par
                i0_ = st[0]
                defer(lambda bank=bank, i=i0_, c=c: S.add(
                    "pe", lambda pe: pe.matmul(banks[bank][:, :], lhsT=ones[:, :], rhs=sq[:, i, :],
                                               start=(c == 3), stop=(c == DC - 1)),
                    reads=[("sq", i), ("ones",)], writes=[("ps", bank)], name="ss"), thr, tag=("sq", i0_))
                del st[:]

        def norm_rstd(t):
            W = t.W
            if t.is_H:
                bank = next_bank()
                mm_group(bank, W, [(ones[:, :], sqH[:, c, :]) for c in range(DC)],
                         reads=[("sqH", c) for c in range(DC)] + [("ones",)], name="ssH")
            else:
                bank = 6 + t.par
            S.add("act", lambda e, t=t, bank=bank, W=W: e.activation(out=rstd_ap(t), in_=banks[bank][:, 0:W], func=AF.Ln,
                                                                     bias=epsb[:, 0:1], scale=1.0),
                  reads=[("ps", bank), ("eps",)], writes=[rstd_reg(t)], name="ln")
            S.add("act", lambda e, t=t: e.activation(out=rstd_ap(t), in_=rstd_ap(t), func=AF.Exp, scale=-0.5),
                  reads=[rstd_reg(t)], writes=[rstd_reg(t)], name="exp")

        def norm_apply(t, gi, final, chunks):
            for c in chunks:
                if final:
                    S.add("dve", lambda e, t=t, c=c, gi=gi: e.scalar_tensor_tensor(
                        out=x_ap(t, c), in0=x_ap(t, c), scalar=gains[:, gi, c:c + 1], in1=rstd_ap(t),
                        op0=ALU.mult, op1=ALU.mult),
                        reads=[x_reg(t, c), rstd_reg(t), ("c_gains",)], writes=[x_reg(t, c)], name="fin")
                else:
                    S.add("dve", lambda e, t=t, c=c, gi=gi: e.scalar_tensor_tensor(
                        out=h_ap(t, c), in0=x_ap(t, c), scalar=gains[:, gi, c:c + 1], in1=rstd_ap(t),
                        op0=ALU.mult, op1=ALU.mult),
                        reads=[x_reg(t, c), rstd_reg(t), ("c_gains",)], writes=[h_reg(t, c)], name="napply")

        def store_chunks(t, q):
            S.add("sp", lambda e, t=t, q=q: e.dma_start(out=outT[:, 2 * q:2 * q + 2, t.tok0:t.tok0 + T],
                                                       in_=xs[:, t.xslot, 2 * q:2 * q + 2, :]),
                  reads=[("x", t.xslot, 2 * q), ("x", t.xslot, 2 * q + 1)], writes=[("out", t.name, q)],
                  dma_key=("st", t.xslot, q), name="store")

        def norm_finish(t, gi, final=False, thr=10, step=2):
            if t.is_H:
                norm_rstd(t)
                norm_apply(t, gi, final, range(DC))
                return
            defer(lambda: norm_rstd(t), thr)
            for q in range(4):
                def fn(q=q):
                    norm_apply(t, gi, final, (2 * q, 2 * q + 1))
                    if final:
                        store_chunks(t, q)
                wr = [] if final else [h_reg(t, 2 * q), h_reg(t, 2 * q + 1)]
                defer(fn, thr + step * (q + 1), writes=wr, tag=("final", t.xslot) if final else None)

        def gateup(k, tiles, skew=2, mid=None, mid_at=2):
            def work(s, slot, t):
                W = t.W
                for fi in range(2):
                    f = 2 * s + fi
                    hreads = [h_reg(t, kc) for kc in range(DC)] + [("ring", slot)]
                    fr = [[h_reg(t, kc), ("ring", slot)] for kc in range(DC)] if (s == 0 and fi == 0) else None
                    bg = next_bank()
                    mm_group(bg, W, [(ring[:, slot, kc * 512 + fi * 128: kc * 512 + fi * 128 + 128], h_ap(t, kc))
                                     for kc in range(DC)], hreads, name="gate", fine_reads=fr)
                    bu = next_bank()
                    mm_group(bu, W, [(ring[:, slot, kc * 512 + 256 + fi * 128: kc * 512 + 256 + fi * 128 + 128], h_ap(t, kc))
                                     for kc in range(DC)], hreads, name="up")
                    si = nxt("silu", NSILU)
                    S.add("act", lambda e, si=si, bg=bg, W=W: e.activation(out=silu[:, si, 0:W], in_=banks[bg][:, 0:W],
                                                                         func=AF.Silu),
                          reads=[("ps", bg)], writes=[("silu", si)], name="silu")
                    S.add("dve", lambda e, si=si, bu=bu, W=W, t=t, f=f: e.tensor_tensor(
                        out=hid_ap(t, f), in0=silu[:, si, 0:W], in1=banks[bu][:, 0:W], op=ALU.mult),
                        reads=[("silu", si), ("ps", bu)], writes=[hid_reg(t, f)], name="hmul")

            slots = {}
            for s in range(skew):
                nm = ("gu", k, s)
                slots[s] = slab_slot[nm] if nm in slab_slot else load_slab(nm)
            lead, last = tiles[:-1], tiles[-1]
            for s in range(skew):
                if mid is not None and s == mid_at:
                    mid()
                for t in lead:
                    work(s, slots[s], t)
            for s in range(skew):
                work(s, slots[s], last)
            for s in range(skew):
                del slab_slot[("gu", k, s)]
            for s in range(skew, 11):
                slot = load_slab(("gu", k, s))
                for t in tiles:
                    work(s, slot, t)
                del slab_slot[("gu", k, s)]

        def down(k, tiles, after, exp_step=2):
            for half in range(2):
                for t in tiles:
                    W = t.W
                    for d in range(4 * half, 4 * half + 4):
                        name = ("dn", k, d)
                        if name not in slab_slot:
                            load_slab(name)
                        slot = slab_slot[name]
                        bank = next_bank()
                        mm_group(bank, W, [(ring[:, slot, fc * 128: fc * 128 + 128], hid_ap(t, fc)) for fc in range(FC)],
                                 [hid_reg(t, fc) for fc in range(FC)] + [("ring", slot)], name="down")
                        S.add("dve", lambda e, t=t, d=d, bank=bank, W=W: e.scalar_tensor_tensor(
                            out=x_ap(t, d), in0=banks[bank][:, 0:W], scalar=0.5, in1=x_ap(t, d),
                            op0=ALU.mult, op1=ALU.add),
                            reads=[("ps", bank), x_reg(t, d)], writes=[x_reg(t, d)], name="xupd")
                        norm_accum(t, d, thr=(10 if d == DC - 1 else 30))
                    if half == 1:
                        after(t)
                for d in range(4 * half, 4 * half + 4):
                    del slab_slot[("dn", k, d)]
            expedite(step=exp_step)

        def pooling(t, g, bank):
            W = t.W
            if t.is_H:
                S.add("act", lambda e, g=g, bank=bank: e.copy(out=uhalo[:, g, :], in_=banks[bank][:, 0:HALO]),
                      reads=[("ps", bank)], writes=[("uhalo", g)], name="uH")
                return
            ub = nxt("uext", 2)
            S.add("act", lambda e, ub=ub, bank=bank: e.copy(out=uext[:, ub, HALO:HALO + T], in_=banks[bank][:, :]),
                  reads=[("ps", bank)], writes=[("uext", ub)], name="ucopy")
            S.add("act", lambda e, ub=ub, g=g: e.copy(out=uext[:, ub, 0:HALO], in_=uhalo[:, g, :]),
                  reads=[("uhalo", g)], writes=[("uext", ub)], name="uhalo_in")
            m = g + 1
            start = {m: HALO}
            for i in range(m, 1, -1):
                start[i - 1] = start[i] - (1 << (i - 1))
            prev_ap, prev_reg = (lambda lo, hi, ub=ub: uext[:, ub, lo:hi]), ("uext", ub)
            for i in range(1, m + 1):
                li = i % 2
                sh = 1 << (i - 1)
                lo = start[i]
                S.add("dve", lambda e, prev_ap=prev_ap, li=li, lo=lo, sh=sh: e.tensor_tensor(
                    out=lvl[:, li, lo:HALO + T], in0=prev_ap(lo, HALO + T), in1=prev_ap(lo - sh, HALO + T - sh), op=ALU.add),
                    reads=[prev_reg], writes=[("lvl", li)], name="padd")
                prev_ap, prev_reg = (lambda lo, hi, li=li: lvl[:, li, lo:hi]), ("lvl", li)
            w = float(1 << m)
            S.add("dve", lambda e, prev_ap=prev_ap, ub=ub, g=g, w=w, t=t: e.scalar_tensor_tensor(
                out=pooled[:, t.par, g, :], in0=prev_ap(HALO, HALO + T), scalar=1.0 / w, in1=uext[:, ub, HALO:HALO + T],
                op0=ALU.mult, op1=ALU.subtract),
                reads=[prev_reg, ("uext", ub)], writes=[("pooled", t.par, g)], name="pooled")
            if t.is_first:
                S.add("dve", lambda e, prev_ap=prev_ap, g=g: e.tensor_tensor(
                    out=t16[:, :], in0=prev_ap(HALO, 2 * HALO), in1=invc[:, g, :], op=ALU.mult),
                    reads=[prev_reg, ("c_invc",)], writes=[("t16",)], name="fix1")
                S.add("dve", lambda e, ub=ub, g=g, t=t: e.tensor_tensor(
                    out=pooled[:, t.par, g, 0:HALO], in0=t16[:, :], in1=uext[:, ub, HALO:2 * HALO], op=ALU.subtract),
                    reads=[("t16",), ("uext", ub)], writes=[("pooled", t.par, g)], name="fix2")
            S.add("act", lambda e, ub=ub, g=g: e.copy(out=uhalo[:, g, :], in_=uext[:, ub, T:T + HALO]),
                  reads=[("uext", ub)], writes=[("uhalo", g)], name="uhalo_out")

        def conv(t, j, bv, bc, bb):
            W = t.W
            vb = nxt("vsb", 2)
            S.add("act", lambda e, vb=vb, bv=bv, W=W: e.copy(out=vsb[:, vb, 0:W], in_=banks[bv][:, 0:W]),
                  reads=[("ps", bv)], writes=[("vsb", vb)], name="vcopy")
            if t.is_H:
                S.add("dve", lambda e, vb=vb, bc=bc, j=j: e.tensor_tensor(
                    out=zhalo[:, j, :], in0=vsb[:, vb, 0:HALO], in1=banks[bc][:, 0:HALO], op=ALU.mult),
                    reads=[("vsb", vb), ("ps", bc)], writes=[("zhalo", j)], name="zH")
                return
            zb = nxt("zext", 2)
            tb = nxt("t1", 2)
            S.add("act", lambda e, zb=zb, j=j: e.copy(out=zext[:, zb, 0:HALO], in_=zhalo[:, j, :]),
                  reads=[("zhalo", j)], writes=[("zext", zb)], name="zhalo_in")
            S.add("dve", lambda e, zb=zb, vb=vb, bc=bc: e.tensor_tensor(
                out=zext[:, zb, HALO:HALO + T], in0=vsb[:, vb, :], in1=banks[bc][:, :], op=ALU.mult),
                reads=[("vsb", vb), ("ps", bc)], writes=[("zext", zb)], name="z")
            si = nxt("silu", NSILU)
            S.add("act", lambda e, si=si, bb=bb: e.copy(out=silu[:, si, :], in_=banks[bb][:, :]),
                  reads=[("ps", bb)], writes=[("silu", si)], name="bcopy")
            S.add("dve", lambda e, zb=zb, tb=tb, j=j: e.tensor_scalar(
                out=t1b[:, tb, :], in0=zext[:, zb, HALO - 2:HALO - 2 + T], scalar1=convw[:, j, 0:1], scalar2=None,
                op0=ALU.mult),
                reads=[("zext", zb), ("c_convw",)], writes=[("t1", tb)], name="c0")
            for k in (1, 2):
                S.add("dve", lambda e, zb=zb, tb=tb, j=j, k=k: e.scalar_tensor_tensor(
                    out=t1b[:, tb, :], in0=zext[:, zb, HALO - 2 + k:HALO - 2 + k + T], scalar=convw[:, j, k:k + 1],
                    in1=t1b[:, tb, :], op0=ALU.mult, op1=ALU.add),
                    reads=[("zext", zb), ("t1", tb), ("c_convw",)], writes=[("t1", tb)], name="c12")
            S.add("dve", lambda e, tb=tb, si=si, t=t, j=j: e.tensor_tensor(
                out=hid_ap(t, j), in0=t1b[:, tb, :], in1=silu[:, si, :], op=ALU.mult),
                reads=[("t1", tb), ("silu", si)], writes=[hid_reg(t, j)], name="ya")
            S.add("dve", lambda e, zb=zb, j=j: e.tensor_copy(out=zhalo[:, j, :], in_=zext[:, zb, T:T + HALO]),
                  reads=[("zext", zb)], writes=[("zhalo", j)], name="zhalo_out")

        def pool_mm(t, gs):
            for g in gs:
                bank = next_bank()
                mm_group(bank, T, [(poolw[:, g, :], pooled[:, t.par, g, :])], [("pooled", t.par, g), ("poolw",)], name="poolmm")
                S.add("act", lambda e, t=t, g=g, bank=bank: e.mul(out=hid_ap(t, 4 + g), in_=banks[bank][:, :],
                                                                  mul=pscale[:, g:g + 1]),
                      reads=[("ps", bank), ("c_pscale",)], writes=[hid_reg(t, 4 + g)], name="yb")

        late_pool = []

        def w_in(tiles):
            def u_work(slot, t, gs):
                for g in gs:
                    bank = next_bank()
                    mm_group(bank, t.W, [(ring[:, slot, kc * 512 + g * 128: kc * 512 + g * 128 + 128], h_ap(t, kc))
                                         for kc in range(DC)],
                             [h_reg(t, kc) for kc in range(DC)] + [("ring", slot)], name="win_u",
                             fine_reads=([[h_reg(t, kc), ("ring", slot)] for kc in range(DC)] if g == 0 else None))
                    pooling(t, g, bank)

            def j_work(j, slot, t):
                bl = []
                for q in range(2 if t.is_H else 3):
                    bank = next_bank()
                    mm_group(bank, t.W, [(ring[:, slot, kc * 384 + q * 128: kc * 384 + q * 128 + 128], h_ap(t, kc))
                                         for kc in range(DC)],
                             [h_reg(t, kc) for kc in range(DC)] + [("ring", slot)], name="win_j")
                    bl.append(bank)
                conv(t, j, bl[0], bl[1], bl[2] if len(bl) > 2 else None)
                if j >= 1 and not t.is_H:
                    pool_mm(t, (j - 1,))

            slot_u = load_slab(("win", "u"))
            slots_j = [load_slab(("win", j)) for j in range(4)]
            lead, last = tiles[:-1], tiles[-1]
            for j in range(4):
                for t in lead:
                    u_work(slot_u, t, (j,))
                    j_work(j, slots_j[j], t)
            for j in range(4):
                u_work(slot_u, last, (j,))
                j_work(j, slots_j[j], last)
                if j == 0:
                    for t in lead:
                        if not t.is_H:
                            pool_mm(t, (3,))
            late_pool.append(last)
            del slab_slot[("win", "u")]
            for j in range(4):
                del slab_slot[("win", j)]

        def w_out(tiles, after):
            slots = [load_slab(("wo", 0)), load_slab(("wo", 1))]
            for t in tiles:
                for s in range(2):
                    slot = slots[s]
                    for oi in range(4):
                        o = 4 * s + oi
                        bank = next_bank()
                        mm_group(bank, T, [(ring[:, slot, kc * 512 + oi * 128: kc * 512 + oi * 128 + 128], hid_ap(t, kc))
                                           for kc in range(DC)],
                                 [hid_reg(t, kc) for kc in range(DC)] + [("ring", slot)], name="wout")
                        S.add("dve", lambda e, t=t, o=o, bank=bank: e.tensor_tensor(
                            out=x_ap(t, o), in0=banks[bank][:, :], in1=x_ap(t, o), op=ALU.add),
                            reads=[("ps", bank), x_reg(t, o)], writes=[x_reg(t, o)], name="xadd")
                        norm_accum(t, o, thr=((40 if t is tiles[-1] else 12) if o == DC - 1 else 40))
                        if o == 1 and late_pool:
                            pool_mm(late_pool.pop(), (3,))
                after(t)
            del slab_slot[("wo", 0)]
            del slab_slot[("wo", 1)]
            expedite(base=40, step=2)

        S.add("dve", lambda e: e.memset(ones[:, :], 1.0 / D), writes=[("ones",)], name="ones")
        S.add("dve", lambda e: e.memset(epsb[:, :], EPS), writes=[("eps",)], name="eps")
        S.add("act", lambda e: e.activation(out=t16[:, 0:1], in_=epsb[:, 0:1], func=AF.Square), reads=[("eps",)], writes=[("t16",)], name="warm")

        tH = TileD("H", HALO, None, 0, -HALO, is_H=True)
        pairs = [
            [TileD("A0", T, 0, 0, 0, is_first=True), TileD("B0", T, 1, 1, T)],
            [TileD("A1", T, 2, 0, 2 * T), TileD("B1", T, 0, 1, 3 * T)],
        ]

        def load_x(t):
            if t.is_H:
                S.add("sp", lambda e: e.dma_start(out=xH[:, :, :], in_=xT[:, :, 0:HALO]),
                      writes=[("xH", c) for c in range(DC)], dma_key=("xH",), name="ldxH")
            else:
                assert not any(it[3] == ("final", t.xslot) for it in pending)
                for q in range(4):
                    xload_ops[t.name] = S.add("sp", lambda e, t=t, q=q: e.dma_start(
                        out=xs[:, t.xslot, 2 * q:2 * q + 2, :],
                        in_=xT[:, 2 * q:2 * q + 2, HALO + t.tok0:HALO + t.tok0 + T]),
                        writes=[("x", t.xslot, 2 * q), ("x", t.xslot, 2 * q + 1)], dma_key=("x", t.xslot, q), name="ldx",
                        )
                    if q == 1:
                        xload_ops[t.name + "q1"] = xload_ops[t.name]

        stores = []

        def finish_tile(t):
            norm_finish(t, G_FINAL, final=True)

        S.add("sp", lambda e: e.dma_start(out=gains[:, :, :], in_=gains_d[:, :, :]), writes=[("c_gains",)],
              dma_key=("c", 0), name="ldgains")
        load_x(tH)
        load_x(pairs[0][0])
        S.add("sp", lambda e: e.dma_start(out=convw[:, :, :], in_=convw_d[:, :, :]), writes=[("c_convw",)],
              dma_key=("c", 1), name="ldconvw")
        S.add("sp", lambda e: e.dma_start(out=pscale[:, :], in_=pscale_d[:, :]), writes=[("c_pscale",)],
              dma_key=("c", 2), name="ldpscale")
        S.add("sp", lambda e: e.dma_start(out=invc[:, :, :], in_=invc_d[:, :, :]), writes=[("c_invc",)],
              dma_key=("c", 3), name="ldinvc")
        load_x(pairs[0][1])

        def initial_norm(t, thr_a=10, thr_b=10):
            for c in range(DC):
                norm_accum(t, c, thr=(thr_b if c == DC - 1 else thr_a))
            norm_finish(t, G_FFN1, thr=thr_b, step=2)

        for p, (tA, tB) in enumerate(pairs):
            tiles = ([tH] if p == 0 else []) + [tA, tB]
            if p == 0:
                for t in tiles:
                    initial_norm(t)
                gateup(1, tiles, skew=3)
                S.add("pool", lambda e: e.dma_start(out=poolw[:, :, :], in_=poolw_d[:, :, :]), writes=[("poolw",)],
                      dma_key=("c", 4), name="ldpoolw")
                load_x(pairs[1][0])
            else:
                load_x(tB)
                gateup(1, tiles, skew=5, mid=lambda: initial_norm(tB, 30, 40))
            down(1, tiles, lambda t: norm_finish(t, G_MIX))
            w_in(tiles)
            w_out([tA, tB], lambda t: norm_finish(t, G_FFN2, thr=(40 if t is tB else 12), step=(2 if t is tB else 4)))
            gateup(2, [tA, tB], skew=3)
            if p == 0:
                initial_norm(pairs[1][0], 30, 40)
            down(2, [tA, tB], finish_tile, exp_step=16)
        flush(force=True)
        S.add("sp", lambda e: None,
              reads=[("out", n, q) for n in ("A0", "B0", "A1", "B1") for q in range(4)], name="final")

        S.plan()
        global _LAST_SCHED
        _LAST_SCHED = S

        with nc.Block() as block:
            def run(eng_name, e):
                for op in S.ops[eng_name]:
                    for dep in op.waits:
                        if dep.dma_key is not None:
                            e.wait_ge(dsem(dep.dma_key), 16 * dep.dma_cnt)
                        else:
                            e.wait_ge(sem[dep.eng], dep.sigidx)
                    ins = op.emit(e)
                    if ins is None:
                        continue
                    if op.dma_key is not None:
                        ins.then_inc(dsem(op.dma_key), 16)
                    elif op.signal:
                        ins.then_inc(sem[op.eng], 1)

            @block.tensor
            def _(e):
                run("pe", e)

            @block.scalar
            def _(e):
                run("act", e)

            @block.vector
            def _(e):
                run("dve", e)

            @block.gpsimd
            def _(e):
                run("pool", e)

            @block.sync
            def _(e):
                run("sp", e)
    return nc


_PROGRAM = None
_LAST_SCHED = None


def _prep_inputs(inp):
    x = np.asarray(inp["x"], np.float32)
    wst = _build_wstream(inp)
    gains = np.stack([np.asarray(inp["norm_ffn1"][0]), np.asarray(inp["norm_mix"][0]),
                      np.asarray(inp["norm_ffn2"][0]), np.asarray(inp["norm_final"])], axis=0)
    gains = np.ascontiguousarray(gains.reshape(4, DC, 128).transpose(2, 0, 1)).astype(np.float32)
    convw = np.ascontiguousarray(np.asarray(inp["conv_w"][0]).reshape(3, 4, 128).transpose(2, 1, 0)).astype(np.float32)
    pscale = np.ascontiguousarray(np.asarray(inp["pool_scale"][0]).reshape(4, 128).T).astype(np.float32)
    poolw = np.ascontiguousarray(np.asarray(inp["pool_w"][0]).transpose(1, 0, 2)).astype(np.float32)
    t1 = np.arange(1, HALO + 1, dtype=np.float32)
    invc_first = np.stack([1.0 / np.minimum(t1, float(w)) for w in WINDOWS], axis=0)
    invc_rest = np.stack([np.full(HALO, 1.0 / w, np.float32) for w in WINDOWS], axis=0)
    in_maps = []
    for core in range(NCORE):
        b, h = core // 2, core % 2
        xt = np.zeros((128, DC, HALO + TOK), np.float32)
        lo = h * TOK - HALO
        src = x[b, max(lo, 0):h * TOK + TOK, :]
        src = src.reshape(src.shape[0], DC, 128).transpose(2, 1, 0)
        xt[:, :, HALO + TOK - src.shape[2]:] = src
        ic = invc_first if h == 0 else invc_rest
        in_maps.append({
            "xT": xt, "wst": wst, "gains": gains, "convw": convw, "pscale": pscale,
            "invc": np.ascontiguousarray(np.broadcast_to(ic[None], (128, 4, HALO))).astype(np.float32),
            "poolw": poolw,
        })
    return in_maps


def kernel(**inputs):
    global _PROGRAM
    if _PROGRAM is None:
        _PROGRAM = build_program()
    in_maps = _prep_inputs(inputs)
    res = run_bass_kernel_spmd(_PROGRAM, in_maps, core_ids=list(range(NCORE)))
    out = np.empty((4, 4096, D), np.float32)
    for core in range(NCORE):
        b, h = core // 2, core % 2
        o = res.results[core]["outT"]
        out[b, h * TOK:(h + 1) * TOK, :] = o.transpose(2, 1, 0).reshape(TOK, D)
    return out
```
